# Optimizing a Trainium2 kernel written in Bass

```python
import math
import jax, jax.numpy as jnp
from jax import lax
import numpy as np

D_MODEL = 2048
BATCH = 4
SEQ = 4096
DEPTH = 2

HEAD_DIM = 128
WIDTH_A = D_MODEL // 2
WIDTH_B = D_MODEL // 2
N_HEADS_A = WIDTH_A // HEAD_DIM
N_HEADS_B = WIDTH_B // HEAD_DIM
CHUNK = 128
Q_BLOCK = 128
SSM_WIDTH = D_MODEL // 2
SSM_GROUP = 16
SSM_GROUPS = SSM_WIDTH // SSM_GROUP
SSM_STATE = 64
EPS = 1e-6
DT_MIN = 1e-3
DT_MAX = 1e-1

kernel_name = "hybrid_sgu_stickbreak_s5_adaln"


def rms_norm(x, g):
    xf = x.astype(jnp.float32)
    y = xf * lax.rsqrt(jnp.mean(xf * xf, axis=-1, keepdims=True) + EPS)
    return (y * g.astype(jnp.float32)).astype(x.dtype)


def spatial_gating(u, v, norm_g, w_s, b_s):
    bsz, l, _ = v.shape
    n_chunks = l // CHUNK
    vh = v.reshape(bsz, n_chunks, CHUNK, N_HEADS_A, HEAD_DIM)
    vh = rms_norm(vh, norm_g.reshape(N_HEADS_A, HEAD_DIM))
    causal = jnp.tril(jnp.ones((CHUNK, CHUNK), dtype=bool))
    w = jnp.where(causal[None], w_s, 0.0).astype(vh.dtype)
    s = jnp.einsum('hts,bnshd->bnthd', w, vh) + b_s.T.astype(vh.dtype)[None, None, :, :, None]
    return u * s.reshape(bsz, l, WIDTH_A)


def stick_breaking(q, k, v):
    bsz, l, h, dh = q.shape
    n_blocks = l // Q_BLOCK
    qb = q.reshape(bsz, n_blocks, Q_BLOCK, h, dh).transpose(1, 0, 3, 2, 4)
    kt = k.transpose(0, 2, 1, 3)
    vt = v.transpose(0, 2, 1, 3)
    scale = 1.0 / math.sqrt(dh)
    k_pos = jnp.arange(l)

    def block(args):
        q_blk, blk = args
        q_pos = blk * Q_BLOCK + jnp.arange(Q_BLOCK)
        mask = k_pos[None, :] < q_pos[:, None]
        z = jnp.einsum('bhqd,bhkd->bhqk', q_blk, kt).astype(jnp.float32) * scale
        log_beta = jax.nn.log_sigmoid(z)
        log_keep = jnp.where(mask, jax.nn.log_sigmoid(-z), 0.0)
        later = lax.cumsum(log_keep, axis=3, reverse=True) - log_keep
        w = jnp.where(mask, jnp.exp(log_beta + later), 0.0)
        return jnp.einsum('bhqk,bhkd->bhqd', w.astype(vt.dtype), vt)

    out = lax.map(block, (qb, jnp.arange(n_blocks)))
    return out.transpose(1, 0, 3, 2, 4).reshape(bsz, l, h * dh)


def s5_ssm(u, lam_re, lam_im, b_re, b_im, c_re, c_im, d_skip, log_dt):
    f32 = jnp.float32
    bsz, l, _ = u.shape
    uf = u.astype(f32).reshape(bsz, l, SSM_GROUPS, SSM_GROUP)
    dt = jnp.exp(log_dt.astype(f32))[:, None]
    lr = lam_re.astype(f32)
    li = lam_im.astype(f32)
    mag = jnp.exp(lr * dt)
    a_re = mag * jnp.cos(li * dt)
    a_im = mag * jnp.sin(li * dt)
    den = lr * lr + li * li
    nr = a_re - 1.0
    coef_re = (nr * lr + a_im * li) / den
    coef_im = (a_im * lr - nr * li) / den
    br = b_re.astype(f32)
    bi = b_im.astype(f32)
    bb_re = coef_re[..., None] * br - coef_im[..., None] * bi
    bb_im = coef_re[..., None] * bi + coef_im[..., None] * br
    bu_re = jnp.einsum('gpc,blgc->blgp', bb_re, uf)
    bu_im = jnp.einsum('gpc,blgc->blgp', bb_im, uf)
    a_re_t = jnp.broadcast_to(a_re, (1, l) + a_re.shape)
    a_im_t = jnp.broadcast_to(a_im, (1, l) + a_im.shape)

    def combine(e1, e2):
        a1r, a1i, b1r, b1i = e1
        a2r, a2i, b2r, b2i = e2
        return (a2r * a1r - a2i * a1i,
                a2r * a1i + a2i * a1r,
                a2r * b1r - a2i * b1i + b2r,
                a2r * b1i + a2i * b1r + b2i)

    _, _, h_re, h_im = lax.associative_scan(combine, (a_re_t, a_im_t, bu_re, bu_im), axis=1)
    y = (jnp.einsum('gcp,blgp->blgc', c_re.astype(f32), h_re)
         - jnp.einsum('gcp,blgp->blgc', c_im.astype(f32), h_im))
    y = y.reshape(bsz, l, SSM_WIDTH) + d_skip.astype(f32) * u.astype(f32)
    return y.astype(u.dtype)


def ab_mixer(h, w_in, w_out, sgu_norm_g, sgu_w, sgu_b):
    bsz, l, _ = h.shape
    proj = h @ w_in
    cuts = np.cumsum([WIDTH_A, WIDTH_A, WIDTH_A, WIDTH_B, WIDTH_B, WIDTH_B]).tolist()
    a_u, a_v, a_z, q, k, v, b_z = jnp.split(proj, cuts, axis=-1)
    out_a = spatial_gating(jax.nn.gelu(a_u), jax.nn.gelu(a_v), sgu_norm_g, sgu_w, sgu_b)
    out_a = out_a * jax.nn.silu(a_z)
    shp = (bsz, l, N_HEADS_B, HEAD_DIM)
    out_b = stick_breaking(q.reshape(shp), k.reshape(shp), v.reshape(shp)) * jax.nn.silu(b_z)
    return jnp.concatenate([out_a, out_b], axis=-1) @ w_out


def ssm_mixer(h, w_in, w_out, lam_re, lam_im, b_re, b_im, c_re, c_im, d_skip, log_dt, w_glu, b_glu):
    proj = h @ w_in
    u, z = jnp.split(proj, 2, axis=-1)
    y = s5_ssm(u, lam_re, lam_im, b_re, b_im, c_re, c_im, d_skip, log_dt)
    g = jax.nn.gelu(y)
    y = g * jax.nn.sigmoid(g @ w_glu + b_glu)
    return (y * jax.nn.silu(z)) @ w_out


def setup_inputs(seed: int = 0) -> dict:
    key = jax.random.key(seed)
    ks = jax.random.split(key, 24)
    n_even = (DEPTH + 1) // 2
    n_odd = DEPTH // 2
    d = D_MODEL
    nrm = jax.random.normal
    w_in_ab_cols = 3 * WIDTH_A + 4 * WIDTH_B
    log_dt = jax.random.uniform(ks[20], (n_odd, SSM_GROUPS), minval=math.log(DT_MIN), maxval=math.log(DT_MAX))
    n_idx = jnp.arange(SSM_STATE, dtype=jnp.float32)
    return {
        "x": nrm(ks[0], (BATCH, SEQ, d)),
        "c": nrm(ks[1], (BATCH, d)),
        "ln_pre_g": 1.0 + 0.02 * nrm(ks[2], (DEPTH, d)),
        "ln_post_g": 1.0 + 0.02 * nrm(ks[3], (DEPTH, d)),
        "w_mod": nrm(ks[4], (DEPTH, d, 3 * d)) * d ** -0.5,
        "b_mod": 0.02 * nrm(ks[5], (DEPTH, 3 * d)),
        "w_in_ab": nrm(ks[6], (n_even, d, w_in_ab_cols)) * d ** -0.5,
        "w_out_ab": nrm(ks[7], (n_even, WIDTH_A + WIDTH_B, d)) * (WIDTH_A + WIDTH_B) ** -0.5,
        "sgu_norm_g": 1.0 + 0.02 * nrm(ks[8], (n_even, WIDTH_A)),
        "sgu_w": nrm(ks[9], (n_even, N_HEADS_A, CHUNK, CHUNK)) * CHUNK ** -0.5,
        "sgu_b": 1.0 + 0.02 * nrm(ks[10], (n_even, N_HEADS_A, CHUNK)),
        "w_in_ssm": nrm(ks[11], (n_odd, d, 2 * SSM_WIDTH)) * d ** -0.5,
        "w_out_ssm": nrm(ks[12], (n_odd, SSM_WIDTH, d)) * SSM_WIDTH ** -0.5,
        "lam_re": -0.5 + 0.01 * nrm(ks[13], (n_odd, SSM_GROUPS, SSM_STATE)),
        "lam_im": math.pi * n_idx + 0.01 * nrm(ks[14], (n_odd, SSM_GROUPS, SSM_STATE)),
        "b_re": nrm(ks[15], (n_odd, SSM_GROUPS, SSM_STATE, SSM_GROUP)) * (2 * SSM_GROUP) ** -0.5,
        "b_im": nrm(ks[16], (n_odd, SSM_GROUPS, SSM_STATE, SSM_GROUP)) * (2 * SSM_GROUP) ** -0.5,
        "c_re": nrm(ks[17], (n_odd, SSM_GROUPS, SSM_GROUP, SSM_STATE)) * (2 * SSM_STATE) ** -0.5,
        "c_im": nrm(ks[18], (n_odd, SSM_GROUPS, SSM_GROUP, SSM_STATE)) * (2 * SSM_STATE) ** -0.5,
        "d_skip": nrm(ks[19], (n_odd, SSM_WIDTH)),
        "log_dt": log_dt,
        "w_glu": nrm(ks[21], (n_odd, SSM_WIDTH, SSM_WIDTH)) * SSM_WIDTH ** -0.5,
        "b_glu": 0.02 * nrm(ks[22], (n_odd, SSM_WIDTH)),
    }


def reference(x, c, ln_pre_g, ln_post_g, w_mod, b_mod, w_in_ab, w_out_ab, sgu_norm_g, sgu_w, sgu_b,
              w_in_ssm, w_out_ssm, lam_re, lam_im, b_re, b_im, c_re, c_im, d_skip, log_dt, w_glu, b_glu):
    cond = jax.nn.silu(c)
    for layer in range(DEPTH):
        mod = cond @ w_mod[layer] + b_mod[layer]
        shift, scale, gate = jnp.split(mod[:, None, :], 3, axis=-1)
        h = rms_norm(x, ln_pre_g[layer]) * (1.0 + scale) + shift
        i = layer // 2
        if layer % 2 == 0:
            y = ab_mixer(h, w_in_ab[i], w_out_ab[i], sgu_norm_g[i], sgu_w[i], sgu_b[i])
        else:
            y = ssm_mixer(h, w_in_ssm[i], w_out_ssm[i], lam_re[i], lam_im[i], b_re[i], b_im[i],
                          c_re[i], c_im[i], d_skip[i], log_dt[i], w_glu[i], b_glu[i])
        x = x + (gate * rms_norm(y, ln_post_g[layer])).astype(x.dtype)
    return x
```

```python
import contextlib
import math
import numpy as np
import concourse.bass as bass
import concourse.mybir as mybir
from concourse.bass_utils import run_bass_kernel_spmd

F32 = mybir.dt.float32
BF16 = mybir.dt.bfloat16
I32 = mybir.dt.int32
AF = mybir.ActivationFunctionType
ALU = mybir.AluOpType
AX = mybir.AxisListType

D = 2048
LH = 2048
NT = 4
EPS = 1e-6
T16 = 16
NCH = LH // T16


class Sched:
    ENGS = ("pe", "act", "dve", "pool", "sp")
    NSLOT = 8

    def __init__(self, nc):
        self.nc = nc
        self.ops = []
        self.last_w = {}
        self.readers = {}
        self.dma_rr = {e: 0 for e in self.ENGS}
        self.slot_last = {}
        self.last_op = {}

    def _add(self, engine, fn, reads, writes, dma, extra=()):
        deps = set(extra)
        for b in reads:
            if b in self.last_w:
                deps.add(self.last_w[b])
        for b in writes:
            if b in self.last_w:
                deps.add(self.last_w[b])
            for r in self.readers.get(b, ()):
                deps.add(r)
        oid = len(self.ops)
        slot = None
        if dma:
            slot = self.dma_rr[engine] % self.NSLOT
            self.dma_rr[engine] += 1
            key = (engine, slot)
            if key in self.slot_last:
                deps.add(self.slot_last[key])
            self.slot_last[key] = oid
        deps.discard(oid)
        self.ops.append(dict(engine=engine, fn=fn, deps=deps, dma=dma, slot=slot))
        for b in writes:
            self.last_w[b] = oid
            self.readers[b] = []
        for b in reads:
            if b not in writes:
                self.readers.setdefault(b, []).append(oid)
        self.last_op[engine] = oid
        return oid

    def op(self, engine, fn, reads=(), writes=()):
        return self._add(engine, fn, tuple(reads), tuple(writes), False)

    def dma(self, engine, fn, reads=(), writes=()):
        return self._add(engine, fn, tuple(reads), tuple(writes), True)

    def barrier(self):
        pend = set(self.last_op.values()) | set(self.slot_last.values())
        for e in self.ENGS:
            self._add(e, lambda eng: eng.nop(), (), (), False, extra=pend)
        self.last_w = {}
        self.readers = {}

    def emit(self, final_wait_engine="sp"):
        nc = self.nc
        ops = self.ops

        def skip(od, o):
            return od["engine"] == "pe" and o["engine"] == "pe" and not od["dma"] and not o["dma"]

        need = [False] * len(ops)
        for o in ops:
            for d in o["deps"]:
                if not skip(ops[d], o):
                    need[d] = True
        cnt = {e: 0 for e in self.ENGS}
        slotcnt = {}
        tok = [None] * len(ops)
        for i, o in enumerate(ops):
            if o["dma"]:
                key = ("dma", o["engine"], o["slot"])
                slotcnt[key] = slotcnt.get(key, 0) + 16
                tok[i] = (key, slotcnt[key])
            elif need[i]:
                cnt[o["engine"]] += 1
                tok[i] = (("c", o["engine"]), cnt[o["engine"]])
        sem_keys = sorted({t[0] for t in tok if t is not None}, key=str)
        with contextlib.ExitStack() as st:
            sems = {}
            for k in sem_keys:
                sems[k] = st.enter_context(nc.semaphore("s_" + "_".join(str(x) for x in k)))
            per_eng = {e: [] for e in self.ENGS}
            waited = {e: {} for e in self.ENGS}
            for i, o in enumerate(ops):
                e = o["engine"]
                waits = {}
                for d in o["deps"]:
                    if skip(ops[d], o):
                        continue
                    k, v = tok[d]
                    if waited[e].get(k, 0) >= v:
                        continue
                    waits[k] = max(waits.get(k, 0), v)
                for k, v in waits.items():
                    waited[e][k] = v
                per_eng[e].append((o["fn"], list(waits.items()), tok[i]))
            fin = [(k, v) for k, v in slotcnt.items() if waited[final_wait_engine].get(k, 0) < v]
            blockfn = {"pe": "tensor", "act": "scalar", "dve": "vector", "pool": "gpsimd", "sp": "sync"}
            with nc.Block() as block:
                for e in self.ENGS:
                    lst = per_eng[e]

                    def body(eng, lst=lst, e=e):
                        for fn, waits, t in lst:
                            for k, v in waits:
                                eng.wait_ge(sems[k], v)
                            ins = fn(eng)
                            if t is not None:
                                ins.then_inc(sems[t[0]], 16 if t[0][0] == "dma" else 1)
                        if e == final_wait_engine:
                            for k, v in fin:
                                eng.wait_ge(sems[k], v)

                    getattr(block, blockfn[e])(body)
        return len(ops)


class Arena:
    def __init__(self, t, n32):
        self.t = t
        self.n32 = n32
        self.off = 0

    def mark(self):
        return self.off

    def reset(self, m):
        self.off = m

    def alloc(self, shape, dt=F32):
        n = int(np.prod(shape[1:]))
        b = 4 if dt in (F32, I32) else 2
        n32 = (n * b + 3) // 4
        assert self.off + n32 <= self.n32, ("SBUF arena overflow", self.off, n32, self.n32)
        v = self.t[0:shape[0], self.off:self.off + n32]
        self.off += n32
        if b == 2:
            v = v.bitcast(dt)
        elif dt != F32:
            v = v.bitcast(dt)
        if len(shape) > 2:
            names = "abcdefg"[:len(shape) - 1]
            kw = {names[i]: shape[i + 1] for i in range(len(shape) - 2)}
            v = v.rearrange("p (%s) -> p %s" % (" ".join(names), " ".join(names)), **kw)
        return v


def build_program(dbg=None):
    nc = bass.Bass("TRN2", target_bir_lowering=False)

    def din(name, shape, dt=F32):
        return nc.dram_tensor(name, list(shape), dt, kind="ExternalInput").ap()

    def dscr(name, shape, dt=F32):
        kind = "ExternalOutput" if (dbg and name in dbg) else "Internal"
        return nc.dram_tensor(name, list(shape), dt, kind=kind).ap()

    x_own = din("x_own", [LH, D])
    x_prev = din("x_prev", [LH, D])
    flag = din("flag", [128, 1])
    cT = din("cT", [128, 16])
    g_preT = din("g_preT", [2, 128, 16])
    g_post = din("g_post", [2, D])
    w_mod = din("w_mod", [2, D, 3 * D])
    b_modT = din("b_modT", [2, 128, 48])
    b_mod = din("b_mod", [2, 3 * D])
    w_in_ab = din("w_in_ab", [D, 7168])
    w_out_ab = din("w_out_ab", [D, D])
    sgu_norm_g = din("sgu_norm_g", [1024])
    sgu_w = din("sgu_w", [8, 128, 128])
    sgu_b = din("sgu_b", [1024])
    w_in_ssm = din("w_in_ssm", [D, D])
    w_out_ssm = din("w_out_ssm", [1024, D])
    w_glu = din("w_glu", [1024, 1024])
    b_gluT = din("b_gluT", [128, 8])
    d_skipT = din("d_skipT", [128, 8])
    lamre_sm = din("lamre_sm", [128, 32])
    lamim_sm = din("lamim_sm", [128, 32])
    logdt_sm = din("logdt_sm", [128, 32])
    bre_smz = din("bre_smz", [128, 32, 32])
    bim_smz = din("bim_smz", [128, 32, 32])
    cre_smz = din("cre_smz", [128, 32, 32])
    cim_smz = din("cim_smz", [128, 32, 32])

    out = nc.dram_tensor("out", [LH, D], F32, kind="ExternalOutput").ap()

    QT = dscr("QT", [8, 128, LH], BF16)
    KT = dscr("KT", [8, 128, 2 * LH], BF16)
    VS = dscr("VS", [2 * LH, 1024], BF16)
    OA = dscr("OA", [8, 128, LH], BF16)
    SBZ = dscr("SBZ", [8, 128, LH], BF16)
    skip_l0 = bool(dbg and dbg.get("skip_l0"))
    X1 = din("X1in", [LH, D]) if skip_l0 else dscr("X1", [LH, D], F32)
    st_in = nc.dram_tensor("st_in", [128, 64], F32)
    st_out = nc.dram_tensor("st_out", [256, 64], F32)

    NA = 52000
    with contextlib.ExitStack() as st:
        arena_t = st.enter_context(nc.sbuf_tensor("arena", [128, NA], F32))
        ps = st.enter_context(nc.psum_tensor("ps", [128, 4096], F32))
        A = Arena(arena_t, NA)
        S = Sched(nc)

        def bank(i, n=1):
            return ps[:, i * 512:(i + n) * 512]

        def bkeys(i, n=1):
            return ["ps%d" % j for j in range(i, i + n)]

        def dump(name, ap, keys, dt=F32):
            if not (dbg and dbg.get("dump")):
                return
            t = nc.dram_tensor(name, list(ap.shape), dt, kind="ExternalOutput").ap()
            S.dma("sp", lambda e: e.dma_start(out=t, in_=ap), reads=keys)

        ident = A.alloc([128, 128], BF16)
        identf = A.alloc([128, 128], F32)
        ones_bf = A.alloc([128, 128], BF16)
        negones = A.alloc([128, 128], BF16)
        negUI = A.alloc([128, 128], BF16)
        flag_sb = A.alloc([128, 1])
        S.op("pool", lambda e: e.memset(identf, 0.0), writes=["identf"])
        S.op("pool", lambda e: e.affine_select(out=identf, in_=identf, compare_op=ALU.not_equal, fill=1.0,
                                               base=0, pattern=[[-1, 128]], channel_multiplier=1),
             reads=["identf"], writes=["identf"])
        S.op("pool", lambda e: e.tensor_copy(out=ident, in_=identf), reads=["identf"], writes=["ident"])
        S.op("pool", lambda e: e.memset(ones_bf, 1.0), writes=["ones_bf"])
        S.op("pool", lambda e: e.memset(negones, -1.0), writes=["negones"])
        S.op("pool", lambda e: e.memset(negUI, -1.0), writes=["negUI"])
        S.op("pool", lambda e: e.affine_select(out=negUI, in_=negUI, compare_op=ALU.is_ge, fill=0.0,
                                               base=0, pattern=[[-1, 128]], channel_multiplier=1),
             reads=["negUI"], writes=["negUI"])
        S.dma("sp", lambda e: e.dma_start(out=flag_sb, in_=flag), writes=["flag"])

        modT = A.alloc([128, 2, 32])
        a_pre = A.alloc([128, 2, 16])
        gpost_row = A.alloc([128, 2, D])
        m_phase = A.mark()
        cT_sb = A.alloc([128, 16])
        cond_bf = A.alloc([128, 16], BF16)
        cond_rep = A.alloc([128, 16, 128], BF16)
        gpre_sb = A.alloc([128, 2, 16])
        bmodT_sb = A.alloc([128, 2, 48])
        brow = A.alloc([128, D])
        grow = A.alloc([128, D])
        wm = [A.alloc([128, 3 * D], BF16) for _ in range(2)]
        S.dma("sp", lambda e: e.dma_start(out=cT_sb, in_=cT), writes=["cT"])
        S.dma("sp", lambda e: e.dma_start(out=gpre_sb, in_=g_preT.rearrange("l p f -> p l f")), writes=["gpre"])
        S.dma("sp", lambda e: e.dma_start(out=bmodT_sb, in_=b_modT.rearrange("l p f -> p l f")), writes=["bmodT"])
        S.op("act", lambda e: e.activation(out=cond_bf, in_=cT_sb, func=AF.Silu), reads=["cT"], writes=["cond"])
        S.op("dve", lambda e: e.tensor_copy(out=cond_rep, in_=cond_bf.unsqueeze(2).broadcast_to([128, 16, 128])),
             reads=["cond"], writes=["cond_rep"])
        for l in range(2):
            S.op("dve", lambda e: e.memset(bank(0, 5), 0.0), writes=bkeys(0, 5))
            for kt in range(16):
                wb = wm[kt % 2]
                wk = "wm%d" % (kt % 2)
                S.dma("pool", lambda e, wb=wb, l=l, kt=kt: e.dma_start(out=wb, in_=w_mod[l, kt * 128:(kt + 1) * 128, :]),
                      writes=[wk])
                for ft in range(32):
                    S.op("pe", lambda e, wb=wb, ft=ft, kt=kt: e.matmul(
                        bank(0)[:, ft:ft + 1], lhsT=wb[:, ft * 128:(ft + 1) * 128], rhs=cond_bf[:, kt:kt + 1],
                        start=False, stop=False, skip_group_check=True),
                        reads=[wk, "cond"] + bkeys(0), writes=bkeys(0))
                for cc in range(4):
                    S.op("pe", lambda e, wb=wb, cc=cc, kt=kt: e.matmul(
                        bank(1 + cc), lhsT=cond_rep[:, kt, :], rhs=wb[:, 4096 + cc * 512:4096 + (cc + 1) * 512],
                        start=False, stop=False, skip_group_check=True),
                        reads=[wk, "cond_rep"] + bkeys(1 + cc), writes=bkeys(1 + cc))
            S.op("dve", lambda e, l=l: e.tensor_tensor(out=modT[:, l, :], in0=bank(0)[:, 0:32], in1=bmodT_sb[:, l, 0:32],
                                                       op=ALU.add),
                 reads=bkeys(0) + ["bmodT"], writes=["modT%d" % l])
            S.op("dve", lambda e, l=l: e.scalar_tensor_tensor(out=a_pre[:, l, :], in0=modT[:, l, 16:32], scalar=1.0,
                                                              in1=gpre_sb[:, l, :], op0=ALU.add, op1=ALU.mult),
                 reads=["modT%d" % l, "gpre"], writes=["a_pre%d" % l])
            S.dma("sp", lambda e, l=l: e.dma_start(out=brow, in_=b_mod[l, 2 * D:3 * D].partition_broadcast(128)),
                  writes=["brow"])
            S.dma("sp", lambda e, l=l: e.dma_start(out=grow, in_=g_post[l].partition_broadcast(128)), writes=["grow"])
            S.op("dve", lambda e: e.tensor_tensor(out=brow, in0=bank(1, 4), in1=brow, op=ALU.add),
                 reads=bkeys(1, 4) + ["brow"], writes=["brow"])
            S.op("dve", lambda e, l=l: e.tensor_tensor(out=gpost_row[:, l, :], in0=brow, in1=grow, op=ALU.mult),
                 reads=["brow", "grow"], writes=["gpost%d" % l])
        dump("d_modT", modT, ["modT0", "modT1"])
        dump("d_apre", a_pre, ["a_pre0", "a_pre1"])
        dump("d_gpost", gpost_row, ["gpost0", "gpost1"])
        S.barrier()
        A.reset(m_phase)

        def norm_transpose_tile(src, tok0, l, xraw, xs, hT, tagbase, junk, ssb, it):
            for s in range(4):
                xr = xraw[(it * 4 + s) % 2]
                xrk = "xraw%d" % ((it * 4 + s) % 2)
                S.dma("sp", lambda e, xr=xr, s=s: e.dma_start(out=xr, in_=src[tok0 + s * 128: tok0 + (s + 1) * 128, :]),
                      writes=[xrk])
                S.op("act", lambda e, xr=xr, s=s: e.activation(out=junk, in_=xr, func=AF.Square,
                                                               accum_out=ssb[:, s:s + 1]),
                     reads=[xrk], writes=["junk", "ss%d" % s])
                S.op("act", lambda e, s=s: e.activation(out=ssb[:, 4 + s:5 + s], in_=ssb[:, s:s + 1], func=AF.Ln,
                                                        scale=1.0 / D, bias=eps_sb),
                     reads=["ss%d" % s], writes=["ln%d" % s])
                S.op("act", lambda e, s=s: e.activation(out=ssb[:, 8 + s:9 + s], in_=ssb[:, 4 + s:5 + s], func=AF.Exp,
                                                        scale=-0.5),
                     reads=["ln%d" % s], writes=["rstd%d" % s])
                S.op("dve", lambda e, xr=xr, s=s: e.tensor_scalar(out=xs[:, s, :], in0=xr, scalar1=ssb[:, 8 + s:9 + s],
                                                                  scalar2=None, op0=ALU.mult),
                     reads=[xrk, "rstd%d" % s], writes=["xs%d" % s])
            for ft in range(16):
                b = ft % 2
                pst = bank(b).bitcast(BF16)[:, 0:512]
                for s in range(4):
                    S.op("pe", lambda e, pst=pst, s=s, ft=ft: e.transpose(out=pst[:, s * 128:(s + 1) * 128],
                                                                          in_=xs[:, s, ft * 128:(ft + 1) * 128],
                                                                          identity=ident),
                         reads=["xs%d" % s, "ident"], writes=bkeys(b))
                eng = "act" if ft % 2 == 0 else "dve"
                if eng == "act":
                    S.op("act", lambda e, pst=pst, ft=ft: e.activation(out=hT[:, ft, :], in_=pst, func=AF.Identity,
                                                                       scale=a_pre[:, l, ft:ft + 1],
                                                                       bias=modT[:, l, ft:ft + 1]),
                         reads=bkeys(b) + ["a_pre%d" % l, "modT%d" % l], writes=["%s_%d" % (tagbase, ft)])
                else:
                    S.op("dve", lambda e, pst=pst, ft=ft: e.tensor_scalar(out=hT[:, ft, :], in0=pst,
                                                                          scalar1=a_pre[:, l, ft:ft + 1],
                                                                          scalar2=modT[:, l, ft:ft + 1],
                                                                          op0=ALU.mult, op1=ALU.add),
                         reads=bkeys(b) + ["a_pre%d" % l, "modT%d" % l], writes=["%s_%d" % (tagbase, ft)])

        def out_proj_resid(l, tt, catf, nk, wo, wokey, xsrc, dst, xres, yt, ssb, junk2, it0):
            for s in range(4):
                it = it0 + s
                pb = (it % 2) * 4
                for cc in range(4):
                    for kt in range(nk):
                        cap, ck = catf(kt, s)
                        S.op("pe", lambda e, pb=pb, cc=cc, kt=kt, cap=cap: e.matmul(
                            bank(pb + cc), lhsT=cap,
                            rhs=wo[:, kt, cc * 512:(cc + 1) * 512], start=(kt == 0), stop=(kt == nk - 1)),
                            reads=[ck, wokey], writes=bkeys(pb + cc))
                xr = xres[it % 2]
                xrk = "xres%d" % (it % 2)
                r0 = tt * 512 + s * 128
                S.dma("sp", lambda e, xr=xr, r0=r0: e.dma_start(out=xr, in_=xsrc[r0:r0 + 128, :]), writes=[xrk])
                S.op("act", lambda e, pb=pb: e.activation(out=junk2, in_=bank(pb, 4), func=AF.Square,
                                                          accum_out=ssb[:, 0:1]),
                     reads=bkeys(pb, 4), writes=["junk2", "pss"])
                S.op("act", lambda e: e.activation(out=ssb[:, 1:2], in_=ssb[:, 0:1], func=AF.Ln, scale=1.0 / D,
                                                   bias=eps_sb), reads=["pss"], writes=["pln"])
                S.op("act", lambda e: e.activation(out=ssb[:, 2:3], in_=ssb[:, 1:2], func=AF.Exp, scale=-0.5),
                     reads=["pln"], writes=["prstd"])
                y_ = yt[it % 2]
                yk = "yt%d" % (it % 2)
                S.op("dve", lambda e, pb=pb, y_=y_: e.scalar_tensor_tensor(out=y_, in0=bank(pb, 4), scalar=ssb[:, 2:3],
                                                                           in1=gpost_row[:, l, :], op0=ALU.mult,
                                                                           op1=ALU.mult),
                     reads=bkeys(pb, 4) + ["prstd", "gpost%d" % l], writes=[yk])
                S.op("pool", lambda e, y_=y_, xr=xr: e.tensor_tensor(out=y_, in0=y_, in1=xr, op=ALU.add),
                     reads=[yk, xrk], writes=[yk])
                S.dma("sp", lambda e, y_=y_, r0=r0: e.dma_start(out=dst[r0:r0 + 128, :], in_=y_), reads=[yk])

        eps_sb = A.alloc([128, 1])
        S.op("pool", lambda e: e.memset(eps_sb, EPS), writes=["eps"])

        l0_mark = A.mark()
        normg_row = A.alloc([128, 1024])
        bsb = A.alloc([128, 1024])
        WcT = A.alloc([128, 8, 128], BF16)
        tri = A.alloc([128, 128])
        sguw_sb = A.alloc([128, 8, 128])
        S.dma("sp", lambda e: e.dma_start(out=normg_row, in_=sgu_norm_g.partition_broadcast(128)), writes=["normg"])
        S.dma("sp", lambda e: e.dma_start(out=bsb, in_=sgu_b.partition_broadcast(128)), writes=["bsb"])
        S.dma("sp", lambda e: e.dma_start(out=sguw_sb, in_=sgu_w.rearrange("h t s -> t h s")), writes=["sguw"])
        S.op("pool", lambda e: e.memset(tri, 1.0), writes=["tri"])
        S.op("pool", lambda e: e.affine_select(out=tri, in_=tri, compare_op=ALU.is_ge, fill=0.0, base=0,
                                               pattern=[[1, 128]], channel_multiplier=-1),
             reads=["tri"], writes=["tri"])
        for h in range(8):
            S.op("pe", lambda e, h=h: e.transpose(out=bank(h % 2)[:, 0:128], in_=sguw_sb[:, h, :], identity=identf),
                 reads=["sguw", "identf"], writes=bkeys(h % 2))
            S.op("dve", lambda e, h=h: e.tensor_tensor(out=WcT[:, h, :], in0=bank(h % 2)[:, 0:128], in1=tri, op=ALU.mult),
                 reads=bkeys(h % 2) + ["tri"], writes=["WcT"])

        xraw = [A.alloc([128, D]) for _ in range(2)]
        junk = A.alloc([128, D], BF16)
        ssb = A.alloc([128, 12])
        xs = A.alloc([128, 4, D], BF16)
        hT = [A.alloc([128, 16, 512], BF16) for _ in range(2)]
        Wc = [A.alloc([128, 16, 512], BF16) for _ in range(2)]
        guT = A.alloc([128, 8, 512], BF16)
        szT = A.alloc([128, 8, 512], BF16)
        vn = A.alloc([128, 4, 1024], BF16)
        oaT = A.alloc([128, 8, 512], BF16)
        stg = [A.alloc([128, 512], BF16) for _ in range(4)]
        gv = A.alloc([128, 512])
        sq = A.alloc([128, 512])
        ssq = A.alloc([128, 12])
        t1 = A.alloc([128, 1024])
        t2 = A.alloc([128, 1024])

        wcount = [0]
        stgc = [0]
        pcount = [0]
        SCALE_Q = 1.0 / math.sqrt(128.0)

        def load_w(col0):
            i = wcount[0] % 2
            wcount[0] += 1
            S.dma("pool", lambda e, i=i, col0=col0: e.dma_start(
                out=Wc[i], in_=w_in_ab[:, col0:col0 + 512].rearrange("(kt k) c -> k kt c", k=128)),
                writes=["Wc%d" % i])
            return Wc[i], "Wc%d" % i

        def next_bank():
            b = 2 + pcount[0] % 4
            pcount[0] += 1
            return b

        def next_stg():
            i = stgc[0] % 4
            stgc[0] += 1
            return stg[i], "stg%d" % i

        def do_tile_A(tt):
            own = tt >= 4
            src = x_own if own else x_prev
            tok0 = (tt % 4) * 512
            ctx0 = tt * 512
            hb = hT[tt % 2]
            hk = "hT%d" % (tt % 2)
            hkeys = ["%s_%d" % (hk, ft) for ft in range(16)]
            norm_transpose_tile(src, tok0, 0, xraw, xs, hb, hk, junk, ssb, tt)
            if tt == 4:
                dump("d_hT", hb, hkeys, BF16)
                dump("d_ssb", ssb, ["rstd3"])
                dump("d_xs", xs, ["xs0", "xs1", "xs2", "xs3"], BF16)
            segs = []
            if own:
                segs += [("av", 1024, 0), ("av", 1536, 1), ("au", 0, 0), ("au", 512, 1), ("az", 2048, 0), ("az", 2560, 1),
                         ("q", 3072, 0), ("q", 3584, 1)]
            segs += [("k", 4096, 0), ("k", 4608, 1), ("v", 5120, 0), ("v", 5632, 1)]
            if own:
                segs += [("bz", 6144, 0), ("bz", 6656, 1)]
            for kind, col0, cc in segs:
                W_, wk = load_w(col0)
                if kind in ("av", "v"):
                    for s in range(4):
                        b = next_bank()
                        for kt in range(16):
                            S.op("pe", lambda e, b=b, kt=kt, s=s, W_=W_: e.matmul(
                                bank(b), lhsT=hb[:, kt, s * 128:(s + 1) * 128], rhs=W_[:, kt, :],
                                start=(kt == 0), stop=(kt == 15)),
                                reads=[hkeys[kt], wk], writes=bkeys(b))
                        if kind == "v":
                            sg, sk = next_stg()
                            if own:
                                S.op("dve", lambda e, b=b, sg=sg: e.tensor_copy(out=sg, in_=bank(b)),
                                     reads=bkeys(b), writes=[sk])
                            else:
                                S.op("dve", lambda e, b=b, sg=sg: e.tensor_scalar(out=sg, in0=bank(b), scalar1=flag_sb,
                                                                                  scalar2=None, op0=ALU.mult),
                                     reads=bkeys(b) + ["flag"], writes=[sk])
                            r0 = ctx0 + s * 128
                            S.dma("sp", lambda e, sg=sg, r0=r0, cc=cc: e.dma_start(
                                out=VS[r0:r0 + 128, cc * 512:(cc + 1) * 512], in_=sg), reads=[sk])
                        else:
                            S.op("act", lambda e, b=b: e.activation(out=gv, in_=bank(b), func=AF.Gelu_apprx_tanh),
                                 reads=bkeys(b), writes=["gv"])
                            S.op("dve", lambda e: e.tensor_tensor(out=sq, in0=gv, in1=gv, op=ALU.mult),
                                 reads=["gv"], writes=["sq"])
                            S.op("dve", lambda e: e.tensor_reduce(out=ssq[:, 0:4], in_=sq.rearrange("p (h d) -> p h d", h=4),
                                                                  axis=AX.X, op=ALU.add),
                                 reads=["sq"], writes=["ssq"])
                            S.op("act", lambda e: e.activation(out=ssq[:, 4:8], in_=ssq[:, 0:4], func=AF.Ln,
                                                               scale=1.0 / 128, bias=eps_sb),
                                 reads=["ssq"], writes=["ssqln"])
                            S.op("act", lambda e: e.activation(out=ssq[:, 8:12], in_=ssq[:, 4:8], func=AF.Exp, scale=-0.5),
                                 reads=["ssqln"], writes=["ssqr"])
                            S.op("dve", lambda e: e.tensor_tensor(
                                out=sq.rearrange("p (h d) -> p h d", h=4), in0=gv.rearrange("p (h d) -> p h d", h=4),
                                in1=ssq[:, 8:12].unsqueeze(2).broadcast_to([128, 4, 128]), op=ALU.mult),
                                reads=["gv", "ssqr"], writes=["sq"])
                            S.op("dve", lambda e, s=s, cc=cc: e.tensor_tensor(
                                out=vn[:, s, cc * 512:(cc + 1) * 512], in0=sq, in1=normg_row[:, cc * 512:(cc + 1) * 512],
                                op=ALU.mult), reads=["sq", "normg"], writes=["vn%d" % s])
                else:
                    for ct in range(4):
                        b = next_bank()
                        h = cc * 4 + ct
                        for kt in range(16):
                            S.op("pe", lambda e, b=b, kt=kt, ct=ct, W_=W_: e.matmul(
                                bank(b), lhsT=W_[:, kt, ct * 128:(ct + 1) * 128], rhs=hb[:, kt, :],
                                start=(kt == 0), stop=(kt == 15)),
                                reads=[hkeys[kt], wk], writes=bkeys(b))
                        if kind == "au":
                            S.op("act", lambda e, b=b, h=h: e.activation(out=guT[:, h, :], in_=bank(b),
                                                                         func=AF.Gelu_apprx_tanh),
                                 reads=bkeys(b), writes=["guT"])
                        elif kind == "az":
                            S.op("act", lambda e, b=b, h=h: e.activation(out=szT[:, h, :], in_=bank(b), func=AF.Silu),
                                 reads=bkeys(b), writes=["szT"])
                        elif kind == "q":
                            sg, sk = next_stg()
                            S.op("act", lambda e, b=b, sg=sg: e.activation(out=sg, in_=bank(b), func=AF.Copy,
                                                                           scale=SCALE_Q),
                                 reads=bkeys(b), writes=[sk])
                            S.dma("sp", lambda e, sg=sg, h=h, tok0=tok0: e.dma_start(out=QT[h, :, tok0:tok0 + 512], in_=sg),
                                  reads=[sk])
                        elif kind == "k":
                            sg, sk = next_stg()
                            S.op("dve", lambda e, b=b, sg=sg: e.tensor_copy(out=sg, in_=bank(b)),
                                 reads=bkeys(b), writes=[sk])
                            S.dma("sp", lambda e, sg=sg, h=h, ctx0=ctx0: e.dma_start(out=KT[h, :, ctx0:ctx0 + 512], in_=sg),
                                  reads=[sk])
                        elif kind == "bz":
                            sg, sk = next_stg()
                            S.op("act", lambda e, b=b, sg=sg: e.activation(out=sg, in_=bank(b), func=AF.Silu),
                                 reads=bkeys(b), writes=[sk])
                            S.dma("sp", lambda e, sg=sg, h=h, tok0=tok0: e.dma_start(out=SBZ[h, :, tok0:tok0 + 512], in_=sg),
                                  reads=[sk])
                if own and kind == "az" and cc == 1:
                    for s in range(4):
                        for h in range(8):
                            b6 = 6 + h // 4
                            S.op("pe", lambda e, b6=b6, h=h, s=s: e.matmul(
                                bank(b6)[:, (h % 4) * 128:(h % 4 + 1) * 128], lhsT=vn[:, s, h * 128:(h + 1) * 128],
                                rhs=WcT[:, h, :], start=True, stop=True),
                                reads=["vn%d" % s, "WcT"], writes=bkeys(b6))
                        S.op("dve", lambda e: e.tensor_tensor(out=t1, in0=bank(6, 2), in1=bsb, op=ALU.add),
                             reads=bkeys(6, 2) + ["bsb"], writes=["t1"])
                        S.op("pool", lambda e, s=s: e.tensor_tensor(
                            out=t2.rearrange("p (h t) -> p h t", h=8), in0=t1.rearrange("p (h t) -> p h t", h=8),
                            in1=guT[:, :, s * 128:(s + 1) * 128], op=ALU.mult),
                            reads=["t1", "guT"], writes=["t2"])
                        S.op("pool", lambda e, s=s: e.tensor_tensor(
                            out=oaT[:, :, s * 128:(s + 1) * 128], in0=t2.rearrange("p (h t) -> p h t", h=8),
                            in1=szT[:, :, s * 128:(s + 1) * 128], op=ALU.mult),
                            reads=["t2", "szT"], writes=["oaT"])
                    S.dma("sp", lambda e, tok0=tok0: e.dma_start(
                        out=OA[:, :, tok0:tok0 + 512].rearrange("h d t -> d h t"), in_=oaT), reads=["oaT"])

        for tt in range(0 if skip_l0 else 8):
            do_tile_A(tt)
        S.barrier()
        A.reset(l0_mark)

        if dbg and dbg.get("stop") == "A":
            S.emit()
            return nc

        obT = A.alloc([128, 8, LH], BF16)
        wo_sb = A.alloc([128, 16, D], BF16)
        b_mark = A.mark()
        maskd = A.alloc([128, 4, 512])
        negm = A.alloc([128, 4, 512])
        KTh = [A.alloc([128, 2 * LH], BF16) for _ in range(2)]
        QTh = [A.alloc([128, LH], BF16) for _ in range(2)]
        Vh = [A.alloc([128, 32, 128], BF16) for _ in range(2)]
        SZh = [A.alloc([128, LH], BF16) for _ in range(2)]
        esb = [A.alloc([128, 512]) for _ in range(2)]
        spb = [A.alloc([128, 512], BF16) for _ in range(2)]
        spf = A.alloc([128, 512])
        pre = [A.alloc([128, 512]) for _ in range(2)]
        wsb = [A.alloc([128, 512], BF16) for _ in range(2)]
        Rb = A.alloc([128, 512])
        for j in range(4):
            S.op("pool", lambda e, j=j: e.memset(maskd[:, j, :], 1.0), writes=["maskd"])
            S.op("pool", lambda e, j=j: e.affine_select(out=maskd[:, j, :], in_=maskd[:, j, :], compare_op=ALU.is_gt,
                                                        fill=0.0, base=-128 * j, pattern=[[1, 512]],
                                                        channel_multiplier=-1),
                 reads=["maskd"], writes=["maskd"])
        S.op("pool", lambda e: e.tensor_scalar(out=negm, in0=maskd, scalar1=-1.0, scalar2=30000.0, op0=ALU.add,
                                               op1=ALU.mult), reads=["maskd"], writes=["negm"])
        for kt in range(16):
            S.dma("pool", lambda e, kt=kt: e.dma_start(out=wo_sb[:, kt, :], in_=w_out_ab[kt * 128:(kt + 1) * 128, :]),
                  writes=["wo"])
        tcount = [0]

        def attn_head(h):
            hb2 = h % 2
            kth, qth, vh, szh = KTh[hb2], QTh[hb2], Vh[hb2], SZh[hb2]
            kk, qk, vk, zk = "KTh%d" % hb2, "QTh%d" % hb2, "Vh%d" % hb2, "SZh%d" % hb2
            S.dma("sp", lambda e: e.dma_start(out=kth, in_=KT[h]), writes=[kk])
            S.dma("sp", lambda e: e.dma_start(out=qth, in_=QT[h]), writes=[qk])
            S.dma("sp", lambda e: e.dma_start(out=vh, in_=VS[:, h * 128:(h + 1) * 128].rearrange("(kb t) d -> t kb d", t=128)),
                  writes=[vk])
            S.dma("sp", lambda e: e.dma_start(out=szh, in_=SBZ[h]), writes=[zk])
            for Qg in range(4):
                nkb = 16 + 4 * (Qg + 1)
                po = 6 + (h * 4 + Qg) % 2
                qs = slice(Qg * 512, (Qg + 1) * 512)
                S.op("dve", lambda e: e.memset(Rb, 0.0), writes=["R"])
                for idx, kb in enumerate(range(nkb - 1, -1, -1)):
                    i = tcount[0]
                    tcount[0] += 1
                    zb = i % 4
                    cb = 4 + i % 2
                    j = kb - (16 + 4 * Qg)
                    e_, ek = esb[i % 2], "esb%d" % (i % 2)
                    sp_, spk = spb[i % 2], "spb%d" % (i % 2)
                    pr_, prk = pre[i % 2], "pre%d" % (i % 2)
                    w_, wk = wsb[i % 2], "wsb%d" % (i % 2)
                    S.op("pe", lambda e, zb=zb, kb=kb, qs=qs: e.matmul(bank(zb), lhsT=kth[:, kb * 128:(kb + 1) * 128],
                                                                        rhs=qth[:, qs], start=True, stop=True),
                         reads=[kk, qk], writes=bkeys(zb))
                    S.op("act", lambda e, zb=zb, e_=e_: e.activation(out=e_, in_=bank(zb), func=AF.Exp),
                         reads=bkeys(zb), writes=[ek])
                    if j < 0:
                        S.op("act", lambda e, e_=e_, sp_=sp_: e.activation(out=sp_, in_=e_, func=AF.Ln, bias=1.0),
                             reads=[ek], writes=[spk])
                    else:
                        S.op("act", lambda e, e_=e_: e.activation(out=spf, in_=e_, func=AF.Ln, bias=1.0),
                             reads=[ek], writes=["spf"])
                        S.op("dve", lambda e, sp_=sp_, j=j: e.tensor_tensor(out=sp_, in0=spf, in1=maskd[:, j, :],
                                                                            op=ALU.mult),
                             reads=["spf", "maskd"], writes=[spk])
                    S.op("pe", lambda e, zb=zb, sp_=sp_: e.matmul(bank(zb), lhsT=negUI, rhs=sp_, start=False, stop=True,
                                                                   skip_group_check=True),
                         reads=[spk, "negUI"] + bkeys(zb), writes=bkeys(zb))
                    S.op("pe", lambda e, cb=cb, sp_=sp_: e.matmul(bank(cb), lhsT=negones, rhs=sp_, start=True, stop=True),
                         reads=[spk, "negones"], writes=bkeys(cb))
                    S.op("dve", lambda e, zb=zb, pr_=pr_: e.tensor_tensor(out=pr_, in0=bank(zb), in1=Rb, op=ALU.add),
                         reads=bkeys(zb) + ["R"], writes=[prk])
                    if j >= 0:
                        S.op("pool", lambda e, pr_=pr_, j=j: e.tensor_tensor(out=pr_, in0=pr_, in1=negm[:, j, :],
                                                                             op=ALU.add),
                             reads=[prk, "negm"], writes=[prk])
                    S.op("act", lambda e, pr_=pr_, w_=w_: e.activation(out=w_, in_=pr_, func=AF.Exp),
                         reads=[prk], writes=[wk])
                    S.op("pe", lambda e, po=po, kb=kb, w_=w_, idx=idx, nkb=nkb: e.matmul(
                        bank(po), lhsT=vh[:, kb, :], rhs=w_, start=(idx == 0), stop=(idx == nkb - 1)),
                        reads=[wk, vk], writes=bkeys(po))
                    S.op("dve", lambda e, cb=cb: e.tensor_tensor(out=Rb, in0=Rb, in1=bank(cb), op=ALU.add),
                         reads=bkeys(cb) + ["R"], writes=["R"])
                S.op("dve", lambda e, po=po, qs=qs: e.tensor_tensor(out=obT[:, h, qs], in0=bank(po), in1=szh[:, qs],
                                                                    op=ALU.mult),
                     reads=bkeys(po) + [zk], writes=["obT%d" % Qg])

        nheads = dbg.get("heads", 8) if dbg else 8
        if skip_l0:
            nheads = 0
        for h in range(nheads):
            attn_head(h)
        dump("d_obT", obT, ["obT0", "obT1", "obT2", "obT3"], BF16)
        S.barrier()
        A.reset(b_mark)
        if dbg and dbg.get("stop") == "B":
            S.emit()
            return nc

        oa_sb = [A.alloc([128, 8, 512], BF16) for _ in range(2)]
        xres = [A.alloc([128, D]) for _ in range(2)]
        yt = [A.alloc([128, D]) for _ in range(2)]
        junk2 = A.alloc([128, D], BF16)
        ssc = A.alloc([128, 4])

        def out_tile_C(tt):
            ob = oa_sb[tt % 2]
            ok = "oa_sb%d" % (tt % 2)
            S.dma("sp", lambda e: e.dma_start(out=ob, in_=OA[:, :, tt * 512:(tt + 1) * 512].rearrange("h d t -> d h t")),
                  writes=[ok])

            def catf(kt, s):
                if kt < 8:
                    return ob[:, kt, s * 128:(s + 1) * 128], ok
                return obT[:, kt - 8, tt * 512 + s * 128: tt * 512 + (s + 1) * 128], "obT%d" % tt
            out_proj_resid(0, tt, catf, 16, wo_sb, "wo", x_own, X1, xres, yt, ssc, junk2, tt * 4)

        for tt in range(0 if skip_l0 else 4):
            out_tile_C(tt)
        S.barrier()
        A.reset(l0_mark)
        if dbg and dbg.get("stop") == "C":
            S.emit()
            return nc
        UT = dscr("UT", [8, 128, LH], BF16)
        SZ1 = dscr("SZ1", [8, 128, LH], BF16)
        GT = dscr("GT", [8, 128, LH], BF16)
        l1_mark = A.mark()
        xraw = [A.alloc([128, D]) for _ in range(2)]
        junk = A.alloc([128, D], BF16)
        ssb = A.alloc([128, 12])
        xs = A.alloc([128, 4, D], BF16)
        hT1 = [A.alloc([128, 16, 512], BF16) for _ in range(2)]
        Wd = [A.alloc([128, 16, 512], BF16) for _ in range(2)]
        stg1 = [A.alloc([128, 512], BF16) for _ in range(4)]
        cntD = [0, 0, 0]

        def do_tile_D(tt):
            hb = hT1[tt % 2]
            hk = "hT1%d" % (tt % 2)
            hkeys = ["%s_%d" % (hk, ft) for ft in range(16)]
            norm_transpose_tile(X1, tt * 512, 1, xraw, xs, hb, hk, junk, ssb, tt)
            for cc in range(4):
                i = cntD[0] % 2
                cntD[0] += 1
                W_, wk = Wd[i], "Wd%d" % i
                S.dma("pool", lambda e, W_=W_, cc=cc: e.dma_start(
                    out=W_, in_=w_in_ssm[:, cc * 512:(cc + 1) * 512].rearrange("(kt k) c -> k kt c", k=128)),
                    writes=[wk])
                for ct in range(4):
                    b = 2 + cntD[1] % 4
                    cntD[1] += 1
                    f = (cc % 2) * 4 + ct
                    for kt in range(16):
                        S.op("pe", lambda e, b=b, kt=kt, ct=ct, W_=W_: e.matmul(
                            bank(b), lhsT=W_[:, kt, ct * 128:(ct + 1) * 128], rhs=hb[:, kt, :],
                            start=(kt == 0), stop=(kt == 15)), reads=[hkeys[kt], wk], writes=bkeys(b))
                    si = cntD[2] % 4
                    cntD[2] += 1
                    sg, sk = stg1[si], "stg1%d" % si
                    if cc < 2:
                        S.op("dve", lambda e, b=b, sg=sg: e.tensor_copy(out=sg, in_=bank(b)), reads=bkeys(b), writes=[sk])
                        S.dma("sp", lambda e, sg=sg, f=f: e.dma_start(out=UT[f, :, tt * 512:(tt + 1) * 512], in_=sg),
                              reads=[sk])
                    else:
                        S.op("act", lambda e, b=b, sg=sg: e.activation(out=sg, in_=bank(b), func=AF.Silu),
                             reads=bkeys(b), writes=[sk])
                        S.dma("sp", lambda e, sg=sg, f=f: e.dma_start(out=SZ1[f, :, tt * 512:(tt + 1) * 512], in_=sg),
                              reads=[sk])

        for tt in range(4):
            do_tile_D(tt)
        S.barrier()
        A.reset(l1_mark)
        if dbg and dbg.get("stop") == "D":
            S.emit()
            return nc

        Kblk = A.alloc([128, 8, 16, 128], BF16)
        Csm_re = A.alloc([128, 32, 32])
        Csm_nim = A.alloc([128, 32, 32])
        pw_re = A.alloc([128, 17, 32])
        pw_im = A.alloc([128, 17, 32])
        Ak_re = A.alloc([128, 7, 32])
        Ak_im = A.alloc([128, 7, 32])
        Sst_re = A.alloc([128, 32, 128], BF16)
        Sst_im = A.alloc([128, 32, 128], BF16)
        dsk = A.alloc([128, 8])
        p3_mark = A.mark()
        Bb_re = A.alloc([128, 32, 32])
        Bb_im = A.alloc([128, 32, 32])
        S_re = A.alloc([128, 32, 132])
        S_im = A.alloc([128, 32, 132])
        p2_mark = A.mark()
        prm = A.alloc([128, 16, 32])
        kint = A.alloc([128, 32], I32)
        Braw_re = A.alloc([128, 32, 32])
        Braw_im = A.alloc([128, 32, 32])
        tA = A.alloc([128, 32, 32])
        tB = A.alloc([128, 32, 32])
        BendT = A.alloc([128, 4, 16, 2, 128], BF16)
        uTf = [A.alloc([128, LH], BF16) for _ in range(2)]
        uz = [A.alloc([128, LH], BF16) for _ in range(4)]
        bmask = A.alloc([128, 128])
        ktmp = A.alloc([128, 128])
        m01 = A.alloc([128, 2])
        S.op("pool", lambda e: e.memset(bmask, 1.0), writes=["bmask"])
        for j in range(4):
            S.op("pool", lambda e, j=j: e.affine_select(out=bmask[:, 32 * j:32 * j + 32], in_=bmask[:, 32 * j:32 * j + 32],
                                                        compare_op=ALU.is_ge, fill=0.0, base=-32 * j, pattern=[[0, 32]],
                                                        channel_multiplier=1), reads=["bmask"], writes=["bmask"])
            S.op("pool", lambda e, j=j: e.affine_select(out=bmask[:, 32 * j:32 * j + 32], in_=bmask[:, 32 * j:32 * j + 32],
                                                        compare_op=ALU.is_ge, fill=0.0, base=32 * j + 31, pattern=[[0, 32]],
                                                        channel_multiplier=-1), reads=["bmask"], writes=["bmask"])
        S.op("pool", lambda e: e.tensor_tensor(out=m01[:, 0:1], in0=bmask[:, 0:1], in1=bmask[:, 64:65], op=ALU.add),
             reads=["bmask"], writes=["m01"])
        S.op("pool", lambda e: e.tensor_tensor(out=m01[:, 1:2], in0=bmask[:, 32:33], in1=bmask[:, 96:97], op=ALU.add),
             reads=["bmask"], writes=["m01"])

        TWO_PI = 2.0 * math.pi
        LR, LI, DT, MAG, ANG, U, FR, SN, CS, DEN, NRm, T0, T1, CR, CI, T2 = [prm[:, i, :] for i in range(16)]
        S.dma("sp", lambda e: e.dma_start(out=LR, in_=lamre_sm), writes=["prm"])
        S.dma("sp", lambda e: e.dma_start(out=LI, in_=lamim_sm), writes=["prm"])
        S.dma("sp", lambda e: e.dma_start(out=DT, in_=logdt_sm), writes=["prm"])
        S.dma("sp", lambda e: e.dma_start(out=Braw_re, in_=bre_smz), writes=["Braw"])
        S.dma("sp", lambda e: e.dma_start(out=Braw_im, in_=bim_smz), writes=["Braw"])
        S.dma("sp", lambda e: e.dma_start(out=Csm_re, in_=cre_smz), writes=["Csm"])
        S.dma("sp", lambda e: e.dma_start(out=Csm_nim, in_=cim_smz), writes=["Csm"])
        S.dma("sp", lambda e: e.dma_start(out=dsk, in_=d_skipT), writes=["dsk"])

        def P(eng, fn):
            S.op(eng, fn, reads=["prm", "Braw", "Csm"], writes=["prm"])

        P("act", lambda e: e.activation(out=DT, in_=DT, func=AF.Exp))
        P("dve", lambda e: e.tensor_tensor(out=MAG, in0=LR, in1=DT, op=ALU.mult))
        P("act", lambda e: e.activation(out=MAG, in_=MAG, func=AF.Exp))
        P("dve", lambda e: e.tensor_tensor(out=ANG, in0=LI, in1=DT, op=ALU.mult))
        P("dve", lambda e: e.tensor_scalar(out=U, in0=ANG, scalar1=1.0 / TWO_PI, scalar2=None, op0=ALU.mult))
        P("dve", lambda e: e.tensor_copy(out=kint, in_=U))
        P("dve", lambda e: e.tensor_copy(out=FR, in_=kint))
        P("dve", lambda e: e.tensor_tensor(out=FR, in0=U, in1=FR, op=ALU.subtract))
        P("act", lambda e: e.activation(out=SN, in_=FR, func=AF.Sin, scale=TWO_PI))
        P("dve", lambda e: e.tensor_scalar(out=U, in0=U, scalar1=0.25, scalar2=None, op0=ALU.add))
        P("dve", lambda e: e.tensor_copy(out=kint, in_=U))
        P("dve", lambda e: e.tensor_copy(out=FR, in_=kint))
        P("dve", lambda e: e.tensor_tensor(out=FR, in0=U, in1=FR, op=ALU.subtract))
        P("act", lambda e: e.activation(out=CS, in_=FR, func=AF.Sin, scale=TWO_PI))
        P("dve", lambda e: e.memset(pw_re[:, 0, :], 1.0))
        P("dve", lambda e: e.memset(pw_im[:, 0, :], 0.0))
        P("dve", lambda e: e.tensor_tensor(out=pw_re[:, 1, :], in0=MAG, in1=CS, op=ALU.mult))
        P("dve", lambda e: e.tensor_tensor(out=pw_im[:, 1, :], in0=MAG, in1=SN, op=ALU.mult))
        P("dve", lambda e: e.tensor_tensor(out=DEN, in0=LR, in1=LR, op=ALU.mult))
        P("dve", lambda e: e.tensor_tensor(out=T0, in0=LI, in1=LI, op=ALU.mult))
        P("dve", lambda e: e.tensor_tensor(out=DEN, in0=DEN, in1=T0, op=ALU.add))
        P("dve", lambda e: e.reciprocal(out=DEN, in_=DEN))
        P("dve", lambda e: e.tensor_scalar(out=NRm, in0=pw_re[:, 1, :], scalar1=-1.0, scalar2=None, op0=ALU.add))
        P("dve", lambda e: e.tensor_tensor(out=T0, in0=NRm, in1=LR, op=ALU.mult))
        P("dve", lambda e: e.tensor_tensor(out=T1, in0=pw_im[:, 1, :], in1=LI, op=ALU.mult))
        P("dve", lambda e: e.tensor_tensor(out=T0, in0=T0, in1=T1, op=ALU.add))
        P("dve", lambda e: e.tensor_tensor(out=CR, in0=T0, in1=DEN, op=ALU.mult))
        P("dve", lambda e: e.tensor_tensor(out=T0, in0=pw_im[:, 1, :], in1=LR, op=ALU.mult))
        P("dve", lambda e: e.tensor_tensor(out=T1, in0=NRm, in1=LI, op=ALU.mult))
        P("dve", lambda e: e.tensor_tensor(out=T0, in0=T0, in1=T1, op=ALU.subtract))
        P("dve", lambda e: e.tensor_tensor(out=CI, in0=T0, in1=DEN, op=ALU.mult))

        def bc(ap2, n):
            return ap2.unsqueeze(2).broadcast_to([128, 32, n])

        def cmul3(eng, o_re, o_im, a_re, a_im, b_re, b_im, ta, tb, rk, wk, conj=False):
            op_re = ALU.add if conj else ALU.subtract
            S.op(eng, lambda e: e.tensor_tensor(out=ta, in0=b_re, in1=a_re, op=ALU.mult), reads=rk, writes=["cm_ta"])
            S.op(eng, lambda e: e.tensor_tensor(out=tb, in0=b_im, in1=a_im, op=ALU.mult), reads=rk, writes=["cm_tb"])
            S.op(eng, lambda e: e.tensor_tensor(out=o_re, in0=ta, in1=tb, op=op_re),
                 reads=["cm_ta", "cm_tb"], writes=wk)
            op_im = ALU.subtract if conj else ALU.add
            S.op(eng, lambda e: e.tensor_tensor(out=ta, in0=b_im, in1=a_re, op=ALU.mult), reads=rk, writes=["cm_ta"])
            S.op(eng, lambda e: e.tensor_tensor(out=tb, in0=b_re, in1=a_im, op=ALU.mult), reads=rk, writes=["cm_tb"])
            S.op(eng, lambda e: e.tensor_tensor(out=o_im, in0=ta, in1=tb, op=op_im),
                 reads=["cm_ta", "cm_tb"], writes=wk)

        cmul3("dve", Bb_re, Bb_im, bc(CR, 32), bc(CI, 32), Braw_re, Braw_im, tA, tB, ["prm", "Braw"], ["Bb"])
        S.op("pool", lambda e: e.tensor_scalar(out=Csm_nim, in0=Csm_nim, scalar1=-1.0, scalar2=None, op0=ALU.mult),
             reads=["Csm"], writes=["Csm"])
        for k in range(1, 16):
            cmul3("dve", pw_re[:, k + 1, :], pw_im[:, k + 1, :], pw_re[:, 1, :], pw_im[:, 1, :],
                  pw_re[:, k, :], pw_im[:, k, :], tA[:, 0, :], tB[:, 0, :], ["prm", "pw"], ["pw"])
        S.op("dve", lambda e: e.tensor_copy(out=Ak_re[:, 0, :], in_=pw_re[:, 16, :]), reads=["pw"], writes=["Ak"])
        S.op("dve", lambda e: e.tensor_copy(out=Ak_im[:, 0, :], in_=pw_im[:, 16, :]), reads=["pw"], writes=["Ak"])
        for k in range(6):
            cmul3("dve", Ak_re[:, k + 1, :], Ak_im[:, k + 1, :], Ak_re[:, k, :], Ak_im[:, k, :],
                  Ak_re[:, k, :], Ak_im[:, k, :], tA[:, 0, :], tB[:, 0, :], ["Ak"], ["Ak"])
        S.op("dve", lambda e: e.memset(S_re[:, :, 0:4], 0.0), writes=["S_re"])
        S.op("dve", lambda e: e.memset(S_im[:, :, 0:4], 0.0), writes=["S_im"])

        if dbg and dbg.get("stop") == "S1a":
            dump("d_pw_re", pw_re, ["pw"])
            dump("d_pw_im", pw_im, ["pw"])
            S.emit()
            return nc
        bsm_re = Braw_re
        bsm_im = Braw_im
        ucount = [0]
        for half in range(2):
            for tau in range(16):
                k = 15 - tau
                cmul3("dve", bsm_re, bsm_im, bc(pw_re[:, k, :], 32), bc(pw_im[:, k, :], 32), Bb_re, Bb_im, tA, tB,
                      ["pw", "Bb", "Braw"], ["bsm"])
                for fl in range(4):
                    f = half * 4 + fl
                    tb0 = 4 * (fl % 2)
                    kb1 = 1 + 4 * (fl % 2)
                    for ri, src in ((0, bsm_re), (1, bsm_im)):
                        S.op("pe", lambda e, src=src, f=f, ri=ri, tb0=tb0: e.transpose(
                            out=bank(tb0)[:, ri * 128:(ri + 1) * 128],
                            in_=src[:, 4 * f:4 * f + 4, :].rearrange("p a b -> p (a b)"), identity=identf),
                            reads=["bsm", "identf"], writes=bkeys(tb0))
                    S.op("act", lambda e, fl=fl, tau=tau, tb0=tb0: e.activation(
                        out=BendT[:, fl, tau, :, :].rearrange("p a b -> p (a b)"), in_=bank(tb0)[:, 0:256], func=AF.Copy),
                        reads=bkeys(tb0), writes=["BendT"])
                    for j in range(4):
                        gp = 4 * f + j
                        S.op("pe", lambda e, gp=gp, j=j, kb1=kb1, f=f: e.matmul(
                            bank(kb1)[:, 32 * j:32 * j + 32],
                            lhsT=bsm_re[:, 4 * f:4 * f + 4, :].rearrange("p a b -> p (a b)"), rhs=Csm_re[:, gp, :],
                            start=True, stop=False), reads=["bsm", "Csm"], writes=bkeys(kb1))
                        S.op("pe", lambda e, gp=gp, j=j, kb1=kb1, f=f: e.matmul(
                            bank(kb1)[:, 32 * j:32 * j + 32],
                            lhsT=bsm_im[:, 4 * f:4 * f + 4, :].rearrange("p a b -> p (a b)"), rhs=Csm_nim[:, gp, :],
                            start=False, stop=True), reads=["bsm", "Csm"], writes=bkeys(kb1))
                    if k == 0:
                        S.op("dve", lambda e, kb1=kb1: e.tensor_tensor(out=ktmp, in0=bank(kb1)[:, 0:128], in1=bmask,
                                                                       op=ALU.mult),
                             reads=bkeys(kb1) + ["bmask"], writes=["ktmp"])
                        S.op("dve", lambda e, f=f: e.scalar_tensor_tensor(out=Kblk[:, f, 0, :], in0=identf,
                                                                          scalar=dsk[:, f:f + 1], in1=ktmp,
                                                                          op0=ALU.mult, op1=ALU.add),
                             reads=["ktmp", "dsk", "identf"], writes=["Kblk"])
                    else:
                        S.op("dve", lambda e, f=f, k=k, kb1=kb1: e.tensor_tensor(out=Kblk[:, f, k, :],
                                                                                 in0=bank(kb1)[:, 0:128], in1=bmask,
                                                                                 op=ALU.mult),
                             reads=bkeys(kb1) + ["bmask"], writes=["Kblk"])
            if dbg and dbg.get("stop") == "S1b":
                dump("d_Kblk", Kblk, ["Kblk"], BF16)
                S.emit()
                return nc
            for fl in range(4):
                f = half * 4 + fl
                ui = ucount[0] % 2
                ucount[0] += 1
                u_, uk = uTf[ui], "uTf%d" % ui
                S.dma("sp", lambda e, u_=u_, f=f: e.dma_start(out=u_, in_=UT[f]), writes=[uk])
                uv = u_.rearrange("p (n t) -> p t n", t=16)
                uzv = [uz[i].rearrange("p (t n) -> p t n", t=16) for i in range(4)]
                for i in range(4):
                    S.op("dve", lambda e, uv=uv, uzv=uzv, i=i: e.tensor_scalar(out=uzv[i], in0=uv,
                                                                              scalar1=bmask[:, 32 * i:32 * i + 1],
                                                                              scalar2=None, op0=ALU.mult),
                         reads=[uk, "bmask"], writes=["uz%d" % i])
                bmode = dbg.get("bmode", 9) if dbg else 9
                for j in range(4 if bmode >= 1 else 0):
                    gp = 4 * f + j
                    pb = 2 + (gp % 2)
                    w0 = 64 * (j // 2)
                    jj = j % 2
                    for ri in range(2):
                        for tau in range(16):
                            S.op("pe", lambda e, pb=pb, ri=ri, tau=tau, fl=fl, j=j, uzv=uzv: e.matmul(
                                bank(pb)[:, ri * 128:(ri + 1) * 128], lhsT=BendT[:, fl, tau, ri, :],
                                rhs=uzv[j][:, tau, :], start=(tau == 0), stop=(tau == 15)),
                                reads=["BendT", "uz%d" % j], writes=bkeys(pb))
                    if bmode < 2:
                        continue
                    S.op("dve", lambda e, pb=pb, gp=gp: e.tensor_copy(out=S_re[:, gp, 4:132], in_=bank(pb)[:, 0:128]),
                         reads=bkeys(pb), writes=["S_re"])
                    S.op("dve", lambda e, pb=pb, gp=gp: e.tensor_copy(out=S_im[:, gp, 4:132], in_=bank(pb)[:, 128:256]),
                         reads=bkeys(pb), writes=["S_im"])
        dump("d_pw_re", pw_re, ["pw"])
        dump("d_pw_im", pw_im, ["pw"])
        dump("d_Sloc_re", S_re, ["S_re"])
        dump("d_Sloc_im", S_im, ["S_im"])
        dump("d_Kblk", Kblk, ["Kblk"], BF16)
        S.barrier()
        A.reset(p2_mark)
        if dbg and dbg.get("stop") == "S1":
            S.emit()
            return nc

        E_re = A.alloc([128, 32, 128])
        E_im = A.alloc([128, 32, 128])
        sc = [A.alloc([128, 16, 128]) for _ in range(4)]
        et = [A.alloc([128, 32 * 64]) for _ in range(2)]
        Sp = A.alloc([128, 64])
        Sp2 = A.alloc([128, 64])
        S.op("pool", lambda e: e.tensor_copy(out=E_re[:, :, 0:1], in_=Ak_re[:, 0, :].unsqueeze(2)), writes=["E"])
        S.op("pool", lambda e: e.tensor_copy(out=E_im[:, :, 0:1], in_=Ak_im[:, 0, :].unsqueeze(2)), writes=["E"])
        for k in range(7):
            w = 1 << k
            ta = et[0][:, 0:32 * w].rearrange("p (a b) -> p a b", a=32)
            tb = et[1][:, 0:32 * w].rearrange("p (a b) -> p a b", a=32)
            cmul3("pool", E_re[:, :, w:2 * w], E_im[:, :, w:2 * w], bc(Ak_re[:, k, :], w), bc(Ak_im[:, k, :], w),
                  E_re[:, :, 0:w], E_im[:, :, 0:w], ta, tb, ["E"], ["E"])
        for k in range(7):
            w = 1 << k
            L = 128 - w
            for hf in range(2):
                g0 = hf * 16
                are = Ak_re[:, k, g0:g0 + 16].unsqueeze(2).broadcast_to([128, 16, L])
                aim = Ak_im[:, k, g0:g0 + 16].unsqueeze(2).broadcast_to([128, 16, L])
                sre_lo = S_re[:, g0:g0 + 16, 4:4 + L]
                sim_lo = S_im[:, g0:g0 + 16, 4:4 + L]
                sre_hi = S_re[:, g0:g0 + 16, 4 + w:132]
                sim_hi = S_im[:, g0:g0 + 16, 4 + w:132]
                t = [sc[i][:, :, 0:L] for i in range(4)]
                S.op("dve", lambda e, t=t, are=are, sre_lo=sre_lo: e.tensor_tensor(out=t[0], in0=sre_lo, in1=are, op=ALU.mult),
                     reads=["S_re"], writes=["sc0"])
                S.op("dve", lambda e, t=t, aim=aim, sim_lo=sim_lo: e.tensor_tensor(out=t[1], in0=sim_lo, in1=aim, op=ALU.mult),
                     reads=["S_im"], writes=["sc1"])
                S.op("pool", lambda e, t=t, are=are, sim_lo=sim_lo: e.tensor_tensor(out=t[2], in0=sim_lo, in1=are, op=ALU.mult),
                     reads=["S_im"], writes=["sc2"])
                S.op("pool", lambda e, t=t, aim=aim, sre_lo=sre_lo: e.tensor_tensor(out=t[3], in0=sre_lo, in1=aim, op=ALU.mult),
                     reads=["S_re"], writes=["sc3"])
                S.op("dve", lambda e, t=t, sre_hi=sre_hi: e.tensor_tensor(out=sre_hi, in0=sre_hi, in1=t[0], op=ALU.add),
                     reads=["sc0", "S_re"], writes=["S_re"])
                S.op("dve", lambda e, t=t, sre_hi=sre_hi: e.tensor_tensor(out=sre_hi, in0=sre_hi, in1=t[1], op=ALU.subtract),
                     reads=["sc1", "S_re"], writes=["S_re"])
                S.op("pool", lambda e, t=t, sim_hi=sim_hi: e.tensor_tensor(out=sim_hi, in0=sim_hi, in1=t[2], op=ALU.add),
                     reads=["sc2", "S_im"], writes=["S_im"])
                S.op("pool", lambda e, t=t, sim_hi=sim_hi: e.tensor_tensor(out=sim_hi, in0=sim_hi, in1=t[3], op=ALU.add),
                     reads=["sc3", "S_im"], writes=["S_im"])
        SK = ["S_re", "S_im"]
        dump("d_S_re", S_re, SK)
        dump("d_S_im", S_im, SK)
        S.op("dve", lambda e: e.tensor_copy(out=Sp[:, 0:32], in_=S_re[:, :, 131]), reads=SK, writes=["Sp"])
        S.op("dve", lambda e: e.tensor_copy(out=Sp[:, 32:64], in_=S_im[:, :, 131]), reads=SK, writes=["Sp"])
        S.dma("pool", lambda e: e.dma_start(out=st_in[:, :], in_=Sp), reads=["Sp"], writes=["st_in"])
        ncores = dbg.get("ncores", 8) if dbg else 8
        groups = [[2 * i, 2 * i + 1] for i in range(ncores // 2)] if ncores > 1 else [[0]]
        S.op("pool", lambda e: e.collective_compute("AllGather", ALU.bypass, replica_groups=groups,
                                                    ins=[st_in.ap().opt()], outs=[st_out.ap().opt()]),
             reads=["st_in"], writes=["st_out"])
        S.dma("pool", lambda e: e.dma_start(out=Sp2, in_=st_out[0:128, :]), reads=["st_out"], writes=["Sp2"])
        S.op("dve", lambda e: e.tensor_scalar(out=Sp2, in0=Sp2, scalar1=flag_sb, scalar2=None, op0=ALU.mult),
             reads=["Sp2", "flag"], writes=["Sp2"])
        S.op("dve", lambda e: e.tensor_copy(out=S_re[:, :, 3:4], in_=Sp2[:, 0:32].unsqueeze(2)), reads=["Sp2"] + SK,
             writes=["S_re"])
        S.op("dve", lambda e: e.tensor_copy(out=S_im[:, :, 3:4], in_=Sp2[:, 32:64].unsqueeze(2)), reads=["Sp2"] + SK,
             writes=["S_im"])
        for hf in range(2):
            g0 = hf * 16
            pre_ = Sp2[:, g0:g0 + 16].unsqueeze(2).broadcast_to([128, 16, 128])
            pim_ = Sp2[:, 32 + g0:32 + g0 + 16].unsqueeze(2).broadcast_to([128, 16, 128])
            ere = E_re[:, g0:g0 + 16, :]
            eim = E_im[:, g0:g0 + 16, :]
            sre = S_re[:, g0:g0 + 16, 4:132]
            sim = S_im[:, g0:g0 + 16, 4:132]
            S.op("dve", lambda e, ere=ere, pre_=pre_: e.tensor_tensor(out=sc[0], in0=ere, in1=pre_, op=ALU.mult),
                 reads=["E", "Sp2"], writes=["sc0"])
            S.op("dve", lambda e, eim=eim, pim_=pim_: e.tensor_tensor(out=sc[1], in0=eim, in1=pim_, op=ALU.mult),
                 reads=["E", "Sp2"], writes=["sc1"])
            S.op("pool", lambda e, ere=ere, pim_=pim_: e.tensor_tensor(out=sc[2], in0=ere, in1=pim_, op=ALU.mult),
                 reads=["E", "Sp2"], writes=["sc2"])
            S.op("pool", lambda e, eim=eim, pre_=pre_: e.tensor_tensor(out=sc[3], in0=eim, in1=pre_, op=ALU.mult),
                 reads=["E", "Sp2"], writes=["sc3"])
            S.op("dve", lambda e, sre=sre: e.tensor_tensor(out=sre, in0=sre, in1=sc[0], op=ALU.add),
                 reads=["sc0", "S_re"], writes=["S_re"])
            S.op("dve", lambda e, sre=sre: e.tensor_tensor(out=sre, in0=sre, in1=sc[1], op=ALU.subtract),
                 reads=["sc1", "S_re"], writes=["S_re"])
            S.op("pool", lambda e, sim=sim: e.tensor_tensor(out=sim, in0=sim, in1=sc[2], op=ALU.add),
                 reads=["sc2", "S_im"], writes=["S_im"])
            S.op("pool", lambda e, sim=sim: e.tensor_tensor(out=sim, in0=sim, in1=sc[3], op=ALU.add),
                 reads=["sc3", "S_im"], writes=["S_im"])
        S.op("dve", lambda e: e.tensor_copy(out=Sst_re, in_=S_re[:, :, 3:131]), reads=["S_re"], writes=["Sst"])
        S.op("pool", lambda e: e.tensor_copy(out=Sst_im, in_=S_im[:, :, 3:131]), reads=["S_im"], writes=["Sst"])
        S.barrier()
        A.reset(p3_mark)

        Cp = A.alloc([128, 32, 16, 2, 32], BF16)
        tA = A.alloc([128, 32, 32])
        tB = A.alloc([128, 32, 32])
        uTf = [A.alloc([128, LH], BF16) for _ in range(2)]
        upm = [A.alloc([128, LH], BF16) for _ in range(2)]
        gch = [A.alloc([128, 16, 128], BF16) for _ in range(2)]
        gTf = [A.alloc([128, LH], BF16) for _ in range(2)]
        for tau in range(16):
            cmul3("dve", Cp[:, :, tau, 0, :], Cp[:, :, tau, 1, :], bc(pw_re[:, tau + 1, :], 32),
                  bc(pw_im[:, tau + 1, :], 32), Csm_re, Csm_nim, tA, tB, ["pw", "Csm"], ["Cp"], conj=True)
        for f in range(8):
            u_, uk = uTf[f % 2], "uTf%d" % (f % 2)
            S.dma("sp", lambda e, u_=u_, f=f: e.dma_start(out=u_, in_=UT[f]), writes=[uk])
            uv = upm[f % 2].rearrange("p (t n) -> p t n", t=16)
            upk = "upm%d" % (f % 2)
            S.op("pool", lambda e, uv=uv, u_=u_: e.tensor_copy(out=uv, in_=u_.rearrange("p (n t) -> p t n", t=16)),
                 reads=[uk], writes=[upk])
            uk = upk
            yv = bank(0, 4).rearrange("p (t c) -> p t c", t=16)
            S.op("dve", lambda e: e.memset(bank(0, 4), 0.0), writes=bkeys(0, 4))
            for tau in range(16):
                for tp in range(tau + 1):
                    S.op("pe", lambda e, yv=yv, uv=uv, tau=tau, tp=tp, f=f: e.matmul(
                        yv[:, tau, :], lhsT=uv[:, tp, :], rhs=Kblk[:, f, tau - tp, :], start=False, stop=False,
                        skip_group_check=True), reads=["Kblk", uk] + bkeys(0, 4), writes=bkeys(0, 4))
                for j in range(4):
                    gp = 4 * f + j
                    S.op("pe", lambda e, yv=yv, tau=tau, gp=gp, j=j: e.matmul(
                        yv[:, tau, 32 * j:32 * j + 32], lhsT=Sst_re[:, gp, :], rhs=Cp[:, gp, tau, 0, :], start=False,
                        stop=False, skip_group_check=True), reads=["Cp", "Sst"] + bkeys(0, 4), writes=bkeys(0, 4))
                    S.op("pe", lambda e, yv=yv, tau=tau, gp=gp, j=j: e.matmul(
                        yv[:, tau, 32 * j:32 * j + 32], lhsT=Sst_im[:, gp, :], rhs=Cp[:, gp, tau, 1, :], start=False,
                        stop=False, skip_group_check=True), reads=["Cp", "Sst"] + bkeys(0, 4), writes=bkeys(0, 4))
            if dbg and dbg.get("dump") and f == 0:
                ydump = A.alloc([128, 2048])
                S.op("act", lambda e: e.activation(out=ydump, in_=bank(0, 4), func=AF.Copy), reads=bkeys(0, 4),
                     writes=["ydump"])
                dump("d_yssm0", ydump, ["ydump"])
            gc_, gck = gch[f % 2], "gch%d" % (f % 2)
            S.op("act", lambda e, gc_=gc_: e.activation(out=gc_.rearrange("p a b -> p (a b)"), in_=bank(0, 4),
                                                        func=AF.Gelu_apprx_tanh), reads=bkeys(0, 4), writes=[gck])
            ptb = bank(4, 2).bitcast(BF16)[:, 0:2048].rearrange("p (t n) -> p t n", t=16)
            for tau in range(16):
                S.op("pe", lambda e, gc_=gc_, tau=tau, ptb=ptb: e.transpose(out=ptb[:, tau, :], in_=gc_[:, tau, :],
                                                                           identity=ident),
                     reads=[gck, "ident"], writes=bkeys(4, 2))
            g_, gk = gTf[f % 2], "gTf%d" % (f % 2)
            S.op("dve", lambda e, g_=g_, ptb=ptb: e.tensor_copy(out=g_.rearrange("p (n t) -> p t n", t=16), in_=ptb),
                 reads=bkeys(4, 2), writes=[gk])
            S.dma("sp", lambda e, g_=g_, f=f: e.dma_start(out=GT[f], in_=g_), reads=[gk])
        S.barrier()
        A.reset(l1_mark)
        if dbg and dbg.get("stop") == "S3":
            S.emit()
            return nc

        wg_sb = A.alloc([128, 8, 1024], BF16)
        wo2_sb = A.alloc([128, 8, D], BF16)
        bglu = A.alloc([128, 8])
        gtl = [A.alloc([128, 8, 512], BF16) for _ in range(2)]
        szl = [A.alloc([128, 8, 512], BF16) for _ in range(2)]
        yyT = [A.alloc([128, 8, 512], BF16) for _ in range(2)]
        sig = [A.alloc([128, 512]) for _ in range(2)]
        xres = [A.alloc([128, D]) for _ in range(2)]
        yt = [A.alloc([128, D]) for _ in range(2)]
        junk2 = A.alloc([128, D], BF16)
        ssc = A.alloc([128, 4])
        S.dma("sp", lambda e: e.dma_start(out=bglu, in_=b_gluT), writes=["bglu"])
        for kt in range(8):
            S.dma("pool", lambda e, kt=kt: e.dma_start(out=wg_sb[:, kt, :], in_=w_glu[kt * 128:(kt + 1) * 128, :]),
                  writes=["wg"])
        for kt in range(8):
            S.dma("pool", lambda e, kt=kt: e.dma_start(out=wo2_sb[:, kt, :], in_=w_out_ssm[kt * 128:(kt + 1) * 128, :]),
                  writes=["wo2"])
        fcnt = [0]

        def do_tile_F(tt):
            i2 = tt % 2
            g_, gk = gtl[i2], "gtl%d" % i2
            z_, zk = szl[i2], "szl%d" % i2
            y_, yk = yyT[i2], "yyT%d" % i2
            ts = slice(tt * 512, (tt + 1) * 512)
            S.dma("sp", lambda e: e.dma_start(out=g_, in_=GT[:, :, ts].rearrange("f d t -> d f t")), writes=[gk])
            S.dma("sp", lambda e: e.dma_start(out=z_, in_=SZ1[:, :, ts].rearrange("f d t -> d f t")), writes=[zk])
            for ct in range(8):
                b = fcnt[0] % 2
                fcnt[0] += 1
                sg, sgk = sig[b], "sig%d" % b
                for kt in range(8):
                    S.op("pe", lambda e, b=b, kt=kt, ct=ct: e.matmul(
                        bank(b), lhsT=wg_sb[:, kt, ct * 128:(ct + 1) * 128], rhs=g_[:, kt, :],
                        start=(kt == 0), stop=(kt == 7)), reads=["wg", gk], writes=bkeys(b))
                S.op("act", lambda e, b=b, sg=sg, ct=ct: e.activation(out=sg, in_=bank(b), func=AF.Sigmoid,
                                                                      bias=bglu[:, ct:ct + 1]),
                     reads=bkeys(b) + ["bglu"], writes=[sgk])
                S.op("dve", lambda e, sg=sg, ct=ct: e.tensor_tensor(out=sg, in0=sg, in1=g_[:, ct, :], op=ALU.mult),
                     reads=[sgk, gk], writes=[sgk])
                S.op("pool", lambda e, sg=sg, ct=ct: e.tensor_tensor(out=y_[:, ct, :], in0=sg, in1=z_[:, ct, :],
                                                                     op=ALU.mult),
                     reads=[sgk, zk], writes=[yk])

            def catf(kt, s):
                return y_[:, kt, s * 128:(s + 1) * 128], yk
            out_proj_resid(1, tt, catf, 8, wo2_sb, "wo2", X1, out, xres, yt, ssc, junk2, tt * 4)

        for tt in range(4):
            do_tile_F(tt)
        S.emit()
    return nc


def prep_inputs(inputs):
    f = lambda a: np.ascontiguousarray(np.asarray(a, dtype=np.float32))
    x = f(inputs["x"])
    c = f(inputs["c"])
    fm = lambda v, n: np.ascontiguousarray(v.reshape(n, 128).T)
    shared = {
        "g_preT": np.stack([fm(f(inputs["ln_pre_g"])[l], 16) for l in range(2)]),
        "g_post": f(inputs["ln_post_g"]),
        "w_mod": f(inputs["w_mod"]),
        "b_modT": np.stack([fm(f(inputs["b_mod"])[l], 48) for l in range(2)]),
        "b_mod": f(inputs["b_mod"]),
        "w_in_ab": f(inputs["w_in_ab"])[0],
        "w_out_ab": f(inputs["w_out_ab"])[0],
        "sgu_norm_g": f(inputs["sgu_norm_g"])[0],
        "sgu_w": f(inputs["sgu_w"])[0],
        "sgu_b": f(inputs["sgu_b"])[0].reshape(1024),
        "w_in_ssm": f(inputs["w_in_ssm"])[0],
        "w_out_ssm": f(inputs["w_out_ssm"])[0],
        "w_glu": f(inputs["w_glu"])[0],
        "b_gluT": fm(f(inputs["b_glu"])[0], 8),
        "d_skipT": fm(f(inputs["d_skip"])[0], 8),
    }
    def sm(a):
        return np.ascontiguousarray(a.reshape(32, 2, 64).transpose(1, 2, 0).reshape(128, 32))
    shared["lamre_sm"] = sm(f(inputs["lam_re"])[0])
    shared["lamim_sm"] = sm(f(inputs["lam_im"])[0])
    shared["logdt_sm"] = sm(np.repeat(f(inputs["log_dt"])[0][:, None], 64, axis=1))

    def smz_b(b):
        o = np.zeros((2, 64, 32, 2, 16), np.float32)
        bb = b.reshape(32, 2, 64, 16)
        for g2 in range(2):
            o[g2, :, :, g2, :] = bb[:, g2].transpose(1, 0, 2)
        return np.ascontiguousarray(o.reshape(128, 32, 32))

    def smz_c(cm):
        return smz_b(np.ascontiguousarray(cm.transpose(0, 2, 1)))
    shared["bre_smz"] = smz_b(f(inputs["b_re"])[0])
    shared["bim_smz"] = smz_b(f(inputs["b_im"])[0])
    shared["cre_smz"] = smz_c(f(inputs["c_re"])[0])
    shared["cim_smz"] = smz_c(f(inputs["c_im"])[0])
    in_maps = []
    for core in range(8):
        b, s = core // 2, core % 2
        m = dict(shared)
        m["x_own"] = np.ascontiguousarray(x[b, s * LH:(s + 1) * LH])
        m["x_prev"] = np.ascontiguousarray(x[b, 0:LH])
        m["flag"] = np.full((128, 1), float(s), np.float32)
        m["cT"] = fm(c[b], 16)
        in_maps.append(m)
    return in_maps


def kernel(**inputs):
    nc = build_program()
    in_maps = prep_inputs(inputs)
    res = run_bass_kernel_spmd(nc, in_maps, core_ids=list(range(8)))
    out = np.empty((4, 2 * LH, D), np.float32)
    for core in range(8):
        b, s = core // 2, core % 2
        out[b, s * LH:(s + 1) * LH] = np.asarray(res.results[core]["out"], np.float32)
    return out
```

```python
import contextlib
import math
import numpy as np
import concourse.bass as bass
import concourse.mybir as mybir
from concourse.bass_utils import run_bass_kernel_spmd

F32 = mybir.dt.float32
BF16 = mybir.dt.bfloat16
I32 = mybir.dt.int32
AF = mybir.ActivationFunctionType
ALU = mybir.AluOpType
AX = mybir.AxisListType

D = 2048
LH = 2048
NT = 4
EPS = 1e-6
T16 = 16
NCH = LH // T16


class Sched:
    ENGS = ("pe", "act", "dve", "pool", "sp")
    NSLOT = 8

    def __init__(self, nc):
        self.nc = nc
        self.ops = []
        self.last_w = {}
        self.readers = {}
        self.dma_rr = {e: 0 for e in self.ENGS}
        self.slot_last = {}
        self.last_op = {}

    def _add(self, engine, fn, reads, writes, dma, extra=()):
        deps = set(extra)
        for b in reads:
            if b in self.last_w:
                deps.add(self.last_w[b])
        for b in writes:
            if b in self.last_w:
                deps.add(self.last_w[b])
            for r in self.readers.get(b, ()):
                deps.add(r)
        oid = len(self.ops)
        slot = None
        if dma:
            slot = self.dma_rr[engine] % self.NSLOT
            self.dma_rr[engine] += 1
            key = (engine, slot)
            if key in self.slot_last:
                deps.add(self.slot_last[key])
            self.slot_last[key] = oid
        deps.discard(oid)
        self.ops.append(dict(engine=engine, fn=fn, deps=deps, dma=dma, slot=slot))
        for b in writes:
            self.last_w[b] = oid
            self.readers[b] = []
        for b in reads:
            if b not in writes:
                self.readers.setdefault(b, []).append(oid)
        self.last_op[engine] = oid
        return oid

    def op(self, engine, fn, reads=(), writes=()):
        return self._add(engine, fn, tuple(reads), tuple(writes), False)

    def dma(self, engine, fn, reads=(), writes=()):
        return self._add(engine, fn, tuple(reads), tuple(writes), True)

    def barrier(self):
        pend = set(self.last_op.values()) | set(self.slot_last.values())
        for e in self.ENGS:
            self._add(e, lambda eng: eng.nop(), (), (), False, extra=pend)
        self.last_w = {}
        self.readers = {}

    def emit(self, final_wait_engine="sp"):
        nc = self.nc
        ops = self.ops

        def skip(od, o):
            return od["engine"] == "pe" and o["engine"] == "pe" and not od["dma"] and not o["dma"]

        need = [False] * len(ops)
        for o in ops:
            for d in o["deps"]:
                if not skip(ops[d], o):
                    need[d] = True
        cnt = {e: 0 for e in self.ENGS}
        slotcnt = {}
        tok = [None] * len(ops)
        for i, o in enumerate(ops):
            if o["dma"]:
                key = ("dma", o["engine"], o["slot"])
                slotcnt[key] = slotcnt.get(key, 0) + 16
                tok[i] = (key, slotcnt[key])
            elif need[i]:
                cnt[o["engine"]] += 1
                tok[i] = (("c", o["engine"]), cnt[o["engine"]])
        sem_keys = sorted({t[0] for t in tok if t is not None}, key=str)
        with contextlib.ExitStack() as st:
            sems = {}
            for k in sem_keys:
                sems[k] = st.enter_context(nc.semaphore("s_" + "_".join(str(x) for x in k)))
            per_eng = {e: [] for e in self.ENGS}
            waited = {e: {} for e in self.ENGS}
            for i, o in enumerate(ops):
                e = o["engine"]
                waits = {}
                for d in o["deps"]:
                    if skip(ops[d], o):
                        continue
                    k, v = tok[d]
                    if waited[e].get(k, 0) >= v:
                        continue
                    waits[k] = max(waits.get(k, 0), v)
                for k, v in waits.items():
                    waited[e][k] = v
                per_eng[e].append((o["fn"], list(waits.items()), tok[i]))
            fin = [(k, v) for k, v in slotcnt.items() if waited[final_wait_engine].get(k, 0) < v]
            blockfn = {"pe": "tensor", "act": "scalar", "dve": "vector", "pool": "gpsimd", "sp": "sync"}
            with nc.Block() as block:
                for e in self.ENGS:
                    lst = per_eng[e]

                    def body(eng, lst=lst, e=e):
                        for fn, waits, t in lst:
                            for k, v in waits:
                                eng.wait_ge(sems[k], v)
                            ins = fn(eng)
                            if t is not None:
                                ins.then_inc(sems[t[0]], 16 if t[0][0] == "dma" else 1)
                        if e == final_wait_engine:
                            for k, v in fin:
                                eng.wait_ge(sems[k], v)

                    getattr(block, blockfn[e])(body)
        return len(ops)


class Arena:
    def __init__(self, t, n32):
        self.t = t
        self.n32 = n32
        self.off = 0

    def mark(self):
        return self.off

    def reset(self, m):
        self.off = m

    def alloc(self, shape, dt=F32):
        n = int(np.prod(shape[1:]))
        b = 4 if dt in (F32, I32) else 2
        n32 = (n * b + 3) // 4
        assert self.off + n32 <= self.n32, ("SBUF arena overflow", self.off, n32, self.n32)
        v = self.t[0:shape[0], self.off:self.off + n32]
        self.off += n32
        if b == 2:
            v = v.bitcast(dt)
        elif dt != F32:
            v = v.bitcast(dt)
        if len(shape) > 2:
            names = "abcdefg"[:len(shape) - 1]
            kw = {names[i]: shape[i + 1] for i in range(len(shape) - 2)}
            v = v.rearrange("p (%s) -> p %s" % (" ".join(names), " ".join(names)), **kw)
        return v


def build_program(dbg=None):
    nc = bass.Bass("TRN2", target_bir_lowering=False)

    def din(name, shape, dt=F32):
        return nc.dram_tensor(name, list(shape), dt, kind="ExternalInput").ap()

    def dscr(name, shape, dt=F32):
        kind = "ExternalOutput" if (dbg and name in dbg) else "Internal"
        return nc.dram_tensor(name, list(shape), dt, kind=kind).ap()

    x_own = din("x_own", [LH, D])
    x_prev = din("x_prev", [LH, D])
    flag = din("flag", [128, 1])
    cT = din("cT", [128, 16])
    g_preT = din("g_preT", [2, 128, 16])
    g_post = din("g_post", [2, D])
    w_mod = din("w_mod", [2, D, 3 * D])
    b_modT = din("b_modT", [2, 128, 48])
    b_mod = din("b_mod", [2, 3 * D])
    w_in_ab = din("w_in_ab", [D, 7168])
    w_out_ab = din("w_out_ab", [D, D])
    sgu_norm_g = din("sgu_norm_g", [1024])
    sgu_w = din("sgu_w", [8, 128, 128])
    sgu_b = din("sgu_b", [1024])
    w_in_ssm = din("w_in_ssm", [D, D])
    w_out_ssm = din("w_out_ssm", [1024, D])
    w_glu = din("w_glu", [1024, 1024])
    b_gluT = din("b_gluT", [128, 8])
    d_skipT = din("d_skipT", [128, 8])
    lamre_sm = din("lamre_sm", [128, 32])
    lamim_sm = din("lamim_sm", [128, 32])
    logdt_sm = din("logdt_sm", [128, 32])
    bre_smz = din("bre_smz", [128, 32, 32])
    bim_smz = din("bim_smz", [128, 32, 32])
    cre_smz = din("cre_smz", [128, 32, 32])
    cim_smz = din("cim_smz", [128, 32, 32])

    out = nc.dram_tensor("out", [LH, D], F32, kind="ExternalOutput").ap()

    QT = dscr("QT", [8, 128, LH], BF16)
    KT = dscr("KT", [8, 128, 2 * LH], BF16)
    VS = dscr("VS", [2 * LH, 1024], BF16)
    OA = dscr("OA", [8, 128, LH], BF16)
    SBZ = dscr("SBZ", [8, 128, LH], BF16)
    skip_l0 = bool(dbg and dbg.get("skip_l0"))
    X1 = din("X1in", [LH, D]) if skip_l0 else dscr("X1", [LH, D], F32)
    st_in = nc.dram_tensor("st_in", [128, 64], F32)
    st_out = nc.dram_tensor("st_out", [256, 64], F32)

    NA = 52000
    with contextlib.ExitStack() as st:
        arena_t = st.enter_context(nc.sbuf_tensor("arena", [128, NA], F32))
        ps = st.enter_context(nc.psum_tensor("ps", [128, 4096], F32))
        A = Arena(arena_t, NA)
        S = Sched(nc)

        def bank(i, n=1):
            return ps[:, i * 512:(i + n) * 512]

        def bkeys(i, n=1):
            return ["ps%d" % j for j in range(i, i + n)]

        def dump(name, ap, keys, dt=F32):
            if not (dbg and dbg.get("dump")):
                return
            t = nc.dram_tensor(name, list(ap.shape), dt, kind="ExternalOutput").ap()
            S.dma("sp", lambda e: e.dma_start(out=t, in_=ap), reads=keys)

        ident = A.alloc([128, 128], BF16)
        identf = A.alloc([128, 128], F32)
        ones_bf = A.alloc([128, 128], BF16)
        negones = A.alloc([128, 128], BF16)
        negUI = A.alloc([128, 128], BF16)
        flag_sb = A.alloc([128, 1])
        S.op("pool", lambda e: e.memset(identf, 0.0), writes=["identf"])
        S.op("pool", lambda e: e.affine_select(out=identf, in_=identf, compare_op=ALU.not_equal, fill=1.0,
                                               base=0, pattern=[[-1, 128]], channel_multiplier=1),
             reads=["identf"], writes=["identf"])
        S.op("pool", lambda e: e.tensor_copy(out=ident, in_=identf), reads=["identf"], writes=["ident"])
        S.op("pool", lambda e: e.memset(ones_bf, 1.0), writes=["ones_bf"])
        S.op("pool", lambda e: e.memset(negones, -1.0), writes=["negones"])
        S.op("pool", lambda e: e.memset(negUI, -1.0), writes=["negUI"])
        S.op("pool", lambda e: e.affine_select(out=negUI, in_=negUI, compare_op=ALU.is_ge, fill=0.0,
                                               base=0, pattern=[[-1, 128]], channel_multiplier=1),
             reads=["negUI"], writes=["negUI"])
        S.dma("sp", lambda e: e.dma_start(out=flag_sb, in_=flag), writes=["flag"])

        modT = A.alloc([128, 2, 32])
        a_pre = A.alloc([128, 2, 16])
        gpost_row = A.alloc([128, 2, D])
        m_phase = A.mark()
        cT_sb = A.alloc([128, 16])
        cond_bf = A.alloc([128, 16], BF16)
        cond_rep = A.alloc([128, 16, 128], BF16)
        gpre_sb = A.alloc([128, 2, 16])
        bmodT_sb = A.alloc([128, 2, 48])
        brow = A.alloc([128, D])
        grow = A.alloc([128, D])
        wm = [A.alloc([128, 3 * D], BF16) for _ in range(2)]
        S.dma("sp", lambda e: e.dma_start(out=cT_sb, in_=cT), writes=["cT"])
        S.dma("sp", lambda e: e.dma_start(out=gpre_sb, in_=g_preT.rearrange("l p f -> p l f")), writes=["gpre"])
        S.dma("sp", lambda e: e.dma_start(out=bmodT_sb, in_=b_modT.rearrange("l p f -> p l f")), writes=["bmodT"])
        S.op("act", lambda e: e.activation(out=cond_bf, in_=cT_sb, func=AF.Silu), reads=["cT"], writes=["cond"])
        S.op("dve", lambda e: e.tensor_copy(out=cond_rep, in_=cond_bf.unsqueeze(2).broadcast_to([128, 16, 128])),
             reads=["cond"], writes=["cond_rep"])
        for l in range(2):
            S.op("dve", lambda e: e.memset(bank(0, 5), 0.0), writes=bkeys(0, 5))
            for kt in range(16):
                wb = wm[kt % 2]
                wk = "wm%d" % (kt % 2)
                S.dma("pool", lambda e, wb=wb, l=l, kt=kt: e.dma_start(out=wb, in_=w_mod[l, kt * 128:(kt + 1) * 128, :]),
                      writes=[wk])
                for ft in range(32):
                    S.op("pe", lambda e, wb=wb, ft=ft, kt=kt: e.matmul(
                        bank(0)[:, ft:ft + 1], lhsT=wb[:, ft * 128:(ft + 1) * 128], rhs=cond_bf[:, kt:kt + 1],
                        start=False, stop=False, skip_group_check=True),
                        reads=[wk, "cond"] + bkeys(0), writes=bkeys(0))
                for cc in range(4):
                    S.op("pe", lambda e, wb=wb, cc=cc, kt=kt: e.matmul(
                        bank(1 + cc), lhsT=cond_rep[:, kt, :], rhs=wb[:, 4096 + cc * 512:4096 + (cc + 1) * 512],
                        start=False, stop=False, skip_group_check=True),
                        reads=[wk, "cond_rep"] + bkeys(1 + cc), writes=bkeys(1 + cc))
            S.op("dve", lambda e, l=l: e.tensor_tensor(out=modT[:, l, :], in0=bank(0)[:, 0:32], in1=bmodT_sb[:, l, 0:32],
                                                       op=ALU.add),
                 reads=bkeys(0) + ["bmodT"], writes=["modT%d" % l])
            S.op("dve", lambda e, l=l: e.scalar_tensor_tensor(out=a_pre[:, l, :], in0=modT[:, l, 16:32], scalar=1.0,
                                                              in1=gpre_sb[:, l, :], op0=ALU.add, op1=ALU.mult),
                 reads=["modT%d" % l, "gpre"], writes=["a_pre%d" % l])
            S.dma("sp", lambda e, l=l: e.dma_start(out=brow, in_=b_mod[l, 2 * D:3 * D].partition_broadcast(128)),
                  writes=["brow"])
            S.dma("sp", lambda e, l=l: e.dma_start(out=grow, in_=g_post[l].partition_broadcast(128)), writes=["grow"])
            S.op("dve", lambda e: e.tensor_tensor(out=brow, in0=bank(1, 4), in1=brow, op=ALU.add),
                 reads=bkeys(1, 4) + ["brow"], writes=["brow"])
            S.op("dve", lambda e, l=l: e.tensor_tensor(out=gpost_row[:, l, :], in0=brow, in1=grow, op=ALU.mult),
                 reads=["brow", "grow"], writes=["gpost%d" % l])
        dump("d_modT", modT, ["modT0", "modT1"])
        dump("d_apre", a_pre, ["a_pre0", "a_pre1"])
        dump("d_gpost", gpost_row, ["gpost0", "gpost1"])
        S.barrier()
        A.reset(m_phase)

        def norm_transpose_tile(src, tok0, l, xraw, xs, hT, tagbase, junk, ssb, it):
            for s in range(4):
                xr = xraw[(it * 4 + s) % 2]
                xrk = "xraw%d" % ((it * 4 + s) % 2)
                S.dma("sp", lambda e, xr=xr, s=s: e.dma_start(out=xr, in_=src[tok0 + s * 128: tok0 + (s + 1) * 128, :]),
                      writes=[xrk])
                S.op("act", lambda e, xr=xr, s=s: e.activation(out=junk, in_=xr, func=AF.Square,
                                                               accum_out=ssb[:, s:s + 1]),
                     reads=[xrk], writes=["junk", "ss%d" % s])
                S.op("act", lambda e, s=s: e.activation(out=ssb[:, 4 + s:5 + s], in_=ssb[:, s:s + 1], func=AF.Ln,
                                                        scale=1.0 / D, bias=eps_sb),
                     reads=["ss%d" % s], writes=["ln%d" % s])
                S.op("act", lambda e, s=s: e.activation(out=ssb[:, 8 + s:9 + s], in_=ssb[:, 4 + s:5 + s], func=AF.Exp,
                                                        scale=-0.5),
                     reads=["ln%d" % s], writes=["rstd%d" % s])
                S.op("dve", lambda e, xr=xr, s=s: e.tensor_scalar(out=xs[:, s, :], in0=xr, scalar1=ssb[:, 8 + s:9 + s],
                                                                  scalar2=None, op0=ALU.mult),
                     reads=[xrk, "rstd%d" % s], writes=["xs%d" % s])
            for ft in range(16):
                b = ft % 2
                pst = bank(b).bitcast(BF16)[:, 0:512]
                for s in range(4):
                    S.op("pe", lambda e, pst=pst, s=s, ft=ft: e.transpose(out=pst[:, s * 128:(s + 1) * 128],
                                                                          in_=xs[:, s, ft * 128:(ft + 1) * 128],
                                                                          identity=ident),
                         reads=["xs%d" % s, "ident"], writes=bkeys(b))
                eng = "act" if ft % 2 == 0 else "dve"
                if eng == "act":
                    S.op("act", lambda e, pst=pst, ft=ft: e.activation(out=hT[:, ft, :], in_=pst, func=AF.Identity,
                                                                       scale=a_pre[:, l, ft:ft + 1],
                                                                       bias=modT[:, l, ft:ft + 1]),
                         reads=bkeys(b) + ["a_pre%d" % l, "modT%d" % l], writes=["%s_%d" % (tagbase, ft)])
                else:
                    S.op("dve", lambda e, pst=pst, ft=ft: e.tensor_scalar(out=hT[:, ft, :], in0=pst,
                                                                          scalar1=a_pre[:, l, ft:ft + 1],
                                                                          scalar2=modT[:, l, ft:ft + 1],
                                                                          op0=ALU.mult, op1=ALU.add),
                         reads=bkeys(b) + ["a_pre%d" % l, "modT%d" % l], writes=["%s_%d" % (tagbase, ft)])

        def out_proj_resid(l, tt, catf, nk, wo, wokey, xsrc, dst, xres, yt, ssb, junk2, it0):
            for s in range(4):
                it = it0 + s
                pb = (it % 2) * 4
                for cc in range(4):
                    for kt in range(nk):
                        cap, ck = catf(kt, s)
                        S.op("pe", lambda e, pb=pb, cc=cc, kt=kt, cap=cap: e.matmul(
                            bank(pb + cc), lhsT=cap,
                            rhs=wo[:, kt, cc * 512:(cc + 1) * 512], start=(kt == 0), stop=(kt == nk - 1)),
                            reads=[ck, wokey], writes=bkeys(pb + cc))
                xr = xres[it % 2]
                xrk = "xres%d" % (it % 2)
                r0 = tt * 512 + s * 128
                S.dma("sp", lambda e, xr=xr, r0=r0: e.dma_start(out=xr, in_=xsrc[r0:r0 + 128, :]), writes=[xrk])
                S.op("act", lambda e, pb=pb: e.activation(out=junk2, in_=bank(pb, 4), func=AF.Square,
                                                          accum_out=ssb[:, 0:1]),
                     reads=bkeys(pb, 4), writes=["junk2", "pss"])
                S.op("act", lambda e: e.activation(out=ssb[:, 1:2], in_=ssb[:, 0:1], func=AF.Ln, scale=1.0 / D,
                                                   bias=eps_sb), reads=["pss"], writes=["pln"])
                S.op("act", lambda e: e.activation(out=ssb[:, 2:3], in_=ssb[:, 1:2], func=AF.Exp, scale=-0.5),
                     reads=["pln"], writes=["prstd"])
                y_ = yt[it % 2]
                yk = "yt%d" % (it % 2)
                S.op("dve", lambda e, pb=pb, y_=y_: e.scalar_tensor_tensor(out=y_, in0=bank(pb, 4), scalar=ssb[:, 2:3],
                                                                           in1=gpost_row[:, l, :], op0=ALU.mult,
                                                                           op1=ALU.mult),
                     reads=bkeys(pb, 4) + ["prstd", "gpost%d" % l], writes=[yk])
                S.op("pool", lambda e, y_=y_, xr=xr: e.tensor_tensor(out=y_, in0=y_, in1=xr, op=ALU.add),
                     reads=[yk, xrk], writes=[yk])
                S.dma("sp", lambda e, y_=y_, r0=r0: e.dma_start(out=dst[r0:r0 + 128, :], in_=y_), reads=[yk])

        eps_sb = A.alloc([128, 1])
        S.op("pool", lambda e: e.memset(eps_sb, EPS), writes=["eps"])

        l0_mark = A.mark()
        normg_row = A.alloc([128, 1024])
        bsb = A.alloc([128, 1024])
        WcT = A.alloc([128, 8, 128], BF16)
        tri = A.alloc([128, 128])
        sguw_sb = A.alloc([128, 8, 128])
        S.dma("sp", lambda e: e.dma_start(out=normg_row, in_=sgu_norm_g.partition_broadcast(128)), writes=["normg"])
        S.dma("sp", lambda e: e.dma_start(out=bsb, in_=sgu_b.partition_broadcast(128)), writes=["bsb"])
        S.dma("sp", lambda e: e.dma_start(out=sguw_sb, in_=sgu_w.rearrange("h t s -> t h s")), writes=["sguw"])
        S.op("pool", lambda e: e.memset(tri, 1.0), writes=["tri"])
        S.op("pool", lambda e: e.affine_select(out=tri, in_=tri, compare_op=ALU.is_ge, fill=0.0, base=0,
                                               pattern=[[1, 128]], channel_multiplier=-1),
             reads=["tri"], writes=["tri"])
        for h in range(8):
            S.op("pe", lambda e, h=h: e.transpose(out=bank(h % 2)[:, 0:128], in_=sguw_sb[:, h, :], identity=identf),
                 reads=["sguw", "identf"], writes=bkeys(h % 2))
            S.op("dve", lambda e, h=h: e.tensor_tensor(out=WcT[:, h, :], in0=bank(h % 2)[:, 0:128], in1=tri, op=ALU.mult),
                 reads=bkeys(h % 2) + ["tri"], writes=["WcT"])

        xraw = [A.alloc([128, D]) for _ in range(2)]
        junk = A.alloc([128, D], BF16)
        ssb = A.alloc([128, 12])
        xs = A.alloc([128, 4, D], BF16)
        hT = [A.alloc([128, 16, 512], BF16) for _ in range(2)]
        Wc = [A.alloc([128, 16, 512], BF16) for _ in range(2)]
        guT = A.alloc([128, 8, 512], BF16)
        szT = A.alloc([128, 8, 512], BF16)
        vn = A.alloc([128, 4, 1024], BF16)
        oaT = A.alloc([128, 8, 512], BF16)
        stg = [A.alloc([128, 512], BF16) for _ in range(4)]
        gv = A.alloc([128, 512])
        sq = A.alloc([128, 512])
        ssq = A.alloc([128, 12])
        t1 = A.alloc([128, 1024])
        t2 = A.alloc([128, 1024])

        wcount = [0]
        stgc = [0]
        pcount = [0]
        SCALE_Q = 1.0 / math.sqrt(128.0)

        def load_w(col0):
            i = wcount[0] % 2
            wcount[0] += 1
            S.dma("pool", lambda e, i=i, col0=col0: e.dma_start(
                out=Wc[i], in_=w_in_ab[:, col0:col0 + 512].rearrange("(kt k) c -> k kt c", k=128)),
                writes=["Wc%d" % i])
            return Wc[i], "Wc%d" % i

        def next_bank():
            b = 2 + pcount[0] % 4
            pcount[0] += 1
            return b

        def next_stg():
            i = stgc[0] % 4
            stgc[0] += 1
            return stg[i], "stg%d" % i

        def do_tile_A(tt):
            own = tt >= 4
            src = x_own if own else x_prev
            tok0 = (tt % 4) * 512
            ctx0 = tt * 512
            hb = hT[tt % 2]
            hk = "hT%d" % (tt % 2)
            hkeys = ["%s_%d" % (hk, ft) for ft in range(16)]
            norm_transpose_tile(src, tok0, 0, xraw, xs, hb, hk, junk, ssb, tt)
            if tt == 4:
                dump("d_hT", hb, hkeys, BF16)
                dump("d_ssb", ssb, ["rstd3"])
                dump("d_xs", xs, ["xs0", "xs1", "xs2", "xs3"], BF16)
            segs = []
            if own:
                segs += [("av", 1024, 0), ("av", 1536, 1), ("au", 0, 0), ("au", 512, 1), ("az", 2048, 0), ("az", 2560, 1),
                         ("q", 3072, 0), ("q", 3584, 1)]
            segs += [("k", 4096, 0), ("k", 4608, 1), ("v", 5120, 0), ("v", 5632, 1)]
            if own:
                segs += [("bz", 6144, 0), ("bz", 6656, 1)]
            for kind, col0, cc in segs:
                W_, wk = load_w(col0)
                if kind in ("av", "v"):
                    for s in range(4):
                        b = next_bank()
                        for kt in range(16):
                            S.op("pe", lambda e, b=b, kt=kt, s=s, W_=W_: e.matmul(
                                bank(b), lhsT=hb[:, kt, s * 128:(s + 1) * 128], rhs=W_[:, kt, :],
                                start=(kt == 0), stop=(kt == 15)),
                                reads=[hkeys[kt], wk], writes=bkeys(b))
                        if kind == "v":
                            sg, sk = next_stg()
                            if own:
                                S.op("dve", lambda e, b=b, sg=sg: e.tensor_copy(out=sg, in_=bank(b)),
                                     reads=bkeys(b), writes=[sk])
                            else:
                                S.op("dve", lambda e, b=b, sg=sg: e.tensor_scalar(out=sg, in0=bank(b), scalar1=flag_sb,
                                                                                  scalar2=None, op0=ALU.mult),
                                     reads=bkeys(b) + ["flag"], writes=[sk])
                            r0 = ctx0 + s * 128
                            S.dma("sp", lambda e, sg=sg, r0=r0, cc=cc: e.dma_start(
                                out=VS[r0:r0 + 128, cc * 512:(cc + 1) * 512], in_=sg), reads=[sk])
                        else:
                            S.op("act", lambda e, b=b: e.activation(out=gv, in_=bank(b), func=AF.Gelu_apprx_tanh),
                                 reads=bkeys(b), writes=["gv"])
                            S.op("dve", lambda e: e.tensor_tensor(out=sq, in0=gv, in1=gv, op=ALU.mult),
                                 reads=["gv"], writes=["sq"])
                            S.op("dve", lambda e: e.tensor_reduce(out=ssq[:, 0:4], in_=sq.rearrange("p (h d) -> p h d", h=4),
                                                                  axis=AX.X, op=ALU.add),
                                 reads=["sq"], writes=["ssq"])
                            S.op("act", lambda e: e.activation(out=ssq[:, 4:8], in_=ssq[:, 0:4], func=AF.Ln,
                                                               scale=1.0 / 128, bias=eps_sb),
                                 reads=["ssq"], writes=["ssqln"])
                            S.op("act", lambda e: e.activation(out=ssq[:, 8:12], in_=ssq[:, 4:8], func=AF.Exp, scale=-0.5),
                                 reads=["ssqln"], writes=["ssqr"])
                            S.op("dve", lambda e: e.tensor_tensor(
                                out=sq.rearrange("p (h d) -> p h d", h=4), in0=gv.rearrange("p (h d) -> p h d", h=4),
                                in1=ssq[:, 8:12].unsqueeze(2).broadcast_to([128, 4, 128]), op=ALU.mult),
                                reads=["gv", "ssqr"], writes=["sq"])
                            S.op("dve", lambda e, s=s, cc=cc: e.tensor_tensor(
                                out=vn[:, s, cc * 512:(cc + 1) * 512], in0=sq, in1=normg_row[:, cc * 512:(cc + 1) * 512],
                                op=ALU.mult), reads=["sq", "normg"], writes=["vn%d" % s])
                else:
                    for ct in range(4):
                        b = next_bank()
                        h = cc * 4 + ct
                        for kt in range(16):
                            S.op("pe", lambda e, b=b, kt=kt, ct=ct, W_=W_: e.matmul(
                                bank(b), lhsT=W_[:, kt, ct * 128:(ct + 1) * 128], rhs=hb[:, kt, :],
                                start=(kt == 0), stop=(kt == 15)),
                                reads=[hkeys[kt], wk], writes=bkeys(b))
                        if kind == "au":
                            S.op("act", lambda e, b=b, h=h: e.activation(out=guT[:, h, :], in_=bank(b),
                                                                         func=AF.Gelu_apprx_tanh),
                                 reads=bkeys(b), writes=["guT"])
                        elif kind == "az":
                            S.op("act", lambda e, b=b, h=h: e.activation(out=szT[:, h, :], in_=bank(b), func=AF.Silu),
                                 reads=bkeys(b), writes=["szT"])
                        elif kind == "q":
                            sg, sk = next_stg()
                            S.op("act", lambda e, b=b, sg=sg: e.activation(out=sg, in_=bank(b), func=AF.Copy,
                                                                           scale=SCALE_Q),
                                 reads=bkeys(b), writes=[sk])
                            S.dma("sp", lambda e, sg=sg, h=h, tok0=tok0: e.dma_start(out=QT[h, :, tok0:tok0 + 512], in_=sg),
                                  reads=[sk])
                        elif kind == "k":
                            sg, sk = next_stg()
                            S.op("dve", lambda e, b=b, sg=sg: e.tensor_copy(out=sg, in_=bank(b)),
                                 reads=bkeys(b), writes=[sk])
                            S.dma("sp", lambda e, sg=sg, h=h, ctx0=ctx0: e.dma_start(out=KT[h, :, ctx0:ctx0 + 512], in_=sg),
                                  reads=[sk])
                        elif kind == "bz":
                            sg, sk = next_stg()
                            S.op("act", lambda e, b=b, sg=sg: e.activation(out=sg, in_=bank(b), func=AF.Silu),
                                 reads=bkeys(b), writes=[sk])
                            S.dma("sp", lambda e, sg=sg, h=h, tok0=tok0: e.dma_start(out=SBZ[h, :, tok0:tok0 + 512], in_=sg),
                                  reads=[sk])
                if own and kind == "az" and cc == 1:
                    for s in range(4):
                        for h in range(8):
                            b6 = 6 + h // 4
                            S.op("pe", lambda e, b6=b6, h=h, s=s: e.matmul(
                                bank(b6)[:, (h % 4) * 128:(h % 4 + 1) * 128], lhsT=vn[:, s, h * 128:(h + 1) * 128],
                                rhs=WcT[:, h, :], start=True, stop=True),
                                reads=["vn%d" % s, "WcT"], writes=bkeys(b6))
                        S.op("dve", lambda e: e.tensor_tensor(out=t1, in0=bank(6, 2), in1=bsb, op=ALU.add),
                             reads=bkeys(6, 2) + ["bsb"], writes=["t1"])
                        S.op("pool", lambda e, s=s: e.tensor_tensor(
                            out=t2.rearrange("p (h t) -> p h t", h=8), in0=t1.rearrange("p (h t) -> p h t", h=8),
                            in1=guT[:, :, s * 128:(s + 1) * 128], op=ALU.mult),
                            reads=["t1", "guT"], writes=["t2"])
                        S.op("pool", lambda e, s=s: e.tensor_tensor(
                            out=oaT[:, :, s * 128:(s + 1) * 128], in0=t2.rearrange("p (h t) -> p h t", h=8),
                            in1=szT[:, :, s * 128:(s + 1) * 128], op=ALU.mult),
                            reads=["t2", "szT"], writes=["oaT"])
                    S.dma("sp", lambda e, tok0=tok0: e.dma_start(
                        out=OA[:, :, tok0:tok0 + 512].rearrange("h d t -> d h t"), in_=oaT), reads=["oaT"])

        for tt in range(0 if skip_l0 else 8):
            do_tile_A(tt)
        S.barrier()
        A.reset(l0_mark)

        if dbg and dbg.get("stop") == "A":
            S.emit()
            return nc

        obT = A.alloc([128, 8, LH], BF16)
        wo_sb = A.alloc([128, 16, D], BF16)
        b_mark = A.mark()
        maskd = A.alloc([128, 4, 512])
        negm = A.alloc([128, 4, 512])
        KTh = [A.alloc([128, 2 * LH], BF16) for _ in range(2)]
        QTh = [A.alloc([128, LH], BF16) for _ in range(2)]
        Vh = [A.alloc([128, 32, 128], BF16) for _ in range(2)]
        SZh = [A.alloc([128, LH], BF16) for _ in range(2)]
        esb = [A.alloc([128, 512]) for _ in range(2)]
        spb = [A.alloc([128, 512], BF16) for _ in range(2)]
        spf = A.alloc([128, 512])
        pre = [A.alloc([128, 512]) for _ in range(2)]
        wsb = [A.alloc([128, 512], BF16) for _ in range(2)]
        Rb = A.alloc([128, 512])
        for j in range(4):
            S.op("pool", lambda e, j=j: e.memset(maskd[:, j, :], 1.0), writes=["maskd"])
            S.op("pool", lambda e, j=j: e.affine_select(out=maskd[:, j, :], in_=maskd[:, j, :], compare_op=ALU.is_gt,
                                                        fill=0.0, base=-128 * j, pattern=[[1, 512]],
                                                        channel_multiplier=-1),
                 reads=["maskd"], writes=["maskd"])
        S.op("pool", lambda e: e.tensor_scalar(out=negm, in0=maskd, scalar1=-1.0, scalar2=30000.0, op0=ALU.add,
                                               op1=ALU.mult), reads=["maskd"], writes=["negm"])
        for kt in range(16):
            S.dma("pool", lambda e, kt=kt: e.dma_start(out=wo_sb[:, kt, :], in_=w_out_ab[kt * 128:(kt + 1) * 128, :]),
                  writes=["wo"])
        nheads = dbg.get("heads", 8) if dbg else 8
        if skip_l0:
            nheads = 0
        tiles = []
        for h in range(nheads):
            for Qg in range(4):
                nkb = 16 + 4 * (Qg + 1)
                for idx, kb in enumerate(range(nkb - 1, -1, -1)):
                    tiles.append((h, Qg, idx, kb, nkb))
        SKEW = 2

        def head_bufs(h):
            hb2 = h % 2
            return (KTh[hb2], QTh[hb2], Vh[hb2], SZh[hb2],
                    "KTh%d" % hb2, "QTh%d" % hb2, "Vh%d" % hb2, "SZh%d" % hb2)

        def stage_A(i):
            h, Qg, idx, kb, nkb = tiles[i]
            kth, qth, vh, szh, kk, qk, vk, zk = head_bufs(h)
            if Qg == 0 and idx == 0:
                S.dma("sp", lambda e: e.dma_start(out=kth, in_=KT[h]), writes=[kk])
                S.dma("sp", lambda e: e.dma_start(out=qth, in_=QT[h]), writes=[qk])
                S.dma("sp", lambda e: e.dma_start(out=vh, in_=VS[:, h * 128:(h + 1) * 128].rearrange(
                    "(kb t) d -> t kb d", t=128)), writes=[vk])
                S.dma("sp", lambda e: e.dma_start(out=szh, in_=SBZ[h]), writes=[zk])
            qs = slice(Qg * 512, (Qg + 1) * 512)
            zb = i % 3
            cb = 3 + i % 3
            j = kb - (16 + 4 * Qg)
            e_, ek = esb[i % 2], "esb%d" % (i % 2)
            sp_, spk = spb[i % 2], "spb%d" % (i % 2)
            S.op("pe", lambda e: e.matmul(bank(zb), lhsT=kth[:, kb * 128:(kb + 1) * 128], rhs=qth[:, qs],
                                          start=True, stop=True), reads=[kk, qk], writes=bkeys(zb))
            S.op("act", lambda e: e.activation(out=e_, in_=bank(zb), func=AF.Exp), reads=bkeys(zb), writes=[ek])
            if j < 0:
                S.op("act", lambda e: e.activation(out=sp_, in_=e_, func=AF.Ln, bias=1.0), reads=[ek], writes=[spk])
            else:
                S.op("act", lambda e: e.activation(out=spf, in_=e_, func=AF.Ln, bias=1.0), reads=[ek], writes=["spf"])
                S.op("dve", lambda e: e.tensor_tensor(out=sp_, in0=spf, in1=maskd[:, j, :], op=ALU.mult),
                     reads=["spf", "maskd"], writes=[spk])
            S.op("pe", lambda e: e.matmul(bank(zb), lhsT=negUI, rhs=sp_, start=False, stop=True, skip_group_check=True),
                 reads=[spk, "negUI"] + bkeys(zb), writes=bkeys(zb))
            S.op("pe", lambda e: e.matmul(bank(cb), lhsT=negones, rhs=sp_, start=True, stop=True),
                 reads=[spk, "negones"], writes=bkeys(cb))

        def stage_B(i):
            h, Qg, idx, kb, nkb = tiles[i]
            kth, qth, vh, szh, kk, qk, vk, zk = head_bufs(h)
            qs = slice(Qg * 512, (Qg + 1) * 512)
            zb = i % 3
            cb = 3 + i % 3
            po = 6 + (h * 4 + Qg) % 2
            j = kb - (16 + 4 * Qg)
            pr_, prk = pre[i % 2], "pre%d" % (i % 2)
            w_, wk = wsb[i % 2], "wsb%d" % (i % 2)
            if idx == 0:
                S.op("dve", lambda e: e.memset(Rb, 0.0), writes=["R"])
            S.op("dve", lambda e: e.tensor_tensor(out=pr_, in0=bank(zb), in1=Rb, op=ALU.add),
                 reads=bkeys(zb) + ["R"], writes=[prk])
            if j >= 0:
                S.op("pool", lambda e: e.tensor_tensor(out=pr_, in0=pr_, in1=negm[:, j, :], op=ALU.add),
                     reads=[prk, "negm"], writes=[prk])
            S.op("act", lambda e: e.activation(out=w_, in_=pr_, func=AF.Exp), reads=[prk], writes=[wk])
            S.op("pe", lambda e: e.matmul(bank(po), lhsT=vh[:, kb, :], rhs=w_, start=(idx == 0), stop=(idx == nkb - 1)),
                 reads=[wk, vk], writes=bkeys(po))
            S.op("dve", lambda e: e.tensor_tensor(out=Rb, in0=Rb, in1=bank(cb), op=ALU.add),
                 reads=bkeys(cb) + ["R"], writes=["R"])
            if idx == nkb - 1:
                S.op("dve", lambda e: e.tensor_tensor(out=obT[:, h, qs], in0=bank(po), in1=szh[:, qs], op=ALU.mult),
                     reads=bkeys(po) + [zk], writes=["obT%d" % Qg])

        for i in range(len(tiles) + SKEW):
            if i < len(tiles):
                stage_A(i)
            if i - SKEW >= 0:
                stage_B(i - SKEW)
        dump("d_obT", obT, ["obT0", "obT1", "obT2", "obT3"], BF16)
        S.barrier()
        A.reset(b_mark)
        if dbg and dbg.get("stop") == "B":
            S.emit()
            return nc

        oa_sb = [A.alloc([128, 8, 512], BF16) for _ in range(2)]
        xres = [A.alloc([128, D]) for _ in range(2)]
        yt = [A.alloc([128, D]) for _ in range(2)]
        junk2 = A.alloc([128, D], BF16)
        ssc = A.alloc([128, 4])

        def out_tile_C(tt):
            ob = oa_sb[tt % 2]
            ok = "oa_sb%d" % (tt % 2)
            S.dma("sp", lambda e: e.dma_start(out=ob, in_=OA[:, :, tt * 512:(tt + 1) * 512].rearrange("h d t -> d h t")),
                  writes=[ok])

            def catf(kt, s):
                if kt < 8:
                    return ob[:, kt, s * 128:(s + 1) * 128], ok
                return obT[:, kt - 8, tt * 512 + s * 128: tt * 512 + (s + 1) * 128], "obT%d" % tt
            out_proj_resid(0, tt, catf, 16, wo_sb, "wo", x_own, X1, xres, yt, ssc, junk2, tt * 4)

        for tt in range(0 if skip_l0 else 4):
            out_tile_C(tt)
        S.barrier()
        A.reset(l0_mark)
        if dbg and dbg.get("stop") == "C":
            S.emit()
            return nc
        UT = dscr("UT", [8, 128, LH], BF16)
        SZ1 = dscr("SZ1", [8, 128, LH], BF16)
        GT = dscr("GT", [8, 128, LH], BF16)
        l1_mark = A.mark()
        xraw = [A.alloc([128, D]) for _ in range(2)]
        junk = A.alloc([128, D], BF16)
        ssb = A.alloc([128, 12])
        xs = A.alloc([128, 4, D], BF16)
        hT1 = [A.alloc([128, 16, 512], BF16) for _ in range(2)]
        Wd = [A.alloc([128, 16, 512], BF16) for _ in range(2)]
        stg1 = [A.alloc([128, 512], BF16) for _ in range(4)]
        cntD = [0, 0, 0]

        def do_tile_D(tt):
            hb = hT1[tt % 2]
            hk = "hT1%d" % (tt % 2)
            hkeys = ["%s_%d" % (hk, ft) for ft in range(16)]
            norm_transpose_tile(X1, tt * 512, 1, xraw, xs, hb, hk, junk, ssb, tt)
            for cc in range(4):
                i = cntD[0] % 2
                cntD[0] += 1
                W_, wk = Wd[i], "Wd%d" % i
                S.dma("pool", lambda e, W_=W_, cc=cc: e.dma_start(
                    out=W_, in_=w_in_ssm[:, cc * 512:(cc + 1) * 512].rearrange("(kt k) c -> k kt c", k=128)),
                    writes=[wk])
                for ct in range(4):
                    b = 2 + cntD[1] % 4
                    cntD[1] += 1
                    f = (cc % 2) * 4 + ct
                    for kt in range(16):
                        S.op("pe", lambda e, b=b, kt=kt, ct=ct, W_=W_: e.matmul(
                            bank(b), lhsT=W_[:, kt, ct * 128:(ct + 1) * 128], rhs=hb[:, kt, :],
                            start=(kt == 0), stop=(kt == 15)), reads=[hkeys[kt], wk], writes=bkeys(b))
                    si = cntD[2] % 4
                    cntD[2] += 1
                    sg, sk = stg1[si], "stg1%d" % si
                    if cc < 2:
                        S.op("dve", lambda e, b=b, sg=sg: e.tensor_copy(out=sg, in_=bank(b)), reads=bkeys(b), writes=[sk])
                        S.dma("sp", lambda e, sg=sg, f=f: e.dma_start(out=UT[f, :, tt * 512:(tt + 1) * 512], in_=sg),
                              reads=[sk])
                    else:
                        S.op("act", lambda e, b=b, sg=sg: e.activation(out=sg, in_=bank(b), func=AF.Silu),
                             reads=bkeys(b), writes=[sk])
                        S.dma("sp", lambda e, sg=sg, f=f: e.dma_start(out=SZ1[f, :, tt * 512:(tt + 1) * 512], in_=sg),
                              reads=[sk])

        for tt in range(4):
            do_tile_D(tt)
        S.barrier()
        A.reset(l1_mark)
        if dbg and dbg.get("stop") == "D":
            S.emit()
            return nc

        Kblk = A.alloc([128, 8, 16, 128], BF16)
        Csm_re = A.alloc([128, 32, 32])
        Csm_nim = A.alloc([128, 32, 32])
        pw_re = A.alloc([128, 17, 32])
        pw_im = A.alloc([128, 17, 32])
        Ak_re = A.alloc([128, 7, 32])
        Ak_im = A.alloc([128, 7, 32])
        Sst_re = A.alloc([128, 32, 128], BF16)
        Sst_im = A.alloc([128, 32, 128], BF16)
        dsk = A.alloc([128, 8])
        p3_mark = A.mark()
        Bb_re = A.alloc([128, 32, 32])
        Bb_im = A.alloc([128, 32, 32])
        S_re = A.alloc([128, 32, 132])
        S_im = A.alloc([128, 32, 132])
        p2_mark = A.mark()
        prm = A.alloc([128, 16, 32])
        kint = A.alloc([128, 32], I32)
        Braw_re = A.alloc([128, 32, 32])
        Braw_im = A.alloc([128, 32, 32])
        tA = A.alloc([128, 32, 32])
        tB = A.alloc([128, 32, 32])
        BendT = A.alloc([128, 4, 16, 2, 128], BF16)
        uTf = [A.alloc([128, LH], BF16) for _ in range(2)]
        uz = [A.alloc([128, LH], BF16) for _ in range(4)]
        bmask = A.alloc([128, 128])
        ktmp = A.alloc([128, 128])
        m01 = A.alloc([128, 2])
        S.op("pool", lambda e: e.memset(bmask, 1.0), writes=["bmask"])
        for j in range(4):
            S.op("pool", lambda e, j=j: e.affine_select(out=bmask[:, 32 * j:32 * j + 32], in_=bmask[:, 32 * j:32 * j + 32],
                                                        compare_op=ALU.is_ge, fill=0.0, base=-32 * j, pattern=[[0, 32]],
                                                        channel_multiplier=1), reads=["bmask"], writes=["bmask"])
            S.op("pool", lambda e, j=j: e.affine_select(out=bmask[:, 32 * j:32 * j + 32], in_=bmask[:, 32 * j:32 * j + 32],
                                                        compare_op=ALU.is_ge, fill=0.0, base=32 * j + 31, pattern=[[0, 32]],
                                                        channel_multiplier=-1), reads=["bmask"], writes=["bmask"])
        S.op("pool", lambda e: e.tensor_tensor(out=m01[:, 0:1], in0=bmask[:, 0:1], in1=bmask[:, 64:65], op=ALU.add),
             reads=["bmask"], writes=["m01"])
        S.op("pool", lambda e: e.tensor_tensor(out=m01[:, 1:2], in0=bmask[:, 32:33], in1=bmask[:, 96:97], op=ALU.add),
             reads=["bmask"], writes=["m01"])

        TWO_PI = 2.0 * math.pi
        LR, LI, DT, MAG, ANG, U, FR, SN, CS, DEN, NRm, T0, T1, CR, CI, T2 = [prm[:, i, :] for i in range(16)]
        S.dma("sp", lambda e: e.dma_start(out=LR, in_=lamre_sm), writes=["prm"])
        S.dma("sp", lambda e: e.dma_start(out=LI, in_=lamim_sm), writes=["prm"])
        S.dma("sp", lambda e: e.dma_start(out=DT, in_=logdt_sm), writes=["prm"])
        S.dma("sp", lambda e: e.dma_start(out=Braw_re, in_=bre_smz), writes=["Braw"])
        S.dma("sp", lambda e: e.dma_start(out=Braw_im, in_=bim_smz), writes=["Braw"])
        S.dma("sp", lambda e: e.dma_start(out=Csm_re, in_=cre_smz), writes=["Csm"])
        S.dma("sp", lambda e: e.dma_start(out=Csm_nim, in_=cim_smz), writes=["Csm"])
        S.dma("sp", lambda e: e.dma_start(out=dsk, in_=d_skipT), writes=["dsk"])

        def P(eng, fn):
            S.op(eng, fn, reads=["prm", "Braw", "Csm"], writes=["prm"])

        P("act", lambda e: e.activation(out=DT, in_=DT, func=AF.Exp))
        P("dve", lambda e: e.tensor_tensor(out=MAG, in0=LR, in1=DT, op=ALU.mult))
        P("act", lambda e: e.activation(out=MAG, in_=MAG, func=AF.Exp))
        P("dve", lambda e: e.tensor_tensor(out=ANG, in0=LI, in1=DT, op=ALU.mult))
        P("dve", lambda e: e.tensor_scalar(out=U, in0=ANG, scalar1=1.0 / TWO_PI, scalar2=None, op0=ALU.mult))
        P("dve", lambda e: e.tensor_copy(out=kint, in_=U))
        P("dve", lambda e: e.tensor_copy(out=FR, in_=kint))
        P("dve", lambda e: e.tensor_tensor(out=FR, in0=U, in1=FR, op=ALU.subtract))
        P("act", lambda e: e.activation(out=SN, in_=FR, func=AF.Sin, scale=TWO_PI))
        P("dve", lambda e: e.tensor_scalar(out=U, in0=U, scalar1=0.25, scalar2=None, op0=ALU.add))
        P("dve", lambda e: e.tensor_copy(out=kint, in_=U))
        P("dve", lambda e: e.tensor_copy(out=FR, in_=kint))
        P("dve", lambda e: e.tensor_tensor(out=FR, in0=U, in1=FR, op=ALU.subtract))
        P("act", lambda e: e.activation(out=CS, in_=FR, func=AF.Sin, scale=TWO_PI))
        P("dve", lambda e: e.memset(pw_re[:, 0, :], 1.0))
        P("dve", lambda e: e.memset(pw_im[:, 0, :], 0.0))
        P("dve", lambda e: e.tensor_tensor(out=pw_re[:, 1, :], in0=MAG, in1=CS, op=ALU.mult))
        P("dve", lambda e: e.tensor_tensor(out=pw_im[:, 1, :], in0=MAG, in1=SN, op=ALU.mult))
        P("dve", lambda e: e.tensor_tensor(out=DEN, in0=LR, in1=LR, op=ALU.mult))
        P("dve", lambda e: e.tensor_tensor(out=T0, in0=LI, in1=LI, op=ALU.mult))
        P("dve", lambda e: e.tensor_tensor(out=DEN, in0=DEN, in1=T0, op=ALU.add))
        P("dve", lambda e: e.reciprocal(out=DEN, in_=DEN))
        P("dve", lambda e: e.tensor_scalar(out=NRm, in0=pw_re[:, 1, :], scalar1=-1.0, scalar2=None, op0=ALU.add))
        P("dve", lambda e: e.tensor_tensor(out=T0, in0=NRm, in1=LR, op=ALU.mult))
        P("dve", lambda e: e.tensor_tensor(out=T1, in0=pw_im[:, 1, :], in1=LI, op=ALU.mult))
        P("dve", lambda e: e.tensor_tensor(out=T0, in0=T0, in1=T1, op=ALU.add))
        P("dve", lambda e: e.tensor_tensor(out=CR, in0=T0, in1=DEN, op=ALU.mult))
        P("dve", lambda e: e.tensor_tensor(out=T0, in0=pw_im[:, 1, :], in1=LR, op=ALU.mult))
        P("dve", lambda e: e.tensor_tensor(out=T1, in0=NRm, in1=LI, op=ALU.mult))
        P("dve", lambda e: e.tensor_tensor(out=T0, in0=T0, in1=T1, op=ALU.subtract))
        P("dve", lambda e: e.tensor_tensor(out=CI, in0=T0, in1=DEN, op=ALU.mult))

        def bc(ap2, n):
            return ap2.unsqueeze(2).broadcast_to([128, 32, n])

        def cmul3(eng, o_re, o_im, a_re, a_im, b_re, b_im, ta, tb, rk, wk, conj=False):
            op_re = ALU.add if conj else ALU.subtract
            S.op(eng, lambda e: e.tensor_tensor(out=ta, in0=b_re, in1=a_re, op=ALU.mult), reads=rk, writes=["cm_ta"])
            S.op(eng, lambda e: e.tensor_tensor(out=tb, in0=b_im, in1=a_im, op=ALU.mult), reads=rk, writes=["cm_tb"])
            S.op(eng, lambda e: e.tensor_tensor(out=o_re, in0=ta, in1=tb, op=op_re),
                 reads=["cm_ta", "cm_tb"], writes=wk)
            op_im = ALU.subtract if conj else ALU.add
            S.op(eng, lambda e: e.tensor_tensor(out=ta, in0=b_im, in1=a_re, op=ALU.mult), reads=rk, writes=["cm_ta"])
            S.op(eng, lambda e: e.tensor_tensor(out=tb, in0=b_re, in1=a_im, op=ALU.mult), reads=rk, writes=["cm_tb"])
            S.op(eng, lambda e: e.tensor_tensor(out=o_im, in0=ta, in1=tb, op=op_im),
                 reads=["cm_ta", "cm_tb"], writes=wk)

        cmul3("dve", Bb_re, Bb_im, bc(CR, 32), bc(CI, 32), Braw_re, Braw_im, tA, tB, ["prm", "Braw"], ["Bb"])
        S.op("pool", lambda e: e.tensor_scalar(out=Csm_nim, in0=Csm_nim, scalar1=-1.0, scalar2=None, op0=ALU.mult),
             reads=["Csm"], writes=["Csm"])
        for k in range(1, 16):
            cmul3("dve", pw_re[:, k + 1, :], pw_im[:, k + 1, :], pw_re[:, 1, :], pw_im[:, 1, :],
                  pw_re[:, k, :], pw_im[:, k, :], tA[:, 0, :], tB[:, 0, :], ["prm", "pw"], ["pw"])
        S.op("dve", lambda e: e.tensor_copy(out=Ak_re[:, 0, :], in_=pw_re[:, 16, :]), reads=["pw"], writes=["Ak"])
        S.op("dve", lambda e: e.tensor_copy(out=Ak_im[:, 0, :], in_=pw_im[:, 16, :]), reads=["pw"], writes=["Ak"])
        for k in range(6):
            cmul3("dve", Ak_re[:, k + 1, :], Ak_im[:, k + 1, :], Ak_re[:, k, :], Ak_im[:, k, :],
                  Ak_re[:, k, :], Ak_im[:, k, :], tA[:, 0, :], tB[:, 0, :], ["Ak"], ["Ak"])
        S.op("dve", lambda e: e.memset(S_re[:, :, 0:4], 0.0), writes=["S_re"])
        S.op("dve", lambda e: e.memset(S_im[:, :, 0:4], 0.0), writes=["S_im"])

        if dbg and dbg.get("stop") == "S1a":
            dump("d_pw_re", pw_re, ["pw"])
            dump("d_pw_im", pw_im, ["pw"])
            S.emit()
            return nc
        bsm_re = Braw_re
        bsm_im = Braw_im
        ucount = [0]
        for half in range(2):
            for tau in range(16):
                k = 15 - tau
                cmul3("dve", bsm_re, bsm_im, bc(pw_re[:, k, :], 32), bc(pw_im[:, k, :], 32), Bb_re, Bb_im, tA, tB,
                      ["pw", "Bb", "Braw"], ["bsm"])
                for fl in range(4):
                    f = half * 4 + fl
                    tb0 = 4 * (fl % 2)
                    kb1 = 1 + 4 * (fl % 2)
                    for ri, src in ((0, bsm_re), (1, bsm_im)):
                        S.op("pe", lambda e, src=src, f=f, ri=ri, tb0=tb0: e.transpose(
                            out=bank(tb0)[:, ri * 128:(ri + 1) * 128],
                            in_=src[:, 4 * f:4 * f + 4, :].rearrange("p a b -> p (a b)"), identity=identf),
                            reads=["bsm", "identf"], writes=bkeys(tb0))
                    S.op("act", lambda e, fl=fl, tau=tau, tb0=tb0: e.activation(
                        out=BendT[:, fl, tau, :, :].rearrange("p a b -> p (a b)"), in_=bank(tb0)[:, 0:256], func=AF.Copy),
                        reads=bkeys(tb0), writes=["BendT"])
                    for j in range(4):
                        gp = 4 * f + j
                        S.op("pe", lambda e, gp=gp, j=j, kb1=kb1, f=f: e.matmul(
                            bank(kb1)[:, 32 * j:32 * j + 32],
                            lhsT=bsm_re[:, 4 * f:4 * f + 4, :].rearrange("p a b -> p (a b)"), rhs=Csm_re[:, gp, :],
                            start=True, stop=False), reads=["bsm", "Csm"], writes=bkeys(kb1))
                        S.op("pe", lambda e, gp=gp, j=j, kb1=kb1, f=f: e.matmul(
                            bank(kb1)[:, 32 * j:32 * j + 32],
                            lhsT=bsm_im[:, 4 * f:4 * f + 4, :].rearrange("p a b -> p (a b)"), rhs=Csm_nim[:, gp, :],
                            start=False, stop=True), reads=["bsm", "Csm"], writes=bkeys(kb1))
                    if k == 0:
                        S.op("dve", lambda e, kb1=kb1: e.tensor_tensor(out=ktmp, in0=bank(kb1)[:, 0:128], in1=bmask,
                                                                       op=ALU.mult),
                             reads=bkeys(kb1) + ["bmask"], writes=["ktmp"])
                        S.op("dve", lambda e, f=f: e.scalar_tensor_tensor(out=Kblk[:, f, 0, :], in0=identf,
                                                                          scalar=dsk[:, f:f + 1], in1=ktmp,
                                                                          op0=ALU.mult, op1=ALU.add),
                             reads=["ktmp", "dsk", "identf"], writes=["Kblk"])
                    else:
                        S.op("dve", lambda e, f=f, k=k, kb1=kb1: e.tensor_tensor(out=Kblk[:, f, k, :],
                                                                                 in0=bank(kb1)[:, 0:128], in1=bmask,
                                                                                 op=ALU.mult),
                             reads=bkeys(kb1) + ["bmask"], writes=["Kblk"])
            if dbg and dbg.get("stop") == "S1b":
                dump("d_Kblk", Kblk, ["Kblk"], BF16)
                S.emit()
                return nc
            for fl in range(4):
                f = half * 4 + fl
                ui = ucount[0] % 2
                ucount[0] += 1
                u_, uk = uTf[ui], "uTf%d" % ui
                S.dma("sp", lambda e, u_=u_, f=f: e.dma_start(out=u_, in_=UT[f]), writes=[uk])
                uv = u_.rearrange("p (n t) -> p t n", t=16)
                uzv = [uz[i].rearrange("p (t n) -> p t n", t=16) for i in range(4)]
                for i in range(4):
                    S.op("dve", lambda e, uv=uv, uzv=uzv, i=i: e.tensor_scalar(out=uzv[i], in0=uv,
                                                                              scalar1=bmask[:, 32 * i:32 * i + 1],
                                                                              scalar2=None, op0=ALU.mult),
                         reads=[uk, "bmask"], writes=["uz%d" % i])
                bmode = dbg.get("bmode", 9) if dbg else 9
                for j in range(4 if bmode >= 1 else 0):
                    gp = 4 * f + j
                    pb = 2 + (gp % 2)
                    w0 = 64 * (j // 2)
                    jj = j % 2
                    for ri in range(2):
                        for tau in range(16):
                            S.op("pe", lambda e, pb=pb, ri=ri, tau=tau, fl=fl, j=j, uzv=uzv: e.matmul(
                                bank(pb)[:, ri * 128:(ri + 1) * 128], lhsT=BendT[:, fl, tau, ri, :],
                                rhs=uzv[j][:, tau, :], start=(tau == 0), stop=(tau == 15)),
                                reads=["BendT", "uz%d" % j], writes=bkeys(pb))
                    if bmode < 2:
                        continue
                    S.op("dve", lambda e, pb=pb, gp=gp: e.tensor_copy(out=S_re[:, gp, 4:132], in_=bank(pb)[:, 0:128]),
                         reads=bkeys(pb), writes=["S_re"])
                    S.op("dve", lambda e, pb=pb, gp=gp: e.tensor_copy(out=S_im[:, gp, 4:132], in_=bank(pb)[:, 128:256]),
                         reads=bkeys(pb), writes=["S_im"])
        dump("d_pw_re", pw_re, ["pw"])
        dump("d_pw_im", pw_im, ["pw"])
        dump("d_Sloc_re", S_re, ["S_re"])
        dump("d_Sloc_im", S_im, ["S_im"])
        dump("d_Kblk", Kblk, ["Kblk"], BF16)
        S.barrier()
        A.reset(p2_mark)
        if dbg and dbg.get("stop") == "S1":
            S.emit()
            return nc

        E_re = A.alloc([128, 32, 128])
        E_im = A.alloc([128, 32, 128])
        sc = [A.alloc([128, 16, 128]) for _ in range(4)]
        et = [A.alloc([128, 32 * 64]) for _ in range(2)]
        Sp = A.alloc([128, 64])
        Sp2 = A.alloc([128, 64])
        S.op("pool", lambda e: e.tensor_copy(out=E_re[:, :, 0:1], in_=Ak_re[:, 0, :].unsqueeze(2)), writes=["E"])
        S.op("pool", lambda e: e.tensor_copy(out=E_im[:, :, 0:1], in_=Ak_im[:, 0, :].unsqueeze(2)), writes=["E"])
        for k in range(7):
            w = 1 << k
            ta = et[0][:, 0:32 * w].rearrange("p (a b) -> p a b", a=32)
            tb = et[1][:, 0:32 * w].rearrange("p (a b) -> p a b", a=32)
            cmul3("pool", E_re[:, :, w:2 * w], E_im[:, :, w:2 * w], bc(Ak_re[:, k, :], w), bc(Ak_im[:, k, :], w),
                  E_re[:, :, 0:w], E_im[:, :, 0:w], ta, tb, ["E"], ["E"])
        for k in range(7):
            w = 1 << k
            L = 128 - w
            for hf in range(2):
                g0 = hf * 16
                are = Ak_re[:, k, g0:g0 + 16].unsqueeze(2).broadcast_to([128, 16, L])
                aim = Ak_im[:, k, g0:g0 + 16].unsqueeze(2).broadcast_to([128, 16, L])
                sre_lo = S_re[:, g0:g0 + 16, 4:4 + L]
                sim_lo = S_im[:, g0:g0 + 16, 4:4 + L]
                sre_hi = S_re[:, g0:g0 + 16, 4 + w:132]
                sim_hi = S_im[:, g0:g0 + 16, 4 + w:132]
                t = [sc[i][:, :, 0:L] for i in range(4)]
                S.op("dve", lambda e, t=t, are=are, sre_lo=sre_lo: e.tensor_tensor(out=t[0], in0=sre_lo, in1=are, op=ALU.mult),
                     reads=["S_re"], writes=["sc0"])
                S.op("dve", lambda e, t=t, aim=aim, sim_lo=sim_lo: e.tensor_tensor(out=t[1], in0=sim_lo, in1=aim, op=ALU.mult),
                     reads=["S_im"], writes=["sc1"])
                S.op("pool", lambda e, t=t, are=are, sim_lo=sim_lo: e.tensor_tensor(out=t[2], in0=sim_lo, in1=are, op=ALU.mult),
                     reads=["S_im"], writes=["sc2"])
                S.op("pool", lambda e, t=t, aim=aim, sre_lo=sre_lo: e.tensor_tensor(out=t[3], in0=sre_lo, in1=aim, op=ALU.mult),
                     reads=["S_re"], writes=["sc3"])
                S.op("dve", lambda e, t=t, sre_hi=sre_hi: e.tensor_tensor(out=sre_hi, in0=sre_hi, in1=t[0], op=ALU.add),
                     reads=["sc0", "S_re"], writes=["S_re"])
                S.op("dve", lambda e, t=t, sre_hi=sre_hi: e.tensor_tensor(out=sre_hi, in0=sre_hi, in1=t[1], op=ALU.subtract),
                     reads=["sc1", "S_re"], writes=["S_re"])
                S.op("pool", lambda e, t=t, sim_hi=sim_hi: e.tensor_tensor(out=sim_hi, in0=sim_hi, in1=t[2], op=ALU.add),
                     reads=["sc2", "S_im"], writes=["S_im"])
                S.op("pool", lambda e, t=t, sim_hi=sim_hi: e.tensor_tensor(out=sim_hi, in0=sim_hi, in1=t[3], op=ALU.add),
                     reads=["sc3", "S_im"], writes=["S_im"])
        SK = ["S_re", "S_im"]
        dump("d_S_re", S_re, SK)
        dump("d_S_im", S_im, SK)
        S.op("dve", lambda e: e.tensor_copy(out=Sp[:, 0:32], in_=S_re[:, :, 131]), reads=SK, writes=["Sp"])
        S.op("dve", lambda e: e.tensor_copy(out=Sp[:, 32:64], in_=S_im[:, :, 131]), reads=SK, writes=["Sp"])
        S.dma("pool", lambda e: e.dma_start(out=st_in[:, :], in_=Sp), reads=["Sp"], writes=["st_in"])
        ncores = dbg.get("ncores", 8) if dbg else 8
        groups = [[2 * i, 2 * i + 1] for i in range(ncores // 2)] if ncores > 1 else [[0]]
        S.op("pool", lambda e: e.collective_compute("AllGather", ALU.bypass, replica_groups=groups,
                                                    ins=[st_in.ap().opt()], outs=[st_out.ap().opt()]),
             reads=["st_in"], writes=["st_out"])
        S.dma("pool", lambda e: e.dma_start(out=Sp2, in_=st_out[0:128, :]), reads=["st_out"], writes=["Sp2"])
        S.op("dve", lambda e: e.tensor_scalar(out=Sp2, in0=Sp2, scalar1=flag_sb, scalar2=None, op0=ALU.mult),
             reads=["Sp2", "flag"], writes=["Sp2"])
        S.op("dve", lambda e: e.tensor_copy(out=S_re[:, :, 3:4], in_=Sp2[:, 0:32].unsqueeze(2)), reads=["Sp2"] + SK,
             writes=["S_re"])
        S.op("dve", lambda e: e.tensor_copy(out=S_im[:, :, 3:4], in_=Sp2[:, 32:64].unsqueeze(2)), reads=["Sp2"] + SK,
             writes=["S_im"])
        for hf in range(2):
            g0 = hf * 16
            pre_ = Sp2[:, g0:g0 + 16].unsqueeze(2).broadcast_to([128, 16, 128])
            pim_ = Sp2[:, 32 + g0:32 + g0 + 16].unsqueeze(2).broadcast_to([128, 16, 128])
            ere = E_re[:, g0:g0 + 16, :]
            eim = E_im[:, g0:g0 + 16, :]
            sre = S_re[:, g0:g0 + 16, 4:132]
            sim = S_im[:, g0:g0 + 16, 4:132]
            S.op("dve", lambda e, ere=ere, pre_=pre_: e.tensor_tensor(out=sc[0], in0=ere, in1=pre_, op=ALU.mult),
                 reads=["E", "Sp2"], writes=["sc0"])
            S.op("dve", lambda e, eim=eim, pim_=pim_: e.tensor_tensor(out=sc[1], in0=eim, in1=pim_, op=ALU.mult),
                 reads=["E", "Sp2"], writes=["sc1"])
            S.op("pool", lambda e, ere=ere, pim_=pim_: e.tensor_tensor(out=sc[2], in0=ere, in1=pim_, op=ALU.mult),
                 reads=["E", "Sp2"], writes=["sc2"])
            S.op("pool", lambda e, eim=eim, pre_=pre_: e.tensor_tensor(out=sc[3], in0=eim, in1=pre_, op=ALU.mult),
                 reads=["E", "Sp2"], writes=["sc3"])
            S.op("dve", lambda e, sre=sre: e.tensor_tensor(out=sre, in0=sre, in1=sc[0], op=ALU.add),
                 reads=["sc0", "S_re"], writes=["S_re"])
            S.op("dve", lambda e, sre=sre: e.tensor_tensor(out=sre, in0=sre, in1=sc[1], op=ALU.subtract),
                 reads=["sc1", "S_re"], writes=["S_re"])
            S.op("pool", lambda e, sim=sim: e.tensor_tensor(out=sim, in0=sim, in1=sc[2], op=ALU.add),
                 reads=["sc2", "S_im"], writes=["S_im"])
            S.op("pool", lambda e, sim=sim: e.tensor_tensor(out=sim, in0=sim, in1=sc[3], op=ALU.add),
                 reads=["sc3", "S_im"], writes=["S_im"])
        S.op("dve", lambda e: e.tensor_copy(out=Sst_re, in_=S_re[:, :, 3:131]), reads=["S_re"], writes=["Sst"])
        S.op("pool", lambda e: e.tensor_copy(out=Sst_im, in_=S_im[:, :, 3:131]), reads=["S_im"], writes=["Sst"])
        S.barrier()
        A.reset(p3_mark)

        Cp = A.alloc([128, 32, 16, 2, 32], BF16)
        tA = A.alloc([128, 32, 32])
        tB = A.alloc([128, 32, 32])
        uTf = [A.alloc([128, LH], BF16) for _ in range(2)]
        upm = [A.alloc([128, LH], BF16) for _ in range(2)]
        gch = [A.alloc([128, 16, 128], BF16) for _ in range(2)]
        gTf = [A.alloc([128, LH], BF16) for _ in range(2)]
        for tau in range(16):
            cmul3("dve", Cp[:, :, tau, 0, :], Cp[:, :, tau, 1, :], bc(pw_re[:, tau + 1, :], 32),
                  bc(pw_im[:, tau + 1, :], 32), Csm_re, Csm_nim, tA, tB, ["pw", "Csm"], ["Cp"], conj=True)
        for f in range(8):
            u_, uk = uTf[f % 2], "uTf%d" % (f % 2)
            S.dma("sp", lambda e, u_=u_, f=f: e.dma_start(out=u_, in_=UT[f]), writes=[uk])
            uv = upm[f % 2].rearrange("p (t n) -> p t n", t=16)
            upk = "upm%d" % (f % 2)
            S.op("pool", lambda e, uv=uv, u_=u_: e.tensor_copy(out=uv, in_=u_.rearrange("p (n t) -> p t n", t=16)),
                 reads=[uk], writes=[upk])
            uk = upk
            yv = bank(0, 4).rearrange("p (t c) -> p t c", t=16)
            S.op("dve", lambda e: e.memset(bank(0, 4), 0.0), writes=bkeys(0, 4))
            for tau in range(16):
                for tp in range(tau + 1):
                    S.op("pe", lambda e, yv=yv, uv=uv, tau=tau, tp=tp, f=f: e.matmul(
                        yv[:, tau, :], lhsT=uv[:, tp, :], rhs=Kblk[:, f, tau - tp, :], start=False, stop=False,
                        skip_group_check=True), reads=["Kblk", uk] + bkeys(0, 4), writes=bkeys(0, 4))
                for j in range(4):
                    gp = 4 * f + j
                    S.op("pe", lambda e, yv=yv, tau=tau, gp=gp, j=j: e.matmul(
                        yv[:, tau, 32 * j:32 * j + 32], lhsT=Sst_re[:, gp, :], rhs=Cp[:, gp, tau, 0, :], start=False,
                        stop=False, skip_group_check=True), reads=["Cp", "Sst"] + bkeys(0, 4), writes=bkeys(0, 4))
                    S.op("pe", lambda e, yv=yv, tau=tau, gp=gp, j=j: e.matmul(
                        yv[:, tau, 32 * j:32 * j + 32], lhsT=Sst_im[:, gp, :], rhs=Cp[:, gp, tau, 1, :], start=False,
                        stop=False, skip_group_check=True), reads=["Cp", "Sst"] + bkeys(0, 4), writes=bkeys(0, 4))
            if dbg and dbg.get("dump") and f == 0:
                ydump = A.alloc([128, 2048])
                S.op("act", lambda e: e.activation(out=ydump, in_=bank(0, 4), func=AF.Copy), reads=bkeys(0, 4),
                     writes=["ydump"])
                dump("d_yssm0", ydump, ["ydump"])
            gc_, gck = gch[f % 2], "gch%d" % (f % 2)
            S.op("act", lambda e, gc_=gc_: e.activation(out=gc_.rearrange("p a b -> p (a b)"), in_=bank(0, 4),
                                                        func=AF.Gelu_apprx_tanh), reads=bkeys(0, 4), writes=[gck])
            ptb = bank(4, 2).bitcast(BF16)[:, 0:2048].rearrange("p (t n) -> p t n", t=16)
            for tau in range(16):
                S.op("pe", lambda e, gc_=gc_, tau=tau, ptb=ptb: e.transpose(out=ptb[:, tau, :], in_=gc_[:, tau, :],
                                                                           identity=ident),
                     reads=[gck, "ident"], writes=bkeys(4, 2))
            g_, gk = gTf[f % 2], "gTf%d" % (f % 2)
            S.op("dve", lambda e, g_=g_, ptb=ptb: e.tensor_copy(out=g_.rearrange("p (n t) -> p t n", t=16), in_=ptb),
                 reads=bkeys(4, 2), writes=[gk])
            S.dma("sp", lambda e, g_=g_, f=f: e.dma_start(out=GT[f], in_=g_), reads=[gk])
        S.barrier()
        A.reset(l1_mark)
        if dbg and dbg.get("stop") == "S3":
            S.emit()
            return nc

        wg_sb = A.alloc([128, 8, 1024], BF16)
        wo2_sb = A.alloc([128, 8, D], BF16)
        bglu = A.alloc([128, 8])
        gtl = [A.alloc([128, 8, 512], BF16) for _ in range(2)]
        szl = [A.alloc([128, 8, 512], BF16) for _ in range(2)]
        yyT = [A.alloc([128, 8, 512], BF16) for _ in range(2)]
        sig = [A.alloc([128, 512]) for _ in range(2)]
        xres = [A.alloc([128, D]) for _ in range(2)]
        yt = [A.alloc([128, D]) for _ in range(2)]
        junk2 = A.alloc([128, D], BF16)
        ssc = A.alloc([128, 4])
        S.dma("sp", lambda e: e.dma_start(out=bglu, in_=b_gluT), writes=["bglu"])
        for kt in range(8):
            S.dma("pool", lambda e, kt=kt: e.dma_start(out=wg_sb[:, kt, :], in_=w_glu[kt * 128:(kt + 1) * 128, :]),
                  writes=["wg"])
        for kt in range(8):
            S.dma("pool", lambda e, kt=kt: e.dma_start(out=wo2_sb[:, kt, :], in_=w_out_ssm[kt * 128:(kt + 1) * 128, :]),
                  writes=["wo2"])
        fcnt = [0]

        def do_tile_F(tt):
            i2 = tt % 2
            g_, gk = gtl[i2], "gtl%d" % i2
            z_, zk = szl[i2], "szl%d" % i2
            y_, yk = yyT[i2], "yyT%d" % i2
            ts = slice(tt * 512, (tt + 1) * 512)
            S.dma("sp", lambda e: e.dma_start(out=g_, in_=GT[:, :, ts].rearrange("f d t -> d f t")), writes=[gk])
            S.dma("sp", lambda e: e.dma_start(out=z_, in_=SZ1[:, :, ts].rearrange("f d t -> d f t")), writes=[zk])
            for ct in range(8):
                b = fcnt[0] % 2
                fcnt[0] += 1
                sg, sgk = sig[b], "sig%d" % b
                for kt in range(8):
                    S.op("pe", lambda e, b=b, kt=kt, ct=ct: e.matmul(
                        bank(b), lhsT=wg_sb[:, kt, ct * 128:(ct + 1) * 128], rhs=g_[:, kt, :],
                        start=(kt == 0), stop=(kt == 7)), reads=["wg", gk], writes=bkeys(b))
                S.op("act", lambda e, b=b, sg=sg, ct=ct: e.activation(out=sg, in_=bank(b), func=AF.Sigmoid,
                                                                      bias=bglu[:, ct:ct + 1]),
                     reads=bkeys(b) + ["bglu"], writes=[sgk])
                S.op("dve", lambda e, sg=sg, ct=ct: e.tensor_tensor(out=sg, in0=sg, in1=g_[:, ct, :], op=ALU.mult),
                     reads=[sgk, gk], writes=[sgk])
                S.op("pool", lambda e, sg=sg, ct=ct: e.tensor_tensor(out=y_[:, ct, :], in0=sg, in1=z_[:, ct, :],
                                                                     op=ALU.mult),
                     reads=[sgk, zk], writes=[yk])

            def catf(kt, s):
                return y_[:, kt, s * 128:(s + 1) * 128], yk
            out_proj_resid(1, tt, catf, 8, wo2_sb, "wo2", X1, out, xres, yt, ssc, junk2, tt * 4)

        for tt in range(4):
            do_tile_F(tt)
        S.emit()
    return nc


def prep_inputs(inputs):
    f = lambda a: np.ascontiguousarray(np.asarray(a, dtype=np.float32))
    x = f(inputs["x"])
    c = f(inputs["c"])
    fm = lambda v, n: np.ascontiguousarray(v.reshape(n, 128).T)
    shared = {
        "g_preT": np.stack([fm(f(inputs["ln_pre_g"])[l], 16) for l in range(2)]),
        "g_post": f(inputs["ln_post_g"]),
        "w_mod": f(inputs["w_mod"]),
        "b_modT": np.stack([fm(f(inputs["b_mod"])[l], 48) for l in range(2)]),
        "b_mod": f(inputs["b_mod"]),
        "w_in_ab": f(inputs["w_in_ab"])[0],
        "w_out_ab": f(inputs["w_out_ab"])[0],
        "sgu_norm_g": f(inputs["sgu_norm_g"])[0],
        "sgu_w": f(inputs["sgu_w"])[0],
        "sgu_b": f(inputs["sgu_b"])[0].reshape(1024),
        "w_in_ssm": f(inputs["w_in_ssm"])[0],
        "w_out_ssm": f(inputs["w_out_ssm"])[0],
        "w_glu": f(inputs["w_glu"])[0],
        "b_gluT": fm(f(inputs["b_glu"])[0], 8),
        "d_skipT": fm(f(inputs["d_skip"])[0], 8),
    }
    def sm(a):
        return np.ascontiguousarray(a.reshape(32, 2, 64).transpose(1, 2, 0).reshape(128, 32))
    shared["lamre_sm"] = sm(f(inputs["lam_re"])[0])
    shared["lamim_sm"] = sm(f(inputs["lam_im"])[0])
    shared["logdt_sm"] = sm(np.repeat(f(inputs["log_dt"])[0][:, None], 64, axis=1))

    def smz_b(b):
        o = np.zeros((2, 64, 32, 2, 16), np.float32)
        bb = b.reshape(32, 2, 64, 16)
        for g2 in range(2):
            o[g2, :, :, g2, :] = bb[:, g2].transpose(1, 0, 2)
        return np.ascontiguousarray(o.reshape(128, 32, 32))

    def smz_c(cm):
        return smz_b(np.ascontiguousarray(cm.transpose(0, 2, 1)))
    shared["bre_smz"] = smz_b(f(inputs["b_re"])[0])
    shared["bim_smz"] = smz_b(f(inputs["b_im"])[0])
    shared["cre_smz"] = smz_c(f(inputs["c_re"])[0])
    shared["cim_smz"] = smz_c(f(inputs["c_im"])[0])
    in_maps = []
    for core in range(8):
        b, s = core // 2, core % 2
        m = dict(shared)
        m["x_own"] = np.ascontiguousarray(x[b, s * LH:(s + 1) * LH])
        m["x_prev"] = np.ascontiguousarray(x[b, 0:LH])
        m["flag"] = np.full((128, 1), float(s), np.float32)
        m["cT"] = fm(c[b], 16)
        in_maps.append(m)
    return in_maps


def kernel(**inputs):
    nc = build_program()
    in_maps = prep_inputs(inputs)
    res = run_bass_kernel_spmd(nc, in_maps, core_ids=list(range(8)))
    out = np.empty((4, 2 * LH, D), np.float32)
    for core in range(8):
        b, s = core // 2, core % 2
        out[b, s * LH:(s + 1) * LH] = np.asarray(res.results[core]["out"], np.float32)
    return out
```

```python
import contextlib
import math
import numpy as np
import concourse.bass as bass
import concourse.mybir as mybir
from concourse.bass_utils import run_bass_kernel_spmd

F32 = mybir.dt.float32
BF16 = mybir.dt.bfloat16
I32 = mybir.dt.int32
AF = mybir.ActivationFunctionType
ALU = mybir.AluOpType
AX = mybir.AxisListType

D = 2048
LH = 2048
NT = 4
EPS = 1e-6
T16 = 16
NCH = LH // T16


class Sched:
    ENGS = ("pe", "act", "dve", "pool", "sp")
    NSLOT = 8

    def __init__(self, nc):
        self.nc = nc
        self.ops = []
        self.last_w = {}
        self.readers = {}
        self.dma_rr = {e: 0 for e in self.ENGS}
        self.slot_last = {}
        self.last_op = {}

    def _add(self, engine, fn, reads, writes, dma, extra=()):
        deps = set(extra)
        for b in reads:
            if b in self.last_w:
                deps.add(self.last_w[b])
        for b in writes:
            if b in self.last_w:
                deps.add(self.last_w[b])
            for r in self.readers.get(b, ()):
                deps.add(r)
        oid = len(self.ops)
        slot = None
        if dma:
            slot = self.dma_rr[engine] % self.NSLOT
            self.dma_rr[engine] += 1
            key = (engine, slot)
            if key in self.slot_last:
                deps.add(self.slot_last[key])
            self.slot_last[key] = oid
        deps.discard(oid)
        self.ops.append(dict(engine=engine, fn=fn, deps=deps, dma=dma, slot=slot))
        for b in writes:
            self.last_w[b] = oid
            self.readers[b] = []
        for b in reads:
            if b not in writes:
                self.readers.setdefault(b, []).append(oid)
        self.last_op[engine] = oid
        return oid

    def op(self, engine, fn, reads=(), writes=()):
        return self._add(engine, fn, tuple(reads), tuple(writes), False)

    def dma(self, engine, fn, reads=(), writes=()):
        return self._add(engine, fn, tuple(reads), tuple(writes), True)

    def barrier(self):
        pend = set(self.last_op.values()) | set(self.slot_last.values())
        for e in self.ENGS:
            self._add(e, lambda eng: eng.nop(), (), (), False, extra=pend)
        self.last_w = {}
        self.readers = {}

    def emit(self, final_wait_engine="sp"):
        nc = self.nc
        ops = self.ops

        def skip(od, o):
            return od["engine"] == "pe" and o["engine"] == "pe" and not od["dma"] and not o["dma"]

        need = [False] * len(ops)
        for o in ops:
            for d in o["deps"]:
                if not skip(ops[d], o):
                    need[d] = True
        cnt = {e: 0 for e in self.ENGS}
        slotcnt = {}
        tok = [None] * len(ops)
        for i, o in enumerate(ops):
            if o["dma"]:
                key = ("dma", o["engine"], o["slot"])
                slotcnt[key] = slotcnt.get(key, 0) + 16
                tok[i] = (key, slotcnt[key])
            elif need[i]:
                cnt[o["engine"]] += 1
                tok[i] = (("c", o["engine"]), cnt[o["engine"]])
        sem_keys = sorted({t[0] for t in tok if t is not None}, key=str)
        with contextlib.ExitStack() as st:
            sems = {}
            for k in sem_keys:
                sems[k] = st.enter_context(nc.semaphore("s_" + "_".join(str(x) for x in k)))
            per_eng = {e: [] for e in self.ENGS}
            waited = {e: {} for e in self.ENGS}
            for i, o in enumerate(ops):
                e = o["engine"]
                waits = {}
                for d in o["deps"]:
                    if skip(ops[d], o):
                        continue
                    k, v = tok[d]
                    if waited[e].get(k, 0) >= v:
                        continue
                    waits[k] = max(waits.get(k, 0), v)
                for k, v in waits.items():
                    waited[e][k] = v
                per_eng[e].append((o["fn"], list(waits.items()), tok[i]))
            fin = [(k, v) for k, v in slotcnt.items() if waited[final_wait_engine].get(k, 0) < v]
            blockfn = {"pe": "tensor", "act": "scalar", "dve": "vector", "pool": "gpsimd", "sp": "sync"}
            with nc.Block() as block:
                for e in self.ENGS:
                    lst = per_eng[e]

                    def body(eng, lst=lst, e=e):
                        for fn, waits, t in lst:
                            for k, v in waits:
                                eng.wait_ge(sems[k], v)
                            ins = fn(eng)
                            if t is not None:
                                ins.then_inc(sems[t[0]], 16 if t[0][0] == "dma" else 1)
                        if e == final_wait_engine:
                            for k, v in fin:
                                eng.wait_ge(sems[k], v)

                    getattr(block, blockfn[e])(body)
        return len(ops)


class Arena:
    def __init__(self, t, n32):
        self.t = t
        self.n32 = n32
        self.off = 0

    def mark(self):
        return self.off

    def reset(self, m):
        self.off = m

    def alloc(self, shape, dt=F32):
        n = int(np.prod(shape[1:]))
        b = 4 if dt in (F32, I32) else 2
        n32 = (n * b + 3) // 4
        assert self.off + n32 <= self.n32, ("SBUF arena overflow", self.off, n32, self.n32)
        v = self.t[0:shape[0], self.off:self.off + n32]
        self.off += n32
        if b == 2:
            v = v.bitcast(dt)
        elif dt != F32:
            v = v.bitcast(dt)
        if len(shape) > 2:
            names = "abcdefg"[:len(shape) - 1]
            kw = {names[i]: shape[i + 1] for i in range(len(shape) - 2)}
            v = v.rearrange("p (%s) -> p %s" % (" ".join(names), " ".join(names)), **kw)
        return v


def build_program(dbg=None):
    nc = bass.Bass("TRN2", target_bir_lowering=False)

    def din(name, shape, dt=F32):
        return nc.dram_tensor(name, list(shape), dt, kind="ExternalInput").ap()

    def dscr(name, shape, dt=F32):
        kind = "ExternalOutput" if (dbg and name in dbg) else "Internal"
        return nc.dram_tensor(name, list(shape), dt, kind=kind).ap()

    x_own = din("x_own", [LH, D])
    x_prev = din("x_prev", [LH, D])
    flag = din("flag", [128, 1])
    cT = din("cT", [128, 16])
    g_preT = din("g_preT", [2, 128, 16])
    g_post = din("g_post", [2, D])
    w_mod = din("w_mod", [2, D, 3 * D])
    b_modT = din("b_modT", [2, 128, 48])
    b_mod = din("b_mod", [2, 3 * D])
    w_in_ab = din("w_in_ab", [D, 7168])
    w_out_ab = din("w_out_ab", [D, D])
    sgu_norm_g = din("sgu_norm_g", [1024])
    sgu_w = din("sgu_w", [8, 128, 128])
    sgu_b = din("sgu_b", [1024])
    w_in_ssm = din("w_in_ssm", [D, D])
    w_out_ssm = din("w_out_ssm", [1024, D])
    w_glu = din("w_glu", [1024, 1024])
    b_gluT = din("b_gluT", [128, 8])
    d_skipT = din("d_skipT", [128, 8])
    lamre_sm = din("lamre_sm", [128, 32])
    lamim_sm = din("lamim_sm", [128, 32])
    logdt_sm = din("logdt_sm", [128, 32])
    bre_smz = din("bre_smz", [128, 32, 32])
    bim_smz = din("bim_smz", [128, 32, 32])
    cre_smz = din("cre_smz", [128, 32, 32])
    cim_smz = din("cim_smz", [128, 32, 32])

    out = nc.dram_tensor("out", [LH, D], F32, kind="ExternalOutput").ap()

    QT = dscr("QT", [8, 128, LH], BF16)
    KT = dscr("KT", [8, 128, 2 * LH], BF16)
    VS = dscr("VS", [2 * LH, 1024], BF16)
    OA = dscr("OA", [8, 128, LH], BF16)
    SBZ = dscr("SBZ", [8, 128, LH], BF16)
    skip_l0 = bool(dbg and dbg.get("skip_l0"))
    X1 = din("X1in", [LH, D]) if skip_l0 else dscr("X1", [LH, D], F32)
    st_in = nc.dram_tensor("st_in", [128, 64], F32)
    st_out = nc.dram_tensor("st_out", [256, 64], F32)

    NA = 52000
    with contextlib.ExitStack() as st:
        arena_t = st.enter_context(nc.sbuf_tensor("arena", [128, NA], F32))
        ps = st.enter_context(nc.psum_tensor("ps", [128, 4096], F32))
        A = Arena(arena_t, NA)
        S = Sched(nc)

        def bank(i, n=1):
            return ps[:, i * 512:(i + n) * 512]

        def bkeys(i, n=1):
            return ["ps%d" % j for j in range(i, i + n)]

        def dump(name, ap, keys, dt=F32):
            if not (dbg and dbg.get("dump")):
                return
            t = nc.dram_tensor(name, list(ap.shape), dt, kind="ExternalOutput").ap()
            S.dma("sp", lambda e: e.dma_start(out=t, in_=ap), reads=keys)

        ident = A.alloc([128, 128], BF16)
        identf = A.alloc([128, 128], F32)
        ones_bf = A.alloc([128, 128], BF16)
        negones = A.alloc([128, 128], BF16)
        negUI = A.alloc([128, 128], BF16)
        flag_sb = A.alloc([128, 1])
        S.op("pool", lambda e: e.memset(identf, 0.0), writes=["identf"])
        S.op("pool", lambda e: e.affine_select(out=identf, in_=identf, compare_op=ALU.not_equal, fill=1.0,
                                               base=0, pattern=[[-1, 128]], channel_multiplier=1),
             reads=["identf"], writes=["identf"])
        S.op("pool", lambda e: e.tensor_copy(out=ident, in_=identf), reads=["identf"], writes=["ident"])
        S.op("pool", lambda e: e.memset(ones_bf, 1.0), writes=["ones_bf"])
        S.op("pool", lambda e: e.memset(negones, -1.0), writes=["negones"])
        S.op("pool", lambda e: e.memset(negUI, -1.0), writes=["negUI"])
        S.op("pool", lambda e: e.affine_select(out=negUI, in_=negUI, compare_op=ALU.is_ge, fill=0.0,
                                               base=0, pattern=[[-1, 128]], channel_multiplier=1),
             reads=["negUI"], writes=["negUI"])
        S.dma("sp", lambda e: e.dma_start(out=flag_sb, in_=flag), writes=["flag"])

        modT = A.alloc([128, 2, 32])
        a_pre = A.alloc([128, 2, 16])
        gpost_row = A.alloc([128, 2, D])
        m_phase = A.mark()
        cT_sb = A.alloc([128, 16])
        cond_bf = A.alloc([128, 16], BF16)
        cond_rep = A.alloc([128, 16, 128], BF16)
        gpre_sb = A.alloc([128, 2, 16])
        bmodT_sb = A.alloc([128, 2, 48])
        brow = A.alloc([128, D])
        grow = A.alloc([128, D])
        wm = [A.alloc([128, 3 * D], BF16) for _ in range(2)]
        S.dma("sp", lambda e: e.dma_start(out=cT_sb, in_=cT), writes=["cT"])
        S.dma("sp", lambda e: e.dma_start(out=gpre_sb, in_=g_preT.rearrange("l p f -> p l f")), writes=["gpre"])
        S.dma("sp", lambda e: e.dma_start(out=bmodT_sb, in_=b_modT.rearrange("l p f -> p l f")), writes=["bmodT"])
        S.op("act", lambda e: e.activation(out=cond_bf, in_=cT_sb, func=AF.Silu), reads=["cT"], writes=["cond"])
        S.op("dve", lambda e: e.tensor_copy(out=cond_rep, in_=cond_bf.unsqueeze(2).broadcast_to([128, 16, 128])),
             reads=["cond"], writes=["cond_rep"])
        for l in range(2):
            S.op("dve", lambda e: e.memset(bank(0, 5), 0.0), writes=bkeys(0, 5))
            for kt in range(16):
                wb = wm[kt % 2]
                wk = "wm%d" % (kt % 2)
                S.dma("pool", lambda e, wb=wb, l=l, kt=kt: e.dma_start(out=wb, in_=w_mod[l, kt * 128:(kt + 1) * 128, :]),
                      writes=[wk])
                for ft in range(32):
                    S.op("pe", lambda e, wb=wb, ft=ft, kt=kt: e.matmul(
                        bank(0)[:, ft:ft + 1], lhsT=wb[:, ft * 128:(ft + 1) * 128], rhs=cond_bf[:, kt:kt + 1],
                        start=False, stop=False, skip_group_check=True),
                        reads=[wk, "cond"] + bkeys(0), writes=bkeys(0))
                for cc in range(4):
                    S.op("pe", lambda e, wb=wb, cc=cc, kt=kt: e.matmul(
                        bank(1 + cc), lhsT=cond_rep[:, kt, :], rhs=wb[:, 4096 + cc * 512:4096 + (cc + 1) * 512],
                        start=False, stop=False, skip_group_check=True),
                        reads=[wk, "cond_rep"] + bkeys(1 + cc), writes=bkeys(1 + cc))
            S.op("dve", lambda e, l=l: e.tensor_tensor(out=modT[:, l, :], in0=bank(0)[:, 0:32], in1=bmodT_sb[:, l, 0:32],
                                                       op=ALU.add),
                 reads=bkeys(0) + ["bmodT"], writes=["modT%d" % l])
            S.op("dve", lambda e, l=l: e.scalar_tensor_tensor(out=a_pre[:, l, :], in0=modT[:, l, 16:32], scalar=1.0,
                                                              in1=gpre_sb[:, l, :], op0=ALU.add, op1=ALU.mult),
                 reads=["modT%d" % l, "gpre"], writes=["a_pre%d" % l])
            S.dma("sp", lambda e, l=l: e.dma_start(out=brow, in_=b_mod[l, 2 * D:3 * D].partition_broadcast(128)),
                  writes=["brow"])
            S.dma("sp", lambda e, l=l: e.dma_start(out=grow, in_=g_post[l].partition_broadcast(128)), writes=["grow"])
            S.op("dve", lambda e: e.tensor_tensor(out=brow, in0=bank(1, 4), in1=brow, op=ALU.add),
                 reads=bkeys(1, 4) + ["brow"], writes=["brow"])
            S.op("dve", lambda e, l=l: e.tensor_tensor(out=gpost_row[:, l, :], in0=brow, in1=grow, op=ALU.mult),
                 reads=["brow", "grow"], writes=["gpost%d" % l])
        dump("d_modT", modT, ["modT0", "modT1"])
        dump("d_apre", a_pre, ["a_pre0", "a_pre1"])
        dump("d_gpost", gpost_row, ["gpost0", "gpost1"])
        S.barrier()
        A.reset(m_phase)

        def norm_transpose_tile(src, tok0, l, xraw, xs, hT, tagbase, junk, ssb, it):
            for s in range(4):
                xr = xraw[(it * 4 + s) % 2]
                xrk = "xraw%d" % ((it * 4 + s) % 2)
                S.dma("sp", lambda e, xr=xr, s=s: e.dma_start(out=xr, in_=src[tok0 + s * 128: tok0 + (s + 1) * 128, :]),
                      writes=[xrk])
                S.op("act", lambda e, xr=xr, s=s: e.activation(out=junk, in_=xr, func=AF.Square,
                                                               accum_out=ssb[:, s:s + 1]),
                     reads=[xrk], writes=["junk", "ss%d" % s])
                S.op("act", lambda e, s=s: e.activation(out=ssb[:, 4 + s:5 + s], in_=ssb[:, s:s + 1], func=AF.Ln,
                                                        scale=1.0 / D, bias=eps_sb),
                     reads=["ss%d" % s], writes=["ln%d" % s])
                S.op("act", lambda e, s=s: e.activation(out=ssb[:, 8 + s:9 + s], in_=ssb[:, 4 + s:5 + s], func=AF.Exp,
                                                        scale=-0.5),
                     reads=["ln%d" % s], writes=["rstd%d" % s])
                S.op("dve", lambda e, xr=xr, s=s: e.tensor_scalar(out=xs[:, s, :], in0=xr, scalar1=ssb[:, 8 + s:9 + s],
                                                                  scalar2=None, op0=ALU.mult),
                     reads=[xrk, "rstd%d" % s], writes=["xs%d" % s])
            for ft in range(16):
                b = ft % 2
                pst = bank(b).bitcast(BF16)[:, 0:512]
                for s in range(4):
                    S.op("pe", lambda e, pst=pst, s=s, ft=ft: e.transpose(out=pst[:, s * 128:(s + 1) * 128],
                                                                          in_=xs[:, s, ft * 128:(ft + 1) * 128],
                                                                          identity=ident),
                         reads=["xs%d" % s, "ident"], writes=bkeys(b))
                eng = "act" if ft % 2 == 0 else "dve"
                if eng == "act":
                    S.op("act", lambda e, pst=pst, ft=ft: e.activation(out=hT[:, ft, :], in_=pst, func=AF.Identity,
                                                                       scale=a_pre[:, l, ft:ft + 1],
                                                                       bias=modT[:, l, ft:ft + 1]),
                         reads=bkeys(b) + ["a_pre%d" % l, "modT%d" % l], writes=["%s_%d" % (tagbase, ft)])
                else:
                    S.op("dve", lambda e, pst=pst, ft=ft: e.tensor_scalar(out=hT[:, ft, :], in0=pst,
                                                                          scalar1=a_pre[:, l, ft:ft + 1],
                                                                          scalar2=modT[:, l, ft:ft + 1],
                                                                          op0=ALU.mult, op1=ALU.add),
                         reads=bkeys(b) + ["a_pre%d" % l, "modT%d" % l], writes=["%s_%d" % (tagbase, ft)])

        def out_proj_resid(l, tt, catf, nk, wo, wokey, xsrc, dst, xres, yt, ssb, junk2, it0):
            for s in range(4):
                it = it0 + s
                pb = (it % 2) * 4
                for cc in range(4):
                    for kt in range(nk):
                        cap, ck = catf(kt, s)
                        S.op("pe", lambda e, pb=pb, cc=cc, kt=kt, cap=cap: e.matmul(
                            bank(pb + cc), lhsT=cap,
                            rhs=wo[:, kt, cc * 512:(cc + 1) * 512], start=(kt == 0), stop=(kt == nk - 1)),
                            reads=[ck, wokey], writes=bkeys(pb + cc))
                xr = xres[it % 2]
                xrk = "xres%d" % (it % 2)
                r0 = tt * 512 + s * 128
                S.dma("sp", lambda e, xr=xr, r0=r0: e.dma_start(out=xr, in_=xsrc[r0:r0 + 128, :]), writes=[xrk])
                S.op("act", lambda e, pb=pb: e.activation(out=junk2, in_=bank(pb, 4), func=AF.Square,
                                                          accum_out=ssb[:, 0:1]),
                     reads=bkeys(pb, 4), writes=["junk2", "pss"])
                S.op("act", lambda e: e.activation(out=ssb[:, 1:2], in_=ssb[:, 0:1], func=AF.Ln, scale=1.0 / D,
                                                   bias=eps_sb), reads=["pss"], writes=["pln"])
                S.op("act", lambda e: e.activation(out=ssb[:, 2:3], in_=ssb[:, 1:2], func=AF.Exp, scale=-0.5),
                     reads=["pln"], writes=["prstd"])
                y_ = yt[it % 2]
                yk = "yt%d" % (it % 2)
                S.op("dve", lambda e, pb=pb, y_=y_: e.scalar_tensor_tensor(out=y_, in0=bank(pb, 4), scalar=ssb[:, 2:3],
                                                                           in1=gpost_row[:, l, :], op0=ALU.mult,
                                                                           op1=ALU.mult),
                     reads=bkeys(pb, 4) + ["prstd", "gpost%d" % l], writes=[yk])
                S.op("pool", lambda e, y_=y_, xr=xr: e.tensor_tensor(out=y_, in0=y_, in1=xr, op=ALU.add),
                     reads=[yk, xrk], writes=[yk])
                S.dma("sp", lambda e, y_=y_, r0=r0: e.dma_start(out=dst[r0:r0 + 128, :], in_=y_), reads=[yk])

        eps_sb = A.alloc([128, 1])
        S.op("pool", lambda e: e.memset(eps_sb, EPS), writes=["eps"])

        l0_mark = A.mark()
        normg_row = A.alloc([128, 1024])
        bsb = A.alloc([128, 1024])
        WcT = A.alloc([128, 8, 128], BF16)
        tri = A.alloc([128, 128])
        sguw_sb = A.alloc([128, 8, 128])
        S.dma("sp", lambda e: e.dma_start(out=normg_row, in_=sgu_norm_g.partition_broadcast(128)), writes=["normg"])
        S.dma("sp", lambda e: e.dma_start(out=bsb, in_=sgu_b.partition_broadcast(128)), writes=["bsb"])
        S.dma("sp", lambda e: e.dma_start(out=sguw_sb, in_=sgu_w.rearrange("h t s -> t h s")), writes=["sguw"])
        S.op("pool", lambda e: e.memset(tri, 1.0), writes=["tri"])
        S.op("pool", lambda e: e.affine_select(out=tri, in_=tri, compare_op=ALU.is_ge, fill=0.0, base=0,
                                               pattern=[[1, 128]], channel_multiplier=-1),
             reads=["tri"], writes=["tri"])
        for h in range(8):
            S.op("pe", lambda e, h=h: e.transpose(out=bank(h % 2)[:, 0:128], in_=sguw_sb[:, h, :], identity=identf),
                 reads=["sguw", "identf"], writes=bkeys(h % 2))
            S.op("dve", lambda e, h=h: e.tensor_tensor(out=WcT[:, h, :], in0=bank(h % 2)[:, 0:128], in1=tri, op=ALU.mult),
                 reads=bkeys(h % 2) + ["tri"], writes=["WcT"])

        xraw = [A.alloc([128, D]) for _ in range(2)]
        junk = A.alloc([128, D], BF16)
        ssb = A.alloc([128, 12])
        xs = A.alloc([128, 4, D], BF16)
        hT = [A.alloc([128, 16, 512], BF16) for _ in range(2)]
        Wc = [A.alloc([128, 16, 512], BF16) for _ in range(3)]
        guT = A.alloc([128, 8, 512], BF16)
        szT = A.alloc([128, 8, 512], BF16)
        vn = A.alloc([128, 4, 1024], BF16)
        oaT = A.alloc([128, 8, 512], BF16)
        stg = [A.alloc([128, 512], BF16) for _ in range(4)]
        gv = A.alloc([128, 512])
        sq = A.alloc([128, 512])
        ssq = A.alloc([128, 12])
        t1 = A.alloc([128, 1024])
        t2 = A.alloc([128, 1024])

        wcount = [0]
        stgc = [0]
        pcount = [0]
        SCALE_Q = 1.0 / math.sqrt(128.0)

        def load_w(col0):
            i = wcount[0] % 3
            wcount[0] += 1
            S.dma("pool", lambda e, i=i, col0=col0: e.dma_start(
                out=Wc[i], in_=w_in_ab[:, col0:col0 + 512].rearrange("(kt k) c -> k kt c", k=128)),
                writes=["Wc%d" % i])
            return Wc[i], "Wc%d" % i

        def next_bank():
            b = 2 + pcount[0] % 4
            pcount[0] += 1
            return b

        def next_stg():
            i = stgc[0] % 4
            stgc[0] += 1
            return stg[i], "stg%d" % i

        def do_tile_A(tt):
            own = tt >= 4
            src = x_own if own else x_prev
            tok0 = (tt % 4) * 512
            ctx0 = tt * 512
            hb = hT[tt % 2]
            hk = "hT%d" % (tt % 2)
            hkeys = ["%s_%d" % (hk, ft) for ft in range(16)]
            norm_transpose_tile(src, tok0, 0, xraw, xs, hb, hk, junk, ssb, tt)
            if tt == 4:
                dump("d_hT", hb, hkeys, BF16)
                dump("d_ssb", ssb, ["rstd3"])
                dump("d_xs", xs, ["xs0", "xs1", "xs2", "xs3"], BF16)
            segs = []
            if own:
                segs += [("av", 1024, 0), ("av", 1536, 1), ("au", 0, 0), ("au", 512, 1), ("az", 2048, 0), ("az", 2560, 1),
                         ("q", 3072, 0), ("q", 3584, 1)]
            segs += [("k", 4096, 0), ("k", 4608, 1), ("v", 5120, 0), ("v", 5632, 1)]
            if own:
                segs += [("bz", 6144, 0), ("bz", 6656, 1)]
            for kind, col0, cc in segs:
                W_, wk = load_w(col0)
                if kind in ("av", "v"):
                    for s in range(4):
                        b = next_bank()
                        for kt in range(16):
                            S.op("pe", lambda e, b=b, kt=kt, s=s, W_=W_: e.matmul(
                                bank(b), lhsT=hb[:, kt, s * 128:(s + 1) * 128], rhs=W_[:, kt, :],
                                start=(kt == 0), stop=(kt == 15)),
                                reads=[hkeys[kt], wk], writes=bkeys(b))
                        if kind == "v":
                            sg, sk = next_stg()
                            if own:
                                S.op("dve", lambda e, b=b, sg=sg: e.tensor_copy(out=sg, in_=bank(b)),
                                     reads=bkeys(b), writes=[sk])
                            else:
                                S.op("dve", lambda e, b=b, sg=sg: e.tensor_scalar(out=sg, in0=bank(b), scalar1=flag_sb,
                                                                                  scalar2=None, op0=ALU.mult),
                                     reads=bkeys(b) + ["flag"], writes=[sk])
                            r0 = ctx0 + s * 128
                            S.dma("sp", lambda e, sg=sg, r0=r0, cc=cc: e.dma_start(
                                out=VS[r0:r0 + 128, cc * 512:(cc + 1) * 512], in_=sg), reads=[sk])
                        else:
                            S.op("act", lambda e, b=b: e.activation(out=gv, in_=bank(b), func=AF.Gelu_apprx_tanh),
                                 reads=bkeys(b), writes=["gv"])
                            S.op("dve", lambda e: e.tensor_tensor(out=sq, in0=gv, in1=gv, op=ALU.mult),
                                 reads=["gv"], writes=["sq"])
                            S.op("dve", lambda e: e.tensor_reduce(out=ssq[:, 0:4], in_=sq.rearrange("p (h d) -> p h d", h=4),
                                                                  axis=AX.X, op=ALU.add),
                                 reads=["sq"], writes=["ssq"])
                            S.op("act", lambda e: e.activation(out=ssq[:, 4:8], in_=ssq[:, 0:4], func=AF.Ln,
                                                               scale=1.0 / 128, bias=eps_sb),
                                 reads=["ssq"], writes=["ssqln"])
                            S.op("act", lambda e: e.activation(out=ssq[:, 8:12], in_=ssq[:, 4:8], func=AF.Exp, scale=-0.5),
                                 reads=["ssqln"], writes=["ssqr"])
                            S.op("dve", lambda e: e.tensor_tensor(
                                out=sq.rearrange("p (h d) -> p h d", h=4), in0=gv.rearrange("p (h d) -> p h d", h=4),
                                in1=ssq[:, 8:12].unsqueeze(2).broadcast_to([128, 4, 128]), op=ALU.mult),
                                reads=["gv", "ssqr"], writes=["sq"])
                            S.op("dve", lambda e, s=s, cc=cc: e.tensor_tensor(
                                out=vn[:, s, cc * 512:(cc + 1) * 512], in0=sq, in1=normg_row[:, cc * 512:(cc + 1) * 512],
                                op=ALU.mult), reads=["sq", "normg"], writes=["vn%d" % s])
                else:
                    for ct in range(4):
                        b = next_bank()
                        h = cc * 4 + ct
                        for kt in range(16):
                            S.op("pe", lambda e, b=b, kt=kt, ct=ct, W_=W_: e.matmul(
                                bank(b), lhsT=W_[:, kt, ct * 128:(ct + 1) * 128], rhs=hb[:, kt, :],
                                start=(kt == 0), stop=(kt == 15)),
                                reads=[hkeys[kt], wk], writes=bkeys(b))
                        if kind == "au":
                            S.op("act", lambda e, b=b, h=h: e.activation(out=guT[:, h, :], in_=bank(b),
                                                                         func=AF.Gelu_apprx_tanh),
                                 reads=bkeys(b), writes=["guT"])
                        elif kind == "az":
                            S.op("act", lambda e, b=b, h=h: e.activation(out=szT[:, h, :], in_=bank(b), func=AF.Silu),
                                 reads=bkeys(b), writes=["szT"])
                        elif kind == "q":
                            sg, sk = next_stg()
                            S.op("act", lambda e, b=b, sg=sg: e.activation(out=sg, in_=bank(b), func=AF.Copy,
                                                                           scale=SCALE_Q),
                                 reads=bkeys(b), writes=[sk])
                            S.dma("sp", lambda e, sg=sg, h=h, tok0=tok0: e.dma_start(out=QT[h, :, tok0:tok0 + 512], in_=sg),
                                  reads=[sk])
                        elif kind == "k":
                            sg, sk = next_stg()
                            S.op("dve", lambda e, b=b, sg=sg: e.tensor_copy(out=sg, in_=bank(b)),
                                 reads=bkeys(b), writes=[sk])
                            S.dma("sp", lambda e, sg=sg, h=h, ctx0=ctx0: e.dma_start(out=KT[h, :, ctx0:ctx0 + 512], in_=sg),
                                  reads=[sk])
                        elif kind == "bz":
                            sg, sk = next_stg()
                            S.op("act", lambda e, b=b, sg=sg: e.activation(out=sg, in_=bank(b), func=AF.Silu),
                                 reads=bkeys(b), writes=[sk])
                            S.dma("sp", lambda e, sg=sg, h=h, tok0=tok0: e.dma_start(out=SBZ[h, :, tok0:tok0 + 512], in_=sg),
                                  reads=[sk])
                if own and kind == "az" and cc == 1:
                    for s in range(4):
                        for h in range(8):
                            b6 = 6 + h // 4
                            S.op("pe", lambda e, b6=b6, h=h, s=s: e.matmul(
                                bank(b6)[:, (h % 4) * 128:(h % 4 + 1) * 128], lhsT=vn[:, s, h * 128:(h + 1) * 128],
                                rhs=WcT[:, h, :], start=True, stop=True),
                                reads=["vn%d" % s, "WcT"], writes=bkeys(b6))
                        S.op("dve", lambda e: e.tensor_tensor(out=t1, in0=bank(6, 2), in1=bsb, op=ALU.add),
                             reads=bkeys(6, 2) + ["bsb"], writes=["t1"])
                        S.op("pool", lambda e, s=s: e.tensor_tensor(
                            out=t2.rearrange("p (h t) -> p h t", h=8), in0=t1.rearrange("p (h t) -> p h t", h=8),
                            in1=guT[:, :, s * 128:(s + 1) * 128], op=ALU.mult),
                            reads=["t1", "guT"], writes=["t2"])
                        S.op("pool", lambda e, s=s: e.tensor_tensor(
                            out=oaT[:, :, s * 128:(s + 1) * 128], in0=t2.rearrange("p (h t) -> p h t", h=8),
                            in1=szT[:, :, s * 128:(s + 1) * 128], op=ALU.mult),
                            reads=["t2", "szT"], writes=["oaT"])
                    S.dma("sp", lambda e, tok0=tok0: e.dma_start(
                        out=OA[:, :, tok0:tok0 + 512].rearrange("h d t -> d h t"), in_=oaT), reads=["oaT"])

        for tt in range(0 if skip_l0 else 8):
            do_tile_A(tt)
        S.barrier()
        A.reset(l0_mark)

        if dbg and dbg.get("stop") == "A":
            S.emit()
            return nc

        obT = A.alloc([128, 8, LH], BF16)
        wo_sb = A.alloc([128, 16, D], BF16)
        b_mark = A.mark()
        maskd = A.alloc([128, 4, 512])
        negm = A.alloc([128, 4, 512])
        KTh = [A.alloc([128, 2 * LH], BF16) for _ in range(2)]
        QTh = [A.alloc([128, LH], BF16) for _ in range(2)]
        Vh = [A.alloc([128, 32, 128], BF16) for _ in range(2)]
        SZh = [A.alloc([128, LH], BF16) for _ in range(2)]
        esb = [A.alloc([128, 512]) for _ in range(2)]
        spb = [A.alloc([128, 512], BF16) for _ in range(2)]
        spf = A.alloc([128, 512])
        pre = [A.alloc([128, 512]) for _ in range(2)]
        wsb = [A.alloc([128, 512], BF16) for _ in range(2)]
        Rb = A.alloc([128, 512])
        for j in range(4):
            S.op("pool", lambda e, j=j: e.memset(maskd[:, j, :], 1.0), writes=["maskd"])
            S.op("pool", lambda e, j=j: e.affine_select(out=maskd[:, j, :], in_=maskd[:, j, :], compare_op=ALU.is_gt,
                                                        fill=0.0, base=-128 * j, pattern=[[1, 512]],
                                                        channel_multiplier=-1),
                 reads=["maskd"], writes=["maskd"])
        S.op("pool", lambda e: e.tensor_scalar(out=negm, in0=maskd, scalar1=-1.0, scalar2=30000.0, op0=ALU.add,
                                               op1=ALU.mult), reads=["maskd"], writes=["negm"])
        for kt in range(16):
            S.dma("pool", lambda e, kt=kt: e.dma_start(out=wo_sb[:, kt, :], in_=w_out_ab[kt * 128:(kt + 1) * 128, :]),
                  writes=["wo"])
        nheads = dbg.get("heads", 8) if dbg else 8
        if skip_l0:
            nheads = 0
        tiles = []
        for h in range(nheads):
            for Qg in range(4):
                nkb = 16 + 4 * (Qg + 1)
                for idx, kb in enumerate(range(nkb - 1, -1, -1)):
                    tiles.append((h, Qg, idx, kb, nkb))
        SKEW = 2

        def head_bufs(h):
            hb2 = h % 2
            return (KTh[hb2], QTh[hb2], Vh[hb2], SZh[hb2],
                    "KTh%d" % hb2, "QTh%d" % hb2, "Vh%d" % hb2, "SZh%d" % hb2)

        def stage_A(i):
            h, Qg, idx, kb, nkb = tiles[i]
            kth, qth, vh, szh, kk, qk, vk, zk = head_bufs(h)
            if Qg == 0 and idx == 0:
                S.dma("sp", lambda e: e.dma_start(out=kth, in_=KT[h]), writes=[kk])
                S.dma("sp", lambda e: e.dma_start(out=qth, in_=QT[h]), writes=[qk])
                S.dma("sp", lambda e: e.dma_start(out=vh, in_=VS[:, h * 128:(h + 1) * 128].rearrange(
                    "(kb t) d -> t kb d", t=128)), writes=[vk])
                S.dma("sp", lambda e: e.dma_start(out=szh, in_=SBZ[h]), writes=[zk])
            qs = slice(Qg * 512, (Qg + 1) * 512)
            zb = i % 4
            S.op("pe", lambda e: e.matmul(bank(zb), lhsT=kth[:, kb * 128:(kb + 1) * 128], rhs=qth[:, qs],
                                          start=True, stop=True), reads=[kk, qk], writes=bkeys(zb))

        def stage_A2(i):
            h, Qg, idx, kb, nkb = tiles[i]
            zb = i % 4
            cb = 4 + i % 3
            j = kb - (16 + 4 * Qg)
            e_, ek = esb[i % 2], "esb%d" % (i % 2)
            sp_, spk = spb[i % 2], "spb%d" % (i % 2)
            S.op("act", lambda e: e.activation(out=e_, in_=bank(zb), func=AF.Exp), reads=bkeys(zb), writes=[ek])
            if j < 0:
                S.op("act", lambda e: e.activation(out=sp_, in_=e_, func=AF.Ln, bias=1.0), reads=[ek], writes=[spk])
            else:
                S.op("act", lambda e: e.activation(out=spf, in_=e_, func=AF.Ln, bias=1.0), reads=[ek], writes=["spf"])
                S.op("dve", lambda e: e.tensor_tensor(out=sp_, in0=spf, in1=maskd[:, j, :], op=ALU.mult),
                     reads=["spf", "maskd"], writes=[spk])
            S.op("pe", lambda e: e.matmul(bank(zb), lhsT=negUI, rhs=sp_, start=False, stop=True, skip_group_check=True),
                 reads=[spk, "negUI"] + bkeys(zb), writes=bkeys(zb))
            S.op("pe", lambda e: e.matmul(bank(cb), lhsT=negones, rhs=sp_, start=True, stop=True),
                 reads=[spk, "negones"], writes=bkeys(cb))

        def stage_B(i):
            h, Qg, idx, kb, nkb = tiles[i]
            kth, qth, vh, szh, kk, qk, vk, zk = head_bufs(h)
            qs = slice(Qg * 512, (Qg + 1) * 512)
            zb = i % 4
            cb = 4 + i % 3
            po = 7
            j = kb - (16 + 4 * Qg)
            pr_, prk = pre[i % 2], "pre%d" % (i % 2)
            w_, wk = wsb[i % 2], "wsb%d" % (i % 2)
            if idx == 0:
                S.op("dve", lambda e: e.memset(Rb, 0.0), writes=["R"])
            S.op("dve", lambda e: e.tensor_tensor(out=pr_, in0=bank(zb), in1=Rb, op=ALU.add),
                 reads=bkeys(zb) + ["R"], writes=[prk])
            if j >= 0:
                S.op("pool", lambda e: e.tensor_tensor(out=pr_, in0=pr_, in1=negm[:, j, :], op=ALU.add),
                     reads=[prk, "negm"], writes=[prk])
            S.op("act", lambda e: e.activation(out=w_, in_=pr_, func=AF.Exp), reads=[prk], writes=[wk])
            S.op("pe", lambda e: e.matmul(bank(po), lhsT=vh[:, kb, :], rhs=w_, start=(idx == 0), stop=(idx == nkb - 1)),
                 reads=[wk, vk], writes=bkeys(po))
            S.op("dve", lambda e: e.tensor_tensor(out=Rb, in0=Rb, in1=bank(cb), op=ALU.add),
                 reads=bkeys(cb) + ["R"], writes=["R"])
            if idx == nkb - 1:
                S.op("dve", lambda e: e.tensor_tensor(out=obT[:, h, qs], in0=bank(po), in1=szh[:, qs], op=ALU.mult),
                     reads=bkeys(po) + [zk], writes=["obT%d" % Qg])

        for i in range(len(tiles) + SKEW + 1):
            if i < len(tiles):
                stage_A(i)
            if 0 <= i - 1 < len(tiles):
                stage_A2(i - 1)
            if 0 <= i - 1 - SKEW < len(tiles):
                stage_B(i - 1 - SKEW)
        dump("d_obT", obT, ["obT0", "obT1", "obT2", "obT3"], BF16)
        S.barrier()
        A.reset(b_mark)
        if dbg and dbg.get("stop") == "B":
            S.emit()
            return nc

        oa_sb = [A.alloc([128, 8, 512], BF16) for _ in range(2)]
        xres = [A.alloc([128, D]) for _ in range(2)]
        yt = [A.alloc([128, D]) for _ in range(2)]
        junk2 = A.alloc([128, D], BF16)
        ssc = A.alloc([128, 4])

        def out_tile_C(tt):
            ob = oa_sb[tt % 2]
            ok = "oa_sb%d" % (tt % 2)
            S.dma("sp", lambda e: e.dma_start(out=ob, in_=OA[:, :, tt * 512:(tt + 1) * 512].rearrange("h d t -> d h t")),
                  writes=[ok])

            def catf(kt, s):
                if kt < 8:
                    return ob[:, kt, s * 128:(s + 1) * 128], ok
                return obT[:, kt - 8, tt * 512 + s * 128: tt * 512 + (s + 1) * 128], "obT%d" % tt
            out_proj_resid(0, tt, catf, 16, wo_sb, "wo", x_own, X1, xres, yt, ssc, junk2, tt * 4)

        for tt in range(0 if skip_l0 else 4):
            out_tile_C(tt)
        S.barrier()
        A.reset(l0_mark)
        if dbg and dbg.get("stop") == "C":
            S.emit()
            return nc
        UT = dscr("UT", [8, 128, LH], BF16)
        SZ1 = dscr("SZ1", [8, 128, LH], BF16)
        GT = dscr("GT", [8, 128, LH], BF16)
        l1_mark = A.mark()
        xraw = [A.alloc([128, D]) for _ in range(2)]
        junk = A.alloc([128, D], BF16)
        ssb = A.alloc([128, 12])
        xs = A.alloc([128, 4, D], BF16)
        hT1 = [A.alloc([128, 16, 512], BF16) for _ in range(2)]
        Wd = [A.alloc([128, 16, 512], BF16) for _ in range(4)]
        stg1 = [A.alloc([128, 512], BF16) for _ in range(4)]
        cntD = [0, 0, 0]

        def do_tile_D(tt):
            hb = hT1[tt % 2]
            hk = "hT1%d" % (tt % 2)
            hkeys = ["%s_%d" % (hk, ft) for ft in range(16)]
            norm_transpose_tile(X1, tt * 512, 1, xraw, xs, hb, hk, junk, ssb, tt)
            for cc in range(4):
                i = cntD[0] % 4
                cntD[0] += 1
                W_, wk = Wd[i], "Wd%d" % i
                S.dma("pool", lambda e, W_=W_, cc=cc: e.dma_start(
                    out=W_, in_=w_in_ssm[:, cc * 512:(cc + 1) * 512].rearrange("(kt k) c -> k kt c", k=128)),
                    writes=[wk])
                for ct in range(4):
                    b = 2 + cntD[1] % 4
                    cntD[1] += 1
                    f = (cc % 2) * 4 + ct
                    for kt in range(16):
                        S.op("pe", lambda e, b=b, kt=kt, ct=ct, W_=W_: e.matmul(
                            bank(b), lhsT=W_[:, kt, ct * 128:(ct + 1) * 128], rhs=hb[:, kt, :],
                            start=(kt == 0), stop=(kt == 15)), reads=[hkeys[kt], wk], writes=bkeys(b))
                    si = cntD[2] % 4
                    cntD[2] += 1
                    sg, sk = stg1[si], "stg1%d" % si
                    if cc < 2:
                        S.op("dve", lambda e, b=b, sg=sg: e.tensor_copy(out=sg, in_=bank(b)), reads=bkeys(b), writes=[sk])
                        S.dma("sp", lambda e, sg=sg, f=f: e.dma_start(out=UT[f, :, tt * 512:(tt + 1) * 512], in_=sg),
                              reads=[sk])
                    else:
                        S.op("act", lambda e, b=b, sg=sg: e.activation(out=sg, in_=bank(b), func=AF.Silu),
                             reads=bkeys(b), writes=[sk])
                        S.dma("sp", lambda e, sg=sg, f=f: e.dma_start(out=SZ1[f, :, tt * 512:(tt + 1) * 512], in_=sg),
                              reads=[sk])

        for tt in range(4):
            do_tile_D(tt)
        S.barrier()
        A.reset(l1_mark)
        if dbg and dbg.get("stop") == "D":
            S.emit()
            return nc

        Kblk = A.alloc([128, 8, 16, 128], BF16)
        Csm_re = A.alloc([128, 32, 32])
        Csm_nim = A.alloc([128, 32, 32])
        pw_re = A.alloc([128, 17, 32])
        pw_im = A.alloc([128, 17, 32])
        Ak_re = A.alloc([128, 7, 32])
        Ak_im = A.alloc([128, 7, 32])
        Sst_re = A.alloc([128, 32, 128], BF16)
        Sst_im = A.alloc([128, 32, 128], BF16)
        dsk = A.alloc([128, 8])
        p3_mark = A.mark()
        Bb_re = A.alloc([128, 32, 32])
        Bb_im = A.alloc([128, 32, 32])
        S_re = A.alloc([128, 32, 132])
        S_im = A.alloc([128, 32, 132])
        p2_mark = A.mark()
        prm = A.alloc([128, 16, 32])
        kint = A.alloc([128, 32], I32)
        Braw_re = A.alloc([128, 32, 32])
        Braw_im = A.alloc([128, 32, 32])
        tA = A.alloc([128, 32, 32])
        tB = A.alloc([128, 32, 32])
        BendT = A.alloc([128, 4, 16, 2, 128], BF16)
        uTf = [A.alloc([128, LH], BF16) for _ in range(2)]
        uz = [A.alloc([128, LH], BF16) for _ in range(4)]
        bmask = A.alloc([128, 128])
        ktmp = A.alloc([128, 128])
        m01 = A.alloc([128, 2])
        S.op("pool", lambda e: e.memset(bmask, 1.0), writes=["bmask"])
        for j in range(4):
            S.op("pool", lambda e, j=j: e.affine_select(out=bmask[:, 32 * j:32 * j + 32], in_=bmask[:, 32 * j:32 * j + 32],
                                                        compare_op=ALU.is_ge, fill=0.0, base=-32 * j, pattern=[[0, 32]],
                                                        channel_multiplier=1), reads=["bmask"], writes=["bmask"])
            S.op("pool", lambda e, j=j: e.affine_select(out=bmask[:, 32 * j:32 * j + 32], in_=bmask[:, 32 * j:32 * j + 32],
                                                        compare_op=ALU.is_ge, fill=0.0, base=32 * j + 31, pattern=[[0, 32]],
                                                        channel_multiplier=-1), reads=["bmask"], writes=["bmask"])
        S.op("pool", lambda e: e.tensor_tensor(out=m01[:, 0:1], in0=bmask[:, 0:1], in1=bmask[:, 64:65], op=ALU.add),
             reads=["bmask"], writes=["m01"])
        S.op("pool", lambda e: e.tensor_tensor(out=m01[:, 1:2], in0=bmask[:, 32:33], in1=bmask[:, 96:97], op=ALU.add),
             reads=["bmask"], writes=["m01"])

        TWO_PI = 2.0 * math.pi
        LR, LI, DT, MAG, ANG, U, FR, SN, CS, DEN, NRm, T0, T1, CR, CI, T2 = [prm[:, i, :] for i in range(16)]
        S.dma("sp", lambda e: e.dma_start(out=LR, in_=lamre_sm), writes=["prm"])
        S.dma("sp", lambda e: e.dma_start(out=LI, in_=lamim_sm), writes=["prm"])
        S.dma("sp", lambda e: e.dma_start(out=DT, in_=logdt_sm), writes=["prm"])
        S.dma("sp", lambda e: e.dma_start(out=Braw_re, in_=bre_smz), writes=["Braw"])
        S.dma("sp", lambda e: e.dma_start(out=Braw_im, in_=bim_smz), writes=["Braw"])
        S.dma("sp", lambda e: e.dma_start(out=Csm_re, in_=cre_smz), writes=["Csm"])
        S.dma("sp", lambda e: e.dma_start(out=Csm_nim, in_=cim_smz), writes=["Csm"])
        S.dma("sp", lambda e: e.dma_start(out=dsk, in_=d_skipT), writes=["dsk"])

        def P(eng, fn):
            S.op(eng, fn, reads=["prm", "Braw", "Csm"], writes=["prm"])

        P("act", lambda e: e.activation(out=DT, in_=DT, func=AF.Exp))
        P("dve", lambda e: e.tensor_tensor(out=MAG, in0=LR, in1=DT, op=ALU.mult))
        P("act", lambda e: e.activation(out=MAG, in_=MAG, func=AF.Exp))
        P("dve", lambda e: e.tensor_tensor(out=ANG, in0=LI, in1=DT, op=ALU.mult))
        P("dve", lambda e: e.tensor_scalar(out=U, in0=ANG, scalar1=1.0 / TWO_PI, scalar2=None, op0=ALU.mult))
        P("dve", lambda e: e.tensor_copy(out=kint, in_=U))
        P("dve", lambda e: e.tensor_copy(out=FR, in_=kint))
        P("dve", lambda e: e.tensor_tensor(out=FR, in0=U, in1=FR, op=ALU.subtract))
        P("act", lambda e: e.activation(out=SN, in_=FR, func=AF.Sin, scale=TWO_PI))
        P("dve", lambda e: e.tensor_scalar(out=U, in0=U, scalar1=0.25, scalar2=None, op0=ALU.add))
        P("dve", lambda e: e.tensor_copy(out=kint, in_=U))
        P("dve", lambda e: e.tensor_copy(out=FR, in_=kint))
        P("dve", lambda e: e.tensor_tensor(out=FR, in0=U, in1=FR, op=ALU.subtract))
        P("act", lambda e: e.activation(out=CS, in_=FR, func=AF.Sin, scale=TWO_PI))
        P("dve", lambda e: e.memset(pw_re[:, 0, :], 1.0))
        P("dve", lambda e: e.memset(pw_im[:, 0, :], 0.0))
        P("dve", lambda e: e.tensor_tensor(out=pw_re[:, 1, :], in0=MAG, in1=CS, op=ALU.mult))
        P("dve", lambda e: e.tensor_tensor(out=pw_im[:, 1, :], in0=MAG, in1=SN, op=ALU.mult))
        P("dve", lambda e: e.tensor_tensor(out=DEN, in0=LR, in1=LR, op=ALU.mult))
        P("dve", lambda e: e.tensor_tensor(out=T0, in0=LI, in1=LI, op=ALU.mult))
        P("dve", lambda e: e.tensor_tensor(out=DEN, in0=DEN, in1=T0, op=ALU.add))
        P("dve", lambda e: e.reciprocal(out=DEN, in_=DEN))
        P("dve", lambda e: e.tensor_scalar(out=NRm, in0=pw_re[:, 1, :], scalar1=-1.0, scalar2=None, op0=ALU.add))
        P("dve", lambda e: e.tensor_tensor(out=T0, in0=NRm, in1=LR, op=ALU.mult))
        P("dve", lambda e: e.tensor_tensor(out=T1, in0=pw_im[:, 1, :], in1=LI, op=ALU.mult))
        P("dve", lambda e: e.tensor_tensor(out=T0, in0=T0, in1=T1, op=ALU.add))
        P("dve", lambda e: e.tensor_tensor(out=CR, in0=T0, in1=DEN, op=ALU.mult))
        P("dve", lambda e: e.tensor_tensor(out=T0, in0=pw_im[:, 1, :], in1=LR, op=ALU.mult))
        P("dve", lambda e: e.tensor_tensor(out=T1, in0=NRm, in1=LI, op=ALU.mult))
        P("dve", lambda e: e.tensor_tensor(out=T0, in0=T0, in1=T1, op=ALU.subtract))
        P("dve", lambda e: e.tensor_tensor(out=CI, in0=T0, in1=DEN, op=ALU.mult))

        def bc(ap2, n):
            return ap2.unsqueeze(2).broadcast_to([128, 32, n])

        def cmul3(eng, o_re, o_im, a_re, a_im, b_re, b_im, ta, tb, rk, wk, conj=False):
            op_re = ALU.add if conj else ALU.subtract
            S.op(eng, lambda e: e.tensor_tensor(out=ta, in0=b_re, in1=a_re, op=ALU.mult), reads=rk, writes=["cm_ta"])
            S.op(eng, lambda e: e.tensor_tensor(out=tb, in0=b_im, in1=a_im, op=ALU.mult), reads=rk, writes=["cm_tb"])
            S.op(eng, lambda e: e.tensor_tensor(out=o_re, in0=ta, in1=tb, op=op_re),
                 reads=["cm_ta", "cm_tb"], writes=wk)
            op_im = ALU.subtract if conj else ALU.add
            S.op(eng, lambda e: e.tensor_tensor(out=ta, in0=b_im, in1=a_re, op=ALU.mult), reads=rk, writes=["cm_ta"])
            S.op(eng, lambda e: e.tensor_tensor(out=tb, in0=b_re, in1=a_im, op=ALU.mult), reads=rk, writes=["cm_tb"])
            S.op(eng, lambda e: e.tensor_tensor(out=o_im, in0=ta, in1=tb, op=op_im),
                 reads=["cm_ta", "cm_tb"], writes=wk)

        cmul3("dve", Bb_re, Bb_im, bc(CR, 32), bc(CI, 32), Braw_re, Braw_im, tA, tB, ["prm", "Braw"], ["Bb"])
        S.op("pool", lambda e: e.tensor_scalar(out=Csm_nim, in0=Csm_nim, scalar1=-1.0, scalar2=None, op0=ALU.mult),
             reads=["Csm"], writes=["Csm"])
        for k in range(1, 16):
            cmul3("dve", pw_re[:, k + 1, :], pw_im[:, k + 1, :], pw_re[:, 1, :], pw_im[:, 1, :],
                  pw_re[:, k, :], pw_im[:, k, :], tA[:, 0, :], tB[:, 0, :], ["prm", "pw"], ["pw"])
        S.op("dve", lambda e: e.tensor_copy(out=Ak_re[:, 0, :], in_=pw_re[:, 16, :]), reads=["pw"], writes=["Ak"])
        S.op("dve", lambda e: e.tensor_copy(out=Ak_im[:, 0, :], in_=pw_im[:, 16, :]), reads=["pw"], writes=["Ak"])
        for k in range(6):
            cmul3("dve", Ak_re[:, k + 1, :], Ak_im[:, k + 1, :], Ak_re[:, k, :], Ak_im[:, k, :],
                  Ak_re[:, k, :], Ak_im[:, k, :], tA[:, 0, :], tB[:, 0, :], ["Ak"], ["Ak"])
        S.op("dve", lambda e: e.memset(S_re[:, :, 0:4], 0.0), writes=["S_re"])
        S.op("dve", lambda e: e.memset(S_im[:, :, 0:4], 0.0), writes=["S_im"])

        if dbg and dbg.get("stop") == "S1a":
            dump("d_pw_re", pw_re, ["pw"])
            dump("d_pw_im", pw_im, ["pw"])
            S.emit()
            return nc
        bsm_re = Braw_re
        bsm_im = Braw_im
        ucount = [0]
        for half in range(2):
            for tau in range(16):
                k = 15 - tau
                cmul3("dve", bsm_re, bsm_im, bc(pw_re[:, k, :], 32), bc(pw_im[:, k, :], 32), Bb_re, Bb_im, tA, tB,
                      ["pw", "Bb", "Braw"], ["bsm"])
                for fl in range(4):
                    f = half * 4 + fl
                    tb0 = 4 * (fl % 2)
                    kb1 = 1 + 4 * (fl % 2)
                    for ri, src in ((0, bsm_re), (1, bsm_im)):
                        S.op("pe", lambda e, src=src, f=f, ri=ri, tb0=tb0: e.transpose(
                            out=bank(tb0)[:, ri * 128:(ri + 1) * 128],
                            in_=src[:, 4 * f:4 * f + 4, :].rearrange("p a b -> p (a b)"), identity=identf),
                            reads=["bsm", "identf"], writes=bkeys(tb0))
                    S.op("act", lambda e, fl=fl, tau=tau, tb0=tb0: e.activation(
                        out=BendT[:, fl, tau, :, :].rearrange("p a b -> p (a b)"), in_=bank(tb0)[:, 0:256], func=AF.Copy),
                        reads=bkeys(tb0), writes=["BendT"])
                    for j in range(4):
                        gp = 4 * f + j
                        S.op("pe", lambda e, gp=gp, j=j, kb1=kb1, f=f: e.matmul(
                            bank(kb1)[:, 32 * j:32 * j + 32],
                            lhsT=bsm_re[:, 4 * f:4 * f + 4, :].rearrange("p a b -> p (a b)"), rhs=Csm_re[:, gp, :],
                            start=True, stop=False), reads=["bsm", "Csm"], writes=bkeys(kb1))
                        S.op("pe", lambda e, gp=gp, j=j, kb1=kb1, f=f: e.matmul(
                            bank(kb1)[:, 32 * j:32 * j + 32],
                            lhsT=bsm_im[:, 4 * f:4 * f + 4, :].rearrange("p a b -> p (a b)"), rhs=Csm_nim[:, gp, :],
                            start=False, stop=True), reads=["bsm", "Csm"], writes=bkeys(kb1))
                    if k == 0:
                        S.op("dve", lambda e, kb1=kb1: e.tensor_tensor(out=ktmp, in0=bank(kb1)[:, 0:128], in1=bmask,
                                                                       op=ALU.mult),
                             reads=bkeys(kb1) + ["bmask"], writes=["ktmp"])
                        S.op("dve", lambda e, f=f: e.scalar_tensor_tensor(out=Kblk[:, f, 0, :], in0=identf,
                                                                          scalar=dsk[:, f:f + 1], in1=ktmp,
                                                                          op0=ALU.mult, op1=ALU.add),
                             reads=["ktmp", "dsk", "identf"], writes=["Kblk"])
                    else:
                        S.op("dve", lambda e, f=f, k=k, kb1=kb1: e.tensor_tensor(out=Kblk[:, f, k, :],
                                                                                 in0=bank(kb1)[:, 0:128], in1=bmask,
                                                                                 op=ALU.mult),
                             reads=bkeys(kb1) + ["bmask"], writes=["Kblk"])
            if dbg and dbg.get("stop") == "S1b":
                dump("d_Kblk", Kblk, ["Kblk"], BF16)
                S.emit()
                return nc
            for fl in range(4):
                f = half * 4 + fl
                ui = ucount[0] % 2
                ucount[0] += 1
                u_, uk = uTf[ui], "uTf%d" % ui
                S.dma("sp", lambda e, u_=u_, f=f: e.dma_start(out=u_, in_=UT[f]), writes=[uk])
                uv = u_.rearrange("p (n t) -> p t n", t=16)
                uzv = [uz[i].rearrange("p (t n) -> p t n", t=16) for i in range(4)]
                for i in range(4):
                    S.op("dve", lambda e, uv=uv, uzv=uzv, i=i: e.tensor_scalar(out=uzv[i], in0=uv,
                                                                              scalar1=bmask[:, 32 * i:32 * i + 1],
                                                                              scalar2=None, op0=ALU.mult),
                         reads=[uk, "bmask"], writes=["uz%d" % i])
                bmode = dbg.get("bmode", 9) if dbg else 9
                for j in range(4 if bmode >= 1 else 0):
                    gp = 4 * f + j
                    pb = 2 + (gp % 2)
                    w0 = 64 * (j // 2)
                    jj = j % 2
                    for ri in range(2):
                        for tau in range(16):
                            S.op("pe", lambda e, pb=pb, ri=ri, tau=tau, fl=fl, j=j, uzv=uzv: e.matmul(
                                bank(pb)[:, ri * 128:(ri + 1) * 128], lhsT=BendT[:, fl, tau, ri, :],
                                rhs=uzv[j][:, tau, :], start=(tau == 0), stop=(tau == 15)),
                                reads=["BendT", "uz%d" % j], writes=bkeys(pb))
                    if bmode < 2:
                        continue
                    S.op("dve", lambda e, pb=pb, gp=gp: e.tensor_copy(out=S_re[:, gp, 4:132], in_=bank(pb)[:, 0:128]),
                         reads=bkeys(pb), writes=["S_re"])
                    S.op("dve", lambda e, pb=pb, gp=gp: e.tensor_copy(out=S_im[:, gp, 4:132], in_=bank(pb)[:, 128:256]),
                         reads=bkeys(pb), writes=["S_im"])
        dump("d_pw_re", pw_re, ["pw"])
        dump("d_pw_im", pw_im, ["pw"])
        dump("d_Sloc_re", S_re, ["S_re"])
        dump("d_Sloc_im", S_im, ["S_im"])
        dump("d_Kblk", Kblk, ["Kblk"], BF16)
        S.barrier()
        A.reset(p2_mark)
        if dbg and dbg.get("stop") == "S1":
            S.emit()
            return nc

        E_re = A.alloc([128, 32, 128])
        E_im = A.alloc([128, 32, 128])
        sc = [A.alloc([128, 16, 128]) for _ in range(4)]
        et = [A.alloc([128, 32 * 64]) for _ in range(2)]
        Sp = A.alloc([128, 64])
        Sp2 = A.alloc([128, 64])
        S.op("pool", lambda e: e.tensor_copy(out=E_re[:, :, 0:1], in_=Ak_re[:, 0, :].unsqueeze(2)), writes=["E"])
        S.op("pool", lambda e: e.tensor_copy(out=E_im[:, :, 0:1], in_=Ak_im[:, 0, :].unsqueeze(2)), writes=["E"])
        for k in range(7):
            w = 1 << k
            ta = et[0][:, 0:32 * w].rearrange("p (a b) -> p a b", a=32)
            tb = et[1][:, 0:32 * w].rearrange("p (a b) -> p a b", a=32)
            cmul3("pool", E_re[:, :, w:2 * w], E_im[:, :, w:2 * w], bc(Ak_re[:, k, :], w), bc(Ak_im[:, k, :], w),
                  E_re[:, :, 0:w], E_im[:, :, 0:w], ta, tb, ["E"], ["E"])
        for k in range(7):
            w = 1 << k
            L = 128 - w
            for hf in range(2):
                g0 = hf * 16
                are = Ak_re[:, k, g0:g0 + 16].unsqueeze(2).broadcast_to([128, 16, L])
                aim = Ak_im[:, k, g0:g0 + 16].unsqueeze(2).broadcast_to([128, 16, L])
                sre_lo = S_re[:, g0:g0 + 16, 4:4 + L]
                sim_lo = S_im[:, g0:g0 + 16, 4:4 + L]
                sre_hi = S_re[:, g0:g0 + 16, 4 + w:132]
                sim_hi = S_im[:, g0:g0 + 16, 4 + w:132]
                t = [sc[i][:, :, 0:L] for i in range(4)]
                S.op("dve", lambda e, t=t, are=are, sre_lo=sre_lo: e.tensor_tensor(out=t[0], in0=sre_lo, in1=are, op=ALU.mult),
                     reads=["S_re"], writes=["sc0"])
                S.op("dve", lambda e, t=t, aim=aim, sim_lo=sim_lo: e.tensor_tensor(out=t[1], in0=sim_lo, in1=aim, op=ALU.mult),
                     reads=["S_im"], writes=["sc1"])
                S.op("pool", lambda e, t=t, are=are, sim_lo=sim_lo: e.tensor_tensor(out=t[2], in0=sim_lo, in1=are, op=ALU.mult),
                     reads=["S_im"], writes=["sc2"])
                S.op("pool", lambda e, t=t, aim=aim, sre_lo=sre_lo: e.tensor_tensor(out=t[3], in0=sre_lo, in1=aim, op=ALU.mult),
                     reads=["S_re"], writes=["sc3"])
                S.op("dve", lambda e, t=t, sre_hi=sre_hi: e.tensor_tensor(out=sre_hi, in0=sre_hi, in1=t[0], op=ALU.add),
                     reads=["sc0", "S_re"], writes=["S_re"])
                S.op("dve", lambda e, t=t, sre_hi=sre_hi: e.tensor_tensor(out=sre_hi, in0=sre_hi, in1=t[1], op=ALU.subtract),
                     reads=["sc1", "S_re"], writes=["S_re"])
                S.op("pool", lambda e, t=t, sim_hi=sim_hi: e.tensor_tensor(out=sim_hi, in0=sim_hi, in1=t[2], op=ALU.add),
                     reads=["sc2", "S_im"], writes=["S_im"])
                S.op("pool", lambda e, t=t, sim_hi=sim_hi: e.tensor_tensor(out=sim_hi, in0=sim_hi, in1=t[3], op=ALU.add),
                     reads=["sc3", "S_im"], writes=["S_im"])
        SK = ["S_re", "S_im"]
        dump("d_S_re", S_re, SK)
        dump("d_S_im", S_im, SK)
        S.op("dve", lambda e: e.tensor_copy(out=Sp[:, 0:32], in_=S_re[:, :, 131]), reads=SK, writes=["Sp"])
        S.op("dve", lambda e: e.tensor_copy(out=Sp[:, 32:64], in_=S_im[:, :, 131]), reads=SK, writes=["Sp"])
        S.dma("pool", lambda e: e.dma_start(out=st_in[:, :], in_=Sp), reads=["Sp"], writes=["st_in"])
        ncores = dbg.get("ncores", 8) if dbg else 8
        groups = [[2 * i, 2 * i + 1] for i in range(ncores // 2)] if ncores > 1 else [[0]]
        S.op("pool", lambda e: e.collective_compute("AllGather", ALU.bypass, replica_groups=groups,
                                                    ins=[st_in.ap().opt()], outs=[st_out.ap().opt()]),
             reads=["st_in"], writes=["st_out"])
        S.dma("pool", lambda e: e.dma_start(out=Sp2, in_=st_out[0:128, :]), reads=["st_out"], writes=["Sp2"])
        S.op("dve", lambda e: e.tensor_scalar(out=Sp2, in0=Sp2, scalar1=flag_sb, scalar2=None, op0=ALU.mult),
             reads=["Sp2", "flag"], writes=["Sp2"])
        S.op("dve", lambda e: e.tensor_copy(out=S_re[:, :, 3:4], in_=Sp2[:, 0:32].unsqueeze(2)), reads=["Sp2"] + SK,
             writes=["S_re"])
        S.op("dve", lambda e: e.tensor_copy(out=S_im[:, :, 3:4], in_=Sp2[:, 32:64].unsqueeze(2)), reads=["Sp2"] + SK,
             writes=["S_im"])
        for hf in range(2):
            g0 = hf * 16
            pre_ = Sp2[:, g0:g0 + 16].unsqueeze(2).broadcast_to([128, 16, 128])
            pim_ = Sp2[:, 32 + g0:32 + g0 + 16].unsqueeze(2).broadcast_to([128, 16, 128])
            ere = E_re[:, g0:g0 + 16, :]
            eim = E_im[:, g0:g0 + 16, :]
            sre = S_re[:, g0:g0 + 16, 4:132]
            sim = S_im[:, g0:g0 + 16, 4:132]
            S.op("dve", lambda e, ere=ere, pre_=pre_: e.tensor_tensor(out=sc[0], in0=ere, in1=pre_, op=ALU.mult),
                 reads=["E", "Sp2"], writes=["sc0"])
            S.op("dve", lambda e, eim=eim, pim_=pim_: e.tensor_tensor(out=sc[1], in0=eim, in1=pim_, op=ALU.mult),
                 reads=["E", "Sp2"], writes=["sc1"])
            S.op("pool", lambda e, ere=ere, pim_=pim_: e.tensor_tensor(out=sc[2], in0=ere, in1=pim_, op=ALU.mult),
                 reads=["E", "Sp2"], writes=["sc2"])
            S.op("pool", lambda e, eim=eim, pre_=pre_: e.tensor_tensor(out=sc[3], in0=eim, in1=pre_, op=ALU.mult),
                 reads=["E", "Sp2"], writes=["sc3"])
            S.op("dve", lambda e, sre=sre: e.tensor_tensor(out=sre, in0=sre, in1=sc[0], op=ALU.add),
                 reads=["sc0", "S_re"], writes=["S_re"])
            S.op("dve", lambda e, sre=sre: e.tensor_tensor(out=sre, in0=sre, in1=sc[1], op=ALU.subtract),
                 reads=["sc1", "S_re"], writes=["S_re"])
            S.op("pool", lambda e, sim=sim: e.tensor_tensor(out=sim, in0=sim, in1=sc[2], op=ALU.add),
                 reads=["sc2", "S_im"], writes=["S_im"])
            S.op("pool", lambda e, sim=sim: e.tensor_tensor(out=sim, in0=sim, in1=sc[3], op=ALU.add),
                 reads=["sc3", "S_im"], writes=["S_im"])
        S.op("dve", lambda e: e.tensor_copy(out=Sst_re, in_=S_re[:, :, 3:131]), reads=["S_re"], writes=["Sst"])
        S.op("pool", lambda e: e.tensor_copy(out=Sst_im, in_=S_im[:, :, 3:131]), reads=["S_im"], writes=["Sst"])
        S.barrier()
        A.reset(p3_mark)

        Cp = A.alloc([128, 32, 16, 2, 32], BF16)
        tA = A.alloc([128, 32, 32])
        tB = A.alloc([128, 32, 32])
        uTf = [A.alloc([128, LH], BF16) for _ in range(2)]
        upm = [A.alloc([128, LH], BF16) for _ in range(2)]
        gch = [A.alloc([128, 16, 128], BF16) for _ in range(2)]
        gTf = [A.alloc([128, LH], BF16) for _ in range(2)]
        for tau in range(16):
            cmul3("dve", Cp[:, :, tau, 0, :], Cp[:, :, tau, 1, :], bc(pw_re[:, tau + 1, :], 32),
                  bc(pw_im[:, tau + 1, :], 32), Csm_re, Csm_nim, tA, tB, ["pw", "Csm"], ["Cp"], conj=True)
        for f in range(8):
            u_, uk = uTf[f % 2], "uTf%d" % (f % 2)
            S.dma("sp", lambda e, u_=u_, f=f: e.dma_start(out=u_, in_=UT[f]), writes=[uk])
            uv = upm[f % 2].rearrange("p (t n) -> p t n", t=16)
            upk = "upm%d" % (f % 2)
            S.op("pool", lambda e, uv=uv, u_=u_: e.tensor_copy(out=uv, in_=u_.rearrange("p (n t) -> p t n", t=16)),
                 reads=[uk], writes=[upk])
            uk = upk
            yv = bank(0, 4).rearrange("p (t c) -> p t c", t=16)
            S.op("dve", lambda e: e.memset(bank(0, 4), 0.0), writes=bkeys(0, 4))
            for tau in range(16):
                for tp in range(tau + 1):
                    S.op("pe", lambda e, yv=yv, uv=uv, tau=tau, tp=tp, f=f: e.matmul(
                        yv[:, tau, :], lhsT=uv[:, tp, :], rhs=Kblk[:, f, tau - tp, :], start=False, stop=False,
                        skip_group_check=True), reads=["Kblk", uk] + bkeys(0, 4), writes=bkeys(0, 4))
                for j in range(4):
                    gp = 4 * f + j
                    S.op("pe", lambda e, yv=yv, tau=tau, gp=gp, j=j: e.matmul(
                        yv[:, tau, 32 * j:32 * j + 32], lhsT=Sst_re[:, gp, :], rhs=Cp[:, gp, tau, 0, :], start=False,
                        stop=False, skip_group_check=True), reads=["Cp", "Sst"] + bkeys(0, 4), writes=bkeys(0, 4))
                    S.op("pe", lambda e, yv=yv, tau=tau, gp=gp, j=j: e.matmul(
                        yv[:, tau, 32 * j:32 * j + 32], lhsT=Sst_im[:, gp, :], rhs=Cp[:, gp, tau, 1, :], start=False,
                        stop=False, skip_group_check=True), reads=["Cp", "Sst"] + bkeys(0, 4), writes=bkeys(0, 4))
            if dbg and dbg.get("dump") and f == 0:
                ydump = A.alloc([128, 2048])
                S.op("act", lambda e: e.activation(out=ydump, in_=bank(0, 4), func=AF.Copy), reads=bkeys(0, 4),
                     writes=["ydump"])
                dump("d_yssm0", ydump, ["ydump"])
            gc_, gck = gch[f % 2], "gch%d" % (f % 2)
            S.op("act", lambda e, gc_=gc_: e.activation(out=gc_.rearrange("p a b -> p (a b)"), in_=bank(0, 4),
                                                        func=AF.Gelu_apprx_tanh), reads=bkeys(0, 4), writes=[gck])
            ptb = bank(4, 2).bitcast(BF16)[:, 0:2048].rearrange("p (t n) -> p t n", t=16)
            for tau in range(16):
                S.op("pe", lambda e, gc_=gc_, tau=tau, ptb=ptb: e.transpose(out=ptb[:, tau, :], in_=gc_[:, tau, :],
                                                                           identity=ident),
                     reads=[gck, "ident"], writes=bkeys(4, 2))
            g_, gk = gTf[f % 2], "gTf%d" % (f % 2)
            S.op("dve", lambda e, g_=g_, ptb=ptb: e.tensor_copy(out=g_.rearrange("p (n t) -> p t n", t=16), in_=ptb),
                 reads=bkeys(4, 2), writes=[gk])
            S.dma("sp", lambda e, g_=g_, f=f: e.dma_start(out=GT[f], in_=g_), reads=[gk])
        S.barrier()
        A.reset(l1_mark)
        if dbg and dbg.get("stop") == "S3":
            S.emit()
            return nc

        wg_sb = A.alloc([128, 8, 1024], BF16)
        wo2_sb = A.alloc([128, 8, D], BF16)
        bglu = A.alloc([128, 8])
        gtl = [A.alloc([128, 8, 512], BF16) for _ in range(2)]
        szl = [A.alloc([128, 8, 512], BF16) for _ in range(2)]
        yyT = [A.alloc([128, 8, 512], BF16) for _ in range(2)]
        sig = [A.alloc([128, 512]) for _ in range(2)]
        xres = [A.alloc([128, D]) for _ in range(2)]
        yt = [A.alloc([128, D]) for _ in range(2)]
        junk2 = A.alloc([128, D], BF16)
        ssc = A.alloc([128, 4])
        S.dma("sp", lambda e: e.dma_start(out=bglu, in_=b_gluT), writes=["bglu"])
        for kt in range(8):
            S.dma("pool", lambda e, kt=kt: e.dma_start(out=wg_sb[:, kt, :], in_=w_glu[kt * 128:(kt + 1) * 128, :]),
                  writes=["wg"])
        for kt in range(8):
            S.dma("pool", lambda e, kt=kt: e.dma_start(out=wo2_sb[:, kt, :], in_=w_out_ssm[kt * 128:(kt + 1) * 128, :]),
                  writes=["wo2"])
        fcnt = [0]

        def do_tile_F(tt):
            i2 = tt % 2
            g_, gk = gtl[i2], "gtl%d" % i2
            z_, zk = szl[i2], "szl%d" % i2
            y_, yk = yyT[i2], "yyT%d" % i2
            ts = slice(tt * 512, (tt + 1) * 512)
            S.dma("sp", lambda e: e.dma_start(out=g_, in_=GT[:, :, ts].rearrange("f d t -> d f t")), writes=[gk])
            S.dma("sp", lambda e: e.dma_start(out=z_, in_=SZ1[:, :, ts].rearrange("f d t -> d f t")), writes=[zk])
            for ct in range(8):
                b = fcnt[0] % 2
                fcnt[0] += 1
                sg, sgk = sig[b], "sig%d" % b
                for kt in range(8):
                    S.op("pe", lambda e, b=b, kt=kt, ct=ct: e.matmul(
                        bank(b), lhsT=wg_sb[:, kt, ct * 128:(ct + 1) * 128], rhs=g_[:, kt, :],
                        start=(kt == 0), stop=(kt == 7)), reads=["wg", gk], writes=bkeys(b))
                S.op("act", lambda e, b=b, sg=sg, ct=ct: e.activation(out=sg, in_=bank(b), func=AF.Sigmoid,
                                                                      bias=bglu[:, ct:ct + 1]),
                     reads=bkeys(b) + ["bglu"], writes=[sgk])
                S.op("dve", lambda e, sg=sg, ct=ct: e.tensor_tensor(out=sg, in0=sg, in1=g_[:, ct, :], op=ALU.mult),
                     reads=[sgk, gk], writes=[sgk])
                S.op("pool", lambda e, sg=sg, ct=ct: e.tensor_tensor(out=y_[:, ct, :], in0=sg, in1=z_[:, ct, :],
                                                                     op=ALU.mult),
                     reads=[sgk, zk], writes=[yk])

            def catf(kt, s):
                return y_[:, kt, s * 128:(s + 1) * 128], yk
            out_proj_resid(1, tt, catf, 8, wo2_sb, "wo2", X1, out, xres, yt, ssc, junk2, tt * 4)

        for tt in range(4):
            do_tile_F(tt)
        S.emit()
    return nc


def prep_inputs(inputs):
    f = lambda a: np.ascontiguousarray(np.asarray(a, dtype=np.float32))
    x = f(inputs["x"])
    c = f(inputs["c"])
    fm = lambda v, n: np.ascontiguousarray(v.reshape(n, 128).T)
    shared = {
        "g_preT": np.stack([fm(f(inputs["ln_pre_g"])[l], 16) for l in range(2)]),
        "g_post": f(inputs["ln_post_g"]),
        "w_mod": f(inputs["w_mod"]),
        "b_modT": np.stack([fm(f(inputs["b_mod"])[l], 48) for l in range(2)]),
        "b_mod": f(inputs["b_mod"]),
        "w_in_ab": f(inputs["w_in_ab"])[0],
        "w_out_ab": f(inputs["w_out_ab"])[0],
        "sgu_norm_g": f(inputs["sgu_norm_g"])[0],
        "sgu_w": f(inputs["sgu_w"])[0],
        "sgu_b": f(inputs["sgu_b"])[0].reshape(1024),
        "w_in_ssm": f(inputs["w_in_ssm"])[0],
        "w_out_ssm": f(inputs["w_out_ssm"])[0],
        "w_glu": f(inputs["w_glu"])[0],
        "b_gluT": fm(f(inputs["b_glu"])[0], 8),
        "d_skipT": fm(f(inputs["d_skip"])[0], 8),
    }
    def sm(a):
        return np.ascontiguousarray(a.reshape(32, 2, 64).transpose(1, 2, 0).reshape(128, 32))
    shared["lamre_sm"] = sm(f(inputs["lam_re"])[0])
    shared["lamim_sm"] = sm(f(inputs["lam_im"])[0])
    shared["logdt_sm"] = sm(np.repeat(f(inputs["log_dt"])[0][:, None], 64, axis=1))

    def smz_b(b):
        o = np.zeros((2, 64, 32, 2, 16), np.float32)
        bb = b.reshape(32, 2, 64, 16)
        for g2 in range(2):
            o[g2, :, :, g2, :] = bb[:, g2].transpose(1, 0, 2)
        return np.ascontiguousarray(o.reshape(128, 32, 32))

    def smz_c(cm):
        return smz_b(np.ascontiguousarray(cm.transpose(0, 2, 1)))
    shared["bre_smz"] = smz_b(f(inputs["b_re"])[0])
    shared["bim_smz"] = smz_b(f(inputs["b_im"])[0])
    shared["cre_smz"] = smz_c(f(inputs["c_re"])[0])
    shared["cim_smz"] = smz_c(f(inputs["c_im"])[0])
    in_maps = []
    for core in range(8):
        b, s = core // 2, core % 2
        m = dict(shared)
        m["x_own"] = np.ascontiguousarray(x[b, s * LH:(s + 1) * LH])
        m["x_prev"] = np.ascontiguousarray(x[b, 0:LH])
        m["flag"] = np.full((128, 1), float(s), np.float32)
        m["cT"] = fm(c[b], 16)
        in_maps.append(m)
    return in_maps


def kernel(**inputs):
    nc = build_program()
    in_maps = prep_inputs(inputs)
    res = run_bass_kernel_spmd(nc, in_maps, core_ids=list(range(8)))
    out = np.empty((4, 2 * LH, D), np.float32)
    for core in range(8):
        b, s = core // 2, core % 2
        out[b, s * LH:(s + 1) * LH] = np.asarray(res.results[core]["out"], np.float32)
    return out
```

```python
import contextlib
import math
import numpy as np
import concourse.bass as bass
import concourse.mybir as mybir
from concourse.bass_utils import run_bass_kernel_spmd

F32 = mybir.dt.float32
BF16 = mybir.dt.bfloat16
I32 = mybir.dt.int32
AF = mybir.ActivationFunctionType
ALU = mybir.AluOpType
AX = mybir.AxisListType

D = 2048
LH = 2048
NT = 4
EPS = 1e-6
T16 = 16
NCH = LH // T16


class Sched:
    ENGS = ("pe", "act", "dve", "pool", "sp")
    NSLOT = 8

    def __init__(self, nc):
        self.nc = nc
        self.ops = []
        self.last_w = {}
        self.readers = {}
        self.dma_rr = {e: 0 for e in self.ENGS}
        self.slot_last = {}
        self.last_op = {}

    def _add(self, engine, fn, reads, writes, dma, extra=()):
        deps = set(extra)
        for b in reads:
            if b in self.last_w:
                deps.add(self.last_w[b])
        for b in writes:
            if b in self.last_w:
                deps.add(self.last_w[b])
            for r in self.readers.get(b, ()):
                deps.add(r)
        oid = len(self.ops)
        slot = None
        if dma:
            slot = self.dma_rr[engine] % self.NSLOT
            self.dma_rr[engine] += 1
            key = (engine, slot)
            if key in self.slot_last:
                deps.add(self.slot_last[key])
            self.slot_last[key] = oid
        deps.discard(oid)
        self.ops.append(dict(engine=engine, fn=fn, deps=deps, dma=dma, slot=slot))
        for b in writes:
            self.last_w[b] = oid
            self.readers[b] = []
        for b in reads:
            if b not in writes:
                self.readers.setdefault(b, []).append(oid)
        self.last_op[engine] = oid
        return oid

    def op(self, engine, fn, reads=(), writes=()):
        return self._add(engine, fn, tuple(reads), tuple(writes), False)

    def dma(self, engine, fn, reads=(), writes=()):
        return self._add(engine, fn, tuple(reads), tuple(writes), True)

    def barrier(self):
        pend = set(self.last_op.values()) | set(self.slot_last.values())
        for e in self.ENGS:
            self._add(e, lambda eng: eng.nop(), (), (), False, extra=pend)
        self.last_w = {}
        self.readers = {}

    def emit(self, final_wait_engine="sp"):
        nc = self.nc
        ops = self.ops

        def skip(od, o):
            return od["engine"] == "pe" and o["engine"] == "pe" and not od["dma"] and not o["dma"]

        need = [False] * len(ops)
        for o in ops:
            for d in o["deps"]:
                if not skip(ops[d], o):
                    need[d] = True
        cnt = {e: 0 for e in self.ENGS}
        slotcnt = {}
        tok = [None] * len(ops)
        for i, o in enumerate(ops):
            if o["dma"]:
                key = ("dma", o["engine"], o["slot"])
                slotcnt[key] = slotcnt.get(key, 0) + 16
                tok[i] = (key, slotcnt[key])
            elif need[i]:
                cnt[o["engine"]] += 1
                tok[i] = (("c", o["engine"]), cnt[o["engine"]])
        sem_keys = sorted({t[0] for t in tok if t is not None}, key=str)
        with contextlib.ExitStack() as st:
            sems = {}
            for k in sem_keys:
                sems[k] = st.enter_context(nc.semaphore("s_" + "_".join(str(x) for x in k)))
            per_eng = {e: [] for e in self.ENGS}
            waited = {e: {} for e in self.ENGS}
            for i, o in enumerate(ops):
                e = o["engine"]
                waits = {}
                for d in o["deps"]:
                    if skip(ops[d], o):
                        continue
                    k, v = tok[d]
                    if waited[e].get(k, 0) >= v:
                        continue
                    waits[k] = max(waits.get(k, 0), v)
                for k, v in waits.items():
                    waited[e][k] = v
                per_eng[e].append((o["fn"], list(waits.items()), tok[i]))
            fin = [(k, v) for k, v in slotcnt.items() if waited[final_wait_engine].get(k, 0) < v]
            blockfn = {"pe": "tensor", "act": "scalar", "dve": "vector", "pool": "gpsimd", "sp": "sync"}
            with nc.Block() as block:
                for e in self.ENGS:
                    lst = per_eng[e]

                    def body(eng, lst=lst, e=e):
                        for fn, waits, t in lst:
                            for k, v in waits:
                                eng.wait_ge(sems[k], v)
                            ins = fn(eng)
                            if t is not None:
                                ins.then_inc(sems[t[0]], 16 if t[0][0] == "dma" else 1)
                        if e == final_wait_engine:
                            for k, v in fin:
                                eng.wait_ge(sems[k], v)

                    getattr(block, blockfn[e])(body)
        return len(ops)


class Arena:
    def __init__(self, t, n32):
        self.t = t
        self.n32 = n32
        self.off = 0

    def mark(self):
        return self.off

    def reset(self, m):
        self.off = m

    def alloc(self, shape, dt=F32):
        n = int(np.prod(shape[1:]))
        b = 4 if dt in (F32, I32) else 2
        n32 = (n * b + 3) // 4
        assert self.off + n32 <= self.n32, ("SBUF arena overflow", self.off, n32, self.n32)
        v = self.t[0:shape[0], self.off:self.off + n32]
        self.off += n32
        if b == 2:
            v = v.bitcast(dt)
        elif dt != F32:
            v = v.bitcast(dt)
        if len(shape) > 2:
            names = "abcdefg"[:len(shape) - 1]
            kw = {names[i]: shape[i + 1] for i in range(len(shape) - 2)}
            v = v.rearrange("p (%s) -> p %s" % (" ".join(names), " ".join(names)), **kw)
        return v


def build_program(dbg=None):
    nc = bass.Bass("TRN2", target_bir_lowering=False)

    def din(name, shape, dt=F32):
        return nc.dram_tensor(name, list(shape), dt, kind="ExternalInput").ap()

    def dscr(name, shape, dt=F32):
        kind = "ExternalOutput" if (dbg and name in dbg) else "Internal"
        return nc.dram_tensor(name, list(shape), dt, kind=kind).ap()

    x_own = din("x_own", [LH, D])
    x_prev = din("x_prev", [LH, D])
    flag = din("flag", [128, 1])
    cT = din("cT", [128, 16])
    g_preT = din("g_preT", [2, 128, 16])
    g_post = din("g_post", [2, D])
    w_mod = din("w_mod", [2, D, 3 * D])
    b_modT = din("b_modT", [2, 128, 48])
    b_mod = din("b_mod", [2, 3 * D])
    w_in_ab = din("w_in_ab", [D, 7168])
    w_out_ab = din("w_out_ab", [D, D])
    sgu_norm_g = din("sgu_norm_g", [1024])
    sgu_w = din("sgu_w", [8, 128, 128])
    sgu_b = din("sgu_b", [1024])
    w_in_ssm = din("w_in_ssm", [D, D])
    w_out_ssm = din("w_out_ssm", [1024, D])
    w_glu = din("w_glu", [1024, 1024])
    b_gluT = din("b_gluT", [128, 8])
    d_skipT = din("d_skipT", [128, 8])
    lamre_sm = din("lamre_sm", [128, 32])
    lamim_sm = din("lamim_sm", [128, 32])
    logdt_sm = din("logdt_sm", [128, 32])
    bre_smz = din("bre_smz", [128, 32, 32])
    bim_smz = din("bim_smz", [128, 32, 32])
    cre_smz = din("cre_smz", [128, 32, 32])
    cim_smz = din("cim_smz", [128, 32, 32])

    out = nc.dram_tensor("out", [LH, D], F32, kind="ExternalOutput").ap()

    QT = dscr("QT", [8, 128, LH], BF16)
    KT = dscr("KT", [8, 128, 2 * LH], BF16)
    VS = dscr("VS", [2 * LH, 1024], BF16)
    OA = dscr("OA", [8, 128, LH], BF16)
    SBZ = dscr("SBZ", [8, 128, LH], BF16)
    skip_l0 = bool(dbg and dbg.get("skip_l0"))
    X1 = din("X1in", [LH, D]) if skip_l0 else dscr("X1", [LH, D], F32)
    st_in = nc.dram_tensor("st_in", [128, 64], F32)
    st_out = nc.dram_tensor("st_out", [256, 64], F32)

    NA = 52000
    with contextlib.ExitStack() as st:
        arena_t = st.enter_context(nc.sbuf_tensor("arena", [128, NA], F32))
        ps = st.enter_context(nc.psum_tensor("ps", [128, 4096], F32))
        A = Arena(arena_t, NA)
        S = Sched(nc)

        def bank(i, n=1):
            return ps[:, i * 512:(i + n) * 512]

        def bkeys(i, n=1):
            return ["ps%d" % j for j in range(i, i + n)]

        def dump(name, ap, keys, dt=F32):
            if not (dbg and dbg.get("dump")):
                return
            t = nc.dram_tensor(name, list(ap.shape), dt, kind="ExternalOutput").ap()
            S.dma("sp", lambda e: e.dma_start(out=t, in_=ap), reads=keys)

        ident = A.alloc([128, 128], BF16)
        identf = A.alloc([128, 128], F32)
        ones_bf = A.alloc([128, 128], BF16)
        negones = A.alloc([128, 128], BF16)
        negUI = A.alloc([128, 128], BF16)
        flag_sb = A.alloc([128, 1])
        S.op("pool", lambda e: e.memset(identf, 0.0), writes=["identf"])
        S.op("pool", lambda e: e.affine_select(out=identf, in_=identf, compare_op=ALU.not_equal, fill=1.0,
                                               base=0, pattern=[[-1, 128]], channel_multiplier=1),
             reads=["identf"], writes=["identf"])
        S.op("pool", lambda e: e.tensor_copy(out=ident, in_=identf), reads=["identf"], writes=["ident"])
        S.op("pool", lambda e: e.memset(ones_bf, 1.0), writes=["ones_bf"])
        S.op("pool", lambda e: e.memset(negones, -1.0), writes=["negones"])
        S.op("pool", lambda e: e.memset(negUI, -1.0), writes=["negUI"])
        S.op("pool", lambda e: e.affine_select(out=negUI, in_=negUI, compare_op=ALU.is_ge, fill=0.0,
                                               base=0, pattern=[[-1, 128]], channel_multiplier=1),
             reads=["negUI"], writes=["negUI"])
        S.dma("sp", lambda e: e.dma_start(out=flag_sb, in_=flag), writes=["flag"])

        modT = A.alloc([128, 2, 32])
        a_pre = A.alloc([128, 2, 16])
        gpost_row = A.alloc([128, 2, D])
        m_phase = A.mark()
        cT_sb = A.alloc([128, 16])
        cond_bf = A.alloc([128, 16], BF16)
        cond_rep = A.alloc([128, 16, 128], BF16)
        gpre_sb = A.alloc([128, 2, 16])
        bmodT_sb = A.alloc([128, 2, 48])
        brow = A.alloc([128, D])
        grow = A.alloc([128, D])
        wm = [A.alloc([128, 3 * D], BF16) for _ in range(2)]
        S.dma("sp", lambda e: e.dma_start(out=cT_sb, in_=cT), writes=["cT"])
        S.dma("sp", lambda e: e.dma_start(out=gpre_sb, in_=g_preT.rearrange("l p f -> p l f")), writes=["gpre"])
        S.dma("sp", lambda e: e.dma_start(out=bmodT_sb, in_=b_modT.rearrange("l p f -> p l f")), writes=["bmodT"])
        S.op("act", lambda e: e.activation(out=cond_bf, in_=cT_sb, func=AF.Silu), reads=["cT"], writes=["cond"])
        S.op("dve", lambda e: e.tensor_copy(out=cond_rep, in_=cond_bf.unsqueeze(2).broadcast_to([128, 16, 128])),
             reads=["cond"], writes=["cond_rep"])
        for l in range(2):
            S.op("dve", lambda e: e.memset(bank(0, 5), 0.0), writes=bkeys(0, 5))
            for kt in range(16):
                wb = wm[kt % 2]
                wk = "wm%d" % (kt % 2)
                S.dma("pool", lambda e, wb=wb, l=l, kt=kt: e.dma_start(out=wb, in_=w_mod[l, kt * 128:(kt + 1) * 128, :]),
                      writes=[wk])
                for ft in range(32):
                    S.op("pe", lambda e, wb=wb, ft=ft, kt=kt: e.matmul(
                        bank(0)[:, ft:ft + 1], lhsT=wb[:, ft * 128:(ft + 1) * 128], rhs=cond_bf[:, kt:kt + 1],
                        start=False, stop=False, skip_group_check=True),
                        reads=[wk, "cond"] + bkeys(0), writes=bkeys(0))
                for cc in range(4):
                    S.op("pe", lambda e, wb=wb, cc=cc, kt=kt: e.matmul(
                        bank(1 + cc), lhsT=cond_rep[:, kt, :], rhs=wb[:, 4096 + cc * 512:4096 + (cc + 1) * 512],
                        start=False, stop=False, skip_group_check=True),
                        reads=[wk, "cond_rep"] + bkeys(1 + cc), writes=bkeys(1 + cc))
            S.op("dve", lambda e, l=l: e.tensor_tensor(out=modT[:, l, :], in0=bank(0)[:, 0:32], in1=bmodT_sb[:, l, 0:32],
                                                       op=ALU.add),
                 reads=bkeys(0) + ["bmodT"], writes=["modT%d" % l])
            S.op("dve", lambda e, l=l: e.scalar_tensor_tensor(out=a_pre[:, l, :], in0=modT[:, l, 16:32], scalar=1.0,
                                                              in1=gpre_sb[:, l, :], op0=ALU.add, op1=ALU.mult),
                 reads=["modT%d" % l, "gpre"], writes=["a_pre%d" % l])
            S.dma("sp", lambda e, l=l: e.dma_start(out=brow, in_=b_mod[l, 2 * D:3 * D].partition_broadcast(128)),
                  writes=["brow"])
            S.dma("sp", lambda e, l=l: e.dma_start(out=grow, in_=g_post[l].partition_broadcast(128)), writes=["grow"])
            S.op("dve", lambda e: e.tensor_tensor(out=brow, in0=bank(1, 4), in1=brow, op=ALU.add),
                 reads=bkeys(1, 4) + ["brow"], writes=["brow"])
            S.op("dve", lambda e, l=l: e.tensor_tensor(out=gpost_row[:, l, :], in0=brow, in1=grow, op=ALU.mult),
                 reads=["brow", "grow"], writes=["gpost%d" % l])
        dump("d_modT", modT, ["modT0", "modT1"])
        dump("d_apre", a_pre, ["a_pre0", "a_pre1"])
        dump("d_gpost", gpost_row, ["gpost0", "gpost1"])
        S.barrier()
        A.reset(m_phase)

        def norm_transpose_tile(src, tok0, l, xraw, xs, hT, tagbase, junk, ssb, it):
            for s in range(4):
                xr = xraw[(it * 4 + s) % 2]
                xrk = "xraw%d" % ((it * 4 + s) % 2)
                S.dma("sp", lambda e, xr=xr, s=s: e.dma_start(out=xr, in_=src[tok0 + s * 128: tok0 + (s + 1) * 128, :]),
                      writes=[xrk])
                S.op("act", lambda e, xr=xr, s=s: e.activation(out=junk, in_=xr, func=AF.Square,
                                                               accum_out=ssb[:, s:s + 1]),
                     reads=[xrk], writes=["junk", "ss%d" % s])
                S.op("act", lambda e, s=s: e.activation(out=ssb[:, 4 + s:5 + s], in_=ssb[:, s:s + 1], func=AF.Ln,
                                                        scale=1.0 / D, bias=eps_sb),
                     reads=["ss%d" % s], writes=["ln%d" % s])
                S.op("act", lambda e, s=s: e.activation(out=ssb[:, 8 + s:9 + s], in_=ssb[:, 4 + s:5 + s], func=AF.Exp,
                                                        scale=-0.5),
                     reads=["ln%d" % s], writes=["rstd%d" % s])
                S.op("dve", lambda e, xr=xr, s=s: e.tensor_scalar(out=xs[:, s, :], in0=xr, scalar1=ssb[:, 8 + s:9 + s],
                                                                  scalar2=None, op0=ALU.mult),
                     reads=[xrk, "rstd%d" % s], writes=["xs%d" % s])
            for ft in range(16):
                b = ft % 2
                pst = bank(b).bitcast(BF16)[:, 0:512]
                for s in range(4):
                    S.op("pe", lambda e, pst=pst, s=s, ft=ft: e.transpose(out=pst[:, s * 128:(s + 1) * 128],
                                                                          in_=xs[:, s, ft * 128:(ft + 1) * 128],
                                                                          identity=ident),
                         reads=["xs%d" % s, "ident"], writes=bkeys(b))
                eng = "act" if ft % 2 == 0 else "dve"
                if eng == "act":
                    S.op("act", lambda e, pst=pst, ft=ft: e.activation(out=hT[:, ft, :], in_=pst, func=AF.Identity,
                                                                       scale=a_pre[:, l, ft:ft + 1],
                                                                       bias=modT[:, l, ft:ft + 1]),
                         reads=bkeys(b) + ["a_pre%d" % l, "modT%d" % l], writes=["%s_%d" % (tagbase, ft)])
                else:
                    S.op("dve", lambda e, pst=pst, ft=ft: e.tensor_scalar(out=hT[:, ft, :], in0=pst,
                                                                          scalar1=a_pre[:, l, ft:ft + 1],
                                                                          scalar2=modT[:, l, ft:ft + 1],
                                                                          op0=ALU.mult, op1=ALU.add),
                         reads=bkeys(b) + ["a_pre%d" % l, "modT%d" % l], writes=["%s_%d" % (tagbase, ft)])

        def out_proj_resid(l, tt, catf, nk, wo, wokey, xsrc, dst, xres, yt, ssb, junk2, it0):
            for s in range(4):
                it = it0 + s
                pb = (it % 2) * 4
                for cc in range(4):
                    for kt in range(nk):
                        cap, ck = catf(kt, s)
                        S.op("pe", lambda e, pb=pb, cc=cc, kt=kt, cap=cap: e.matmul(
                            bank(pb + cc), lhsT=cap,
                            rhs=wo[:, kt, cc * 512:(cc + 1) * 512], start=(kt == 0), stop=(kt == nk - 1)),
                            reads=[ck, wokey], writes=bkeys(pb + cc))
                xr = xres[it % 2]
                xrk = "xres%d" % (it % 2)
                r0 = tt * 512 + s * 128
                S.dma("sp", lambda e, xr=xr, r0=r0: e.dma_start(out=xr, in_=xsrc[r0:r0 + 128, :]), writes=[xrk])
                S.op("act", lambda e, pb=pb: e.activation(out=junk2, in_=bank(pb, 4), func=AF.Square,
                                                          accum_out=ssb[:, 0:1]),
                     reads=bkeys(pb, 4), writes=["junk2", "pss"])
                S.op("act", lambda e: e.activation(out=ssb[:, 1:2], in_=ssb[:, 0:1], func=AF.Ln, scale=1.0 / D,
                                                   bias=eps_sb), reads=["pss"], writes=["pln"])
                S.op("act", lambda e: e.activation(out=ssb[:, 2:3], in_=ssb[:, 1:2], func=AF.Exp, scale=-0.5),
                     reads=["pln"], writes=["prstd"])
                y_ = yt[it % 2]
                yk = "yt%d" % (it % 2)
                S.op("dve", lambda e, pb=pb, y_=y_: e.scalar_tensor_tensor(out=y_, in0=bank(pb, 4), scalar=ssb[:, 2:3],
                                                                           in1=gpost_row[:, l, :], op0=ALU.mult,
                                                                           op1=ALU.mult),
                     reads=bkeys(pb, 4) + ["prstd", "gpost%d" % l], writes=[yk])
                S.op("pool", lambda e, y_=y_, xr=xr: e.tensor_tensor(out=y_, in0=y_, in1=xr, op=ALU.add),
                     reads=[yk, xrk], writes=[yk])
                S.dma("sp", lambda e, y_=y_, r0=r0: e.dma_start(out=dst[r0:r0 + 128, :], in_=y_), reads=[yk])

        eps_sb = A.alloc([128, 1])
        S.op("pool", lambda e: e.memset(eps_sb, EPS), writes=["eps"])

        l0_mark = A.mark()
        normg_row = A.alloc([128, 1024])
        bsb = A.alloc([128, 1024])
        WcT = A.alloc([128, 8, 128], BF16)
        tri = A.alloc([128, 128])
        sguw_sb = A.alloc([128, 8, 128])
        S.dma("sp", lambda e: e.dma_start(out=normg_row, in_=sgu_norm_g.partition_broadcast(128)), writes=["normg"])
        S.dma("sp", lambda e: e.dma_start(out=bsb, in_=sgu_b.partition_broadcast(128)), writes=["bsb"])
        S.dma("sp", lambda e: e.dma_start(out=sguw_sb, in_=sgu_w.rearrange("h t s -> t h s")), writes=["sguw"])
        S.op("pool", lambda e: e.memset(tri, 1.0), writes=["tri"])
        S.op("pool", lambda e: e.affine_select(out=tri, in_=tri, compare_op=ALU.is_ge, fill=0.0, base=0,
                                               pattern=[[1, 128]], channel_multiplier=-1),
             reads=["tri"], writes=["tri"])
        for h in range(8):
            S.op("pe", lambda e, h=h: e.transpose(out=bank(h % 2)[:, 0:128], in_=sguw_sb[:, h, :], identity=identf),
                 reads=["sguw", "identf"], writes=bkeys(h % 2))
            S.op("dve", lambda e, h=h: e.tensor_tensor(out=WcT[:, h, :], in0=bank(h % 2)[:, 0:128], in1=tri, op=ALU.mult),
                 reads=bkeys(h % 2) + ["tri"], writes=["WcT"])

        xraw = [A.alloc([128, D]) for _ in range(2)]
        junk = A.alloc([128, D], BF16)
        ssb = A.alloc([128, 12])
        xs = A.alloc([128, 4, D], BF16)
        hT = [A.alloc([128, 16, 512], BF16) for _ in range(2)]
        Wc = [A.alloc([128, 16, 512], BF16) for _ in range(3)]
        guT = A.alloc([128, 8, 512], BF16)
        szT = A.alloc([128, 8, 512], BF16)
        vn = A.alloc([128, 4, 1024], BF16)
        oaT = A.alloc([128, 8, 512], BF16)
        stg = [A.alloc([128, 512], BF16) for _ in range(4)]
        gv = A.alloc([128, 512])
        sq = A.alloc([128, 512])
        ssq = A.alloc([128, 12])
        t1 = A.alloc([128, 1024])
        t2 = A.alloc([128, 1024])

        wcount = [0]
        stgc = [0]
        pcount = [0]
        SCALE_Q = 1.0 / math.sqrt(128.0)

        WB = dscr("WB", [14, 128, 16 * 512], BF16)
        wb_done = set()

        def load_w(col0):
            i = wcount[0] % 3
            wcount[0] += 1
            c = col0 // 512
            if c not in wb_done:
                S.dma("pool", lambda e, i=i, col0=col0: e.dma_start(
                    out=Wc[i], in_=w_in_ab[:, col0:col0 + 512].rearrange("(kt k) c -> k kt c", k=128)),
                    writes=["Wc%d" % i])
                S.dma("pool", lambda e, i=i, c=c: e.dma_start(out=WB[c], in_=Wc[i].rearrange("p a b -> p (a b)")),
                      reads=["Wc%d" % i], writes=["WB%d" % c])
                wb_done.add(c)
            else:
                S.dma("sp", lambda e, i=i, c=c: e.dma_start(out=Wc[i].rearrange("p a b -> p (a b)"), in_=WB[c]),
                      reads=["WB%d" % c], writes=["Wc%d" % i])
            return Wc[i], "Wc%d" % i

        def next_bank():
            b = 2 + pcount[0] % 4
            pcount[0] += 1
            return b

        def next_stg():
            i = stgc[0] % 4
            stgc[0] += 1
            return stg[i], "stg%d" % i

        tileA_cnt = [0]

        def do_tile_A(tt):
            own = tt >= 4
            src = x_own if own else x_prev
            tok0 = (tt % 4) * 512
            ctx0 = tt * 512
            hpar = tileA_cnt[0] % 2
            tileA_cnt[0] += 1
            hb = hT[hpar]
            hk = "hT%d" % hpar
            hkeys = ["%s_%d" % (hk, ft) for ft in range(16)]
            norm_transpose_tile(src, tok0, 0, xraw, xs, hb, hk, junk, ssb, tt)
            if tt == 4:
                dump("d_hT", hb, hkeys, BF16)
                dump("d_ssb", ssb, ["rstd3"])
                dump("d_xs", xs, ["xs0", "xs1", "xs2", "xs3"], BF16)
            segs = []
            if own:
                segs += [("av", 1024, 0), ("av", 1536, 1), ("au", 0, 0), ("au", 512, 1), ("az", 2048, 0), ("az", 2560, 1),
                         ("q", 3072, 0), ("q", 3584, 1)]
            segs += [("k", 4096, 0), ("k", 4608, 1), ("v", 5120, 0), ("v", 5632, 1)]
            if own:
                segs += [("bz", 6144, 0), ("bz", 6656, 1)]
            for kind, col0, cc in segs:
                W_, wk = load_w(col0)
                if kind in ("av", "v"):
                    for s in range(4):
                        b = next_bank()
                        for kt in range(16):
                            S.op("pe", lambda e, b=b, kt=kt, s=s, W_=W_: e.matmul(
                                bank(b), lhsT=hb[:, kt, s * 128:(s + 1) * 128], rhs=W_[:, kt, :],
                                start=(kt == 0), stop=(kt == 15)),
                                reads=[hkeys[kt], wk], writes=bkeys(b))
                        if kind == "v":
                            sg, sk = next_stg()
                            if own:
                                S.op("dve", lambda e, b=b, sg=sg: e.tensor_copy(out=sg, in_=bank(b)),
                                     reads=bkeys(b), writes=[sk])
                            else:
                                S.op("dve", lambda e, b=b, sg=sg: e.tensor_scalar(out=sg, in0=bank(b), scalar1=flag_sb,
                                                                                  scalar2=None, op0=ALU.mult),
                                     reads=bkeys(b) + ["flag"], writes=[sk])
                            r0 = ctx0 + s * 128
                            S.dma("pool", lambda e, sg=sg, r0=r0, cc=cc: e.dma_start(
                                out=VS[r0:r0 + 128, cc * 512:(cc + 1) * 512], in_=sg), reads=[sk])
                        else:
                            S.op("act", lambda e, b=b: e.activation(out=gv, in_=bank(b), func=AF.Gelu_apprx_tanh),
                                 reads=bkeys(b), writes=["gv"])
                            S.op("dve", lambda e: e.tensor_tensor(out=sq, in0=gv, in1=gv, op=ALU.mult),
                                 reads=["gv"], writes=["sq"])
                            S.op("dve", lambda e: e.tensor_reduce(out=ssq[:, 0:4], in_=sq.rearrange("p (h d) -> p h d", h=4),
                                                                  axis=AX.X, op=ALU.add),
                                 reads=["sq"], writes=["ssq"])
                            S.op("act", lambda e: e.activation(out=ssq[:, 4:8], in_=ssq[:, 0:4], func=AF.Ln,
                                                               scale=1.0 / 128, bias=eps_sb),
                                 reads=["ssq"], writes=["ssqln"])
                            S.op("act", lambda e: e.activation(out=ssq[:, 8:12], in_=ssq[:, 4:8], func=AF.Exp, scale=-0.5),
                                 reads=["ssqln"], writes=["ssqr"])
                            S.op("dve", lambda e: e.tensor_tensor(
                                out=sq.rearrange("p (h d) -> p h d", h=4), in0=gv.rearrange("p (h d) -> p h d", h=4),
                                in1=ssq[:, 8:12].unsqueeze(2).broadcast_to([128, 4, 128]), op=ALU.mult),
                                reads=["gv", "ssqr"], writes=["sq"])
                            S.op("dve", lambda e, s=s, cc=cc: e.tensor_tensor(
                                out=vn[:, s, cc * 512:(cc + 1) * 512], in0=sq, in1=normg_row[:, cc * 512:(cc + 1) * 512],
                                op=ALU.mult), reads=["sq", "normg"], writes=["vn%d" % s])
                else:
                    for ct in range(4):
                        b = next_bank()
                        h = cc * 4 + ct
                        for kt in range(16):
                            S.op("pe", lambda e, b=b, kt=kt, ct=ct, W_=W_: e.matmul(
                                bank(b), lhsT=W_[:, kt, ct * 128:(ct + 1) * 128], rhs=hb[:, kt, :],
                                start=(kt == 0), stop=(kt == 15)),
                                reads=[hkeys[kt], wk], writes=bkeys(b))
                        if kind == "au":
                            S.op("act", lambda e, b=b, h=h: e.activation(out=guT[:, h, :], in_=bank(b),
                                                                         func=AF.Gelu_apprx_tanh),
                                 reads=bkeys(b), writes=["guT"])
                        elif kind == "az":
                            S.op("act", lambda e, b=b, h=h: e.activation(out=szT[:, h, :], in_=bank(b), func=AF.Silu),
                                 reads=bkeys(b), writes=["szT"])
                        elif kind == "q":
                            sg, sk = next_stg()
                            S.op("act", lambda e, b=b, sg=sg: e.activation(out=sg, in_=bank(b), func=AF.Copy,
                                                                           scale=SCALE_Q),
                                 reads=bkeys(b), writes=[sk])
                            S.dma("pool", lambda e, sg=sg, h=h, tok0=tok0: e.dma_start(out=QT[h, :, tok0:tok0 + 512], in_=sg),
                                  reads=[sk])
                        elif kind == "k":
                            sg, sk = next_stg()
                            S.op("dve", lambda e, b=b, sg=sg: e.tensor_copy(out=sg, in_=bank(b)),
                                 reads=bkeys(b), writes=[sk])
                            S.dma("pool", lambda e, sg=sg, h=h, ctx0=ctx0: e.dma_start(out=KT[h, :, ctx0:ctx0 + 512], in_=sg),
                                  reads=[sk])
                        elif kind == "bz":
                            sg, sk = next_stg()
                            S.op("act", lambda e, b=b, sg=sg: e.activation(out=sg, in_=bank(b), func=AF.Silu),
                                 reads=bkeys(b), writes=[sk])
                            S.dma("pool", lambda e, sg=sg, h=h, tok0=tok0: e.dma_start(out=SBZ[h, :, tok0:tok0 + 512], in_=sg),
                                  reads=[sk])
                if own and kind == "az" and cc == 1:
                    for s in range(4):
                        for h in range(8):
                            b6 = 6 + h // 4
                            S.op("pe", lambda e, b6=b6, h=h, s=s: e.matmul(
                                bank(b6)[:, (h % 4) * 128:(h % 4 + 1) * 128], lhsT=vn[:, s, h * 128:(h + 1) * 128],
                                rhs=WcT[:, h, :], start=True, stop=True),
                                reads=["vn%d" % s, "WcT"], writes=bkeys(b6))
                        S.op("dve", lambda e: e.tensor_tensor(out=t1, in0=bank(6, 2), in1=bsb, op=ALU.add),
                             reads=bkeys(6, 2) + ["bsb"], writes=["t1"])
                        S.op("pool", lambda e, s=s: e.tensor_tensor(
                            out=t2.rearrange("p (h t) -> p h t", h=8), in0=t1.rearrange("p (h t) -> p h t", h=8),
                            in1=guT[:, :, s * 128:(s + 1) * 128], op=ALU.mult),
                            reads=["t1", "guT"], writes=["t2"])
                        S.op("pool", lambda e, s=s: e.tensor_tensor(
                            out=oaT[:, :, s * 128:(s + 1) * 128], in0=t2.rearrange("p (h t) -> p h t", h=8),
                            in1=szT[:, :, s * 128:(s + 1) * 128], op=ALU.mult),
                            reads=["t2", "szT"], writes=["oaT"])
                    S.dma("pool", lambda e, tok0=tok0: e.dma_start(
                        out=OA[:, :, tok0:tok0 + 512].rearrange("h d t -> d h t"), in_=oaT), reads=["oaT"])

        for tt in ([] if skip_l0 else [4, 0, 1, 2, 3, 5, 6, 7]):
            do_tile_A(tt)
        S.barrier()
        A.reset(l0_mark)

        if dbg and dbg.get("stop") == "A":
            S.emit()
            return nc

        obT = A.alloc([128, 8, LH], BF16)
        wo_sb = A.alloc([128, 16, D], BF16)
        b_mark = A.mark()
        maskd = A.alloc([128, 4, 512])
        negm = A.alloc([128, 4, 512])
        KTh = [A.alloc([128, 2 * LH], BF16) for _ in range(2)]
        QTh = [A.alloc([128, LH], BF16) for _ in range(2)]
        Vh = [A.alloc([128, 32, 128], BF16) for _ in range(2)]
        SZh = [A.alloc([128, LH], BF16) for _ in range(2)]
        esb = [A.alloc([128, 512]) for _ in range(2)]
        spb = [A.alloc([128, 512], BF16) for _ in range(2)]
        spf = A.alloc([128, 512])
        pre = [A.alloc([128, 512]) for _ in range(2)]
        wsb = [A.alloc([128, 512], BF16) for _ in range(2)]
        Rb = A.alloc([128, 512])
        for j in range(4):
            S.op("pool", lambda e, j=j: e.memset(maskd[:, j, :], 1.0), writes=["maskd"])
            S.op("pool", lambda e, j=j: e.affine_select(out=maskd[:, j, :], in_=maskd[:, j, :], compare_op=ALU.is_gt,
                                                        fill=0.0, base=-128 * j, pattern=[[1, 512]],
                                                        channel_multiplier=-1),
                 reads=["maskd"], writes=["maskd"])
        S.op("pool", lambda e: e.tensor_scalar(out=negm, in0=maskd, scalar1=-1.0, scalar2=30000.0, op0=ALU.add,
                                               op1=ALU.mult), reads=["maskd"], writes=["negm"])
        for kt in range(16):
            S.dma("pool", lambda e, kt=kt: e.dma_start(out=wo_sb[:, kt, :], in_=w_out_ab[kt * 128:(kt + 1) * 128, :]),
                  writes=["wo"])
        nheads = dbg.get("heads", 8) if dbg else 8
        if skip_l0:
            nheads = 0
        tiles = []
        for h in range(nheads):
            for Qg in range(4):
                nkb = 16 + 4 * (Qg + 1)
                for idx, kb in enumerate(range(nkb - 1, -1, -1)):
                    tiles.append((h, Qg, idx, kb, nkb))
        SKEW = 2

        def head_bufs(h):
            hb2 = h % 2
            return (KTh[hb2], QTh[hb2], Vh[hb2], SZh[hb2],
                    "KTh%d" % hb2, "QTh%d" % hb2, "Vh%d" % hb2, "SZh%d" % hb2)

        def stage_A(i):
            h, Qg, idx, kb, nkb = tiles[i]
            kth, qth, vh, szh, kk, qk, vk, zk = head_bufs(h)
            if Qg == 0 and idx == 0:
                S.dma("sp", lambda e: e.dma_start(out=kth, in_=KT[h]), writes=[kk])
                S.dma("sp", lambda e: e.dma_start(out=qth, in_=QT[h]), writes=[qk])
                S.dma("sp", lambda e: e.dma_start(out=vh, in_=VS[:, h * 128:(h + 1) * 128].rearrange(
                    "(kb t) d -> t kb d", t=128)), writes=[vk])
                S.dma("sp", lambda e: e.dma_start(out=szh, in_=SBZ[h]), writes=[zk])
            qs = slice(Qg * 512, (Qg + 1) * 512)
            zb = i % 4
            S.op("pe", lambda e: e.matmul(bank(zb), lhsT=kth[:, kb * 128:(kb + 1) * 128], rhs=qth[:, qs],
                                          start=True, stop=True), reads=[kk, qk], writes=bkeys(zb))

        def stage_A2(i):
            h, Qg, idx, kb, nkb = tiles[i]
            zb = i % 4
            cb = 4 + i % 3
            j = kb - (16 + 4 * Qg)
            e_, ek = esb[i % 2], "esb%d" % (i % 2)
            sp_, spk = spb[i % 2], "spb%d" % (i % 2)
            S.op("act", lambda e: e.activation(out=e_, in_=bank(zb), func=AF.Exp), reads=bkeys(zb), writes=[ek])
            if j < 0:
                S.op("act", lambda e: e.activation(out=sp_, in_=e_, func=AF.Ln, bias=1.0), reads=[ek], writes=[spk])
            else:
                S.op("act", lambda e: e.activation(out=spf, in_=e_, func=AF.Ln, bias=1.0), reads=[ek], writes=["spf"])
                S.op("dve", lambda e: e.tensor_tensor(out=sp_, in0=spf, in1=maskd[:, j, :], op=ALU.mult),
                     reads=["spf", "maskd"], writes=[spk])
            S.op("pe", lambda e: e.matmul(bank(zb), lhsT=negUI, rhs=sp_, start=False, stop=True, skip_group_check=True),
                 reads=[spk, "negUI"] + bkeys(zb), writes=bkeys(zb))
            S.op("pe", lambda e: e.matmul(bank(cb), lhsT=negones, rhs=sp_, start=True, stop=True),
                 reads=[spk, "negones"], writes=bkeys(cb))

        def stage_B(i):
            h, Qg, idx, kb, nkb = tiles[i]
            kth, qth, vh, szh, kk, qk, vk, zk = head_bufs(h)
            qs = slice(Qg * 512, (Qg + 1) * 512)
            zb = i % 4
            cb = 4 + i % 3
            po = 7
            j = kb - (16 + 4 * Qg)
            pr_, prk = pre[i % 2], "pre%d" % (i % 2)
            w_, wk = wsb[i % 2], "wsb%d" % (i % 2)
            if idx == 0:
                S.op("dve", lambda e: e.memset(Rb, 0.0), writes=["R"])
            S.op("dve", lambda e: e.tensor_tensor(out=pr_, in0=bank(zb), in1=Rb, op=ALU.add),
                 reads=bkeys(zb) + ["R"], writes=[prk])
            if j >= 0:
                S.op("pool", lambda e: e.tensor_tensor(out=pr_, in0=pr_, in1=negm[:, j, :], op=ALU.add),
                     reads=[prk, "negm"], writes=[prk])
            S.op("act", lambda e: e.activation(out=w_, in_=pr_, func=AF.Exp), reads=[prk], writes=[wk])
            S.op("pe", lambda e: e.matmul(bank(po), lhsT=vh[:, kb, :], rhs=w_, start=(idx == 0), stop=(idx == nkb - 1)),
                 reads=[wk, vk], writes=bkeys(po))
            S.op("dve", lambda e: e.tensor_tensor(out=Rb, in0=Rb, in1=bank(cb), op=ALU.add),
                 reads=bkeys(cb) + ["R"], writes=["R"])
            if idx == nkb - 1:
                S.op("dve", lambda e: e.tensor_tensor(out=obT[:, h, qs], in0=bank(po), in1=szh[:, qs], op=ALU.mult),
                     reads=bkeys(po) + [zk], writes=["obT%d" % Qg])

        for i in range(len(tiles) + SKEW + 1):
            if i < len(tiles):
                stage_A(i)
            if 0 <= i - 1 < len(tiles):
                stage_A2(i - 1)
            if 0 <= i - 1 - SKEW < len(tiles):
                stage_B(i - 1 - SKEW)
        dump("d_obT", obT, ["obT0", "obT1", "obT2", "obT3"], BF16)
        S.barrier()
        A.reset(b_mark)
        if dbg and dbg.get("stop") == "B":
            S.emit()
            return nc

        oa_sb = [A.alloc([128, 8, 512], BF16) for _ in range(2)]
        xres = [A.alloc([128, D]) for _ in range(2)]
        yt = [A.alloc([128, D]) for _ in range(2)]
        junk2 = A.alloc([128, D], BF16)
        ssc = A.alloc([128, 4])

        def out_tile_C(tt):
            ob = oa_sb[tt % 2]
            ok = "oa_sb%d" % (tt % 2)
            S.dma("sp", lambda e: e.dma_start(out=ob, in_=OA[:, :, tt * 512:(tt + 1) * 512].rearrange("h d t -> d h t")),
                  writes=[ok])

            def catf(kt, s):
                if kt < 8:
                    return ob[:, kt, s * 128:(s + 1) * 128], ok
                return obT[:, kt - 8, tt * 512 + s * 128: tt * 512 + (s + 1) * 128], "obT%d" % tt
            out_proj_resid(0, tt, catf, 16, wo_sb, "wo", x_own, X1, xres, yt, ssc, junk2, tt * 4)

        for tt in range(0 if skip_l0 else 4):
            out_tile_C(tt)
        S.barrier()
        A.reset(l0_mark)
        if dbg and dbg.get("stop") == "C":
            S.emit()
            return nc
        UT = dscr("UT", [8, 128, LH], BF16)
        SZ1 = dscr("SZ1", [8, 128, LH], BF16)
        GT = dscr("GT", [8, 128, LH], BF16)
        l1_mark = A.mark()
        xraw = [A.alloc([128, D]) for _ in range(2)]
        junk = A.alloc([128, D], BF16)
        ssb = A.alloc([128, 12])
        xs = A.alloc([128, 4, D], BF16)
        hT1 = [A.alloc([128, 16, 512], BF16) for _ in range(2)]
        Wd = [A.alloc([128, 16, 512], BF16) for _ in range(4)]
        stg1 = [A.alloc([128, 512], BF16) for _ in range(4)]
        cntD = [0, 0, 0]

        def do_tile_D(tt):
            hb = hT1[tt % 2]
            hk = "hT1%d" % (tt % 2)
            hkeys = ["%s_%d" % (hk, ft) for ft in range(16)]
            norm_transpose_tile(X1, tt * 512, 1, xraw, xs, hb, hk, junk, ssb, tt)
            for cc in range(4):
                i = cntD[0] % 4
                cntD[0] += 1
                W_, wk = Wd[i], "Wd%d" % i
                S.dma("pool", lambda e, W_=W_, cc=cc: e.dma_start(
                    out=W_, in_=w_in_ssm[:, cc * 512:(cc + 1) * 512].rearrange("(kt k) c -> k kt c", k=128)),
                    writes=[wk])
                for ct in range(4):
                    b = 2 + cntD[1] % 4
                    cntD[1] += 1
                    f = (cc % 2) * 4 + ct
                    for kt in range(16):
                        S.op("pe", lambda e, b=b, kt=kt, ct=ct, W_=W_: e.matmul(
                            bank(b), lhsT=W_[:, kt, ct * 128:(ct + 1) * 128], rhs=hb[:, kt, :],
                            start=(kt == 0), stop=(kt == 15)), reads=[hkeys[kt], wk], writes=bkeys(b))
                    si = cntD[2] % 4
                    cntD[2] += 1
                    sg, sk = stg1[si], "stg1%d" % si
                    if cc < 2:
                        S.op("dve", lambda e, b=b, sg=sg: e.tensor_copy(out=sg, in_=bank(b)), reads=bkeys(b), writes=[sk])
                        S.dma("sp", lambda e, sg=sg, f=f: e.dma_start(out=UT[f, :, tt * 512:(tt + 1) * 512], in_=sg),
                              reads=[sk])
                    else:
                        S.op("act", lambda e, b=b, sg=sg: e.activation(out=sg, in_=bank(b), func=AF.Silu),
                             reads=bkeys(b), writes=[sk])
                        S.dma("sp", lambda e, sg=sg, f=f: e.dma_start(out=SZ1[f, :, tt * 512:(tt + 1) * 512], in_=sg),
                              reads=[sk])

        for tt in range(4):
            do_tile_D(tt)
        S.barrier()
        A.reset(l1_mark)
        if dbg and dbg.get("stop") == "D":
            S.emit()
            return nc

        Kblk = A.alloc([128, 8, 16, 128], BF16)
        Csm_re = A.alloc([128, 32, 32])
        Csm_nim = A.alloc([128, 32, 32])
        pw_re = A.alloc([128, 17, 32])
        pw_im = A.alloc([128, 17, 32])
        Ak_re = A.alloc([128, 7, 32])
        Ak_im = A.alloc([128, 7, 32])
        Sst_re = A.alloc([128, 32, 128], BF16)
        Sst_im = A.alloc([128, 32, 128], BF16)
        dsk = A.alloc([128, 8])
        p3_mark = A.mark()
        Bb_re = A.alloc([128, 32, 32])
        Bb_im = A.alloc([128, 32, 32])
        S_re = A.alloc([128, 32, 132])
        S_im = A.alloc([128, 32, 132])
        p2_mark = A.mark()
        prm = A.alloc([128, 16, 32])
        kint = A.alloc([128, 32], I32)
        Braw_re = A.alloc([128, 32, 32])
        Braw_im = A.alloc([128, 32, 32])
        tA = A.alloc([128, 32, 32])
        tB = A.alloc([128, 32, 32])
        BendT = A.alloc([128, 4, 16, 2, 128], BF16)
        uTf = [A.alloc([128, LH], BF16) for _ in range(2)]
        uz = [A.alloc([128, LH], BF16) for _ in range(4)]
        bmask = A.alloc([128, 128])
        ktmp = A.alloc([128, 128])
        m01 = A.alloc([128, 2])
        S.op("pool", lambda e: e.memset(bmask, 1.0), writes=["bmask"])
        for j in range(4):
            S.op("pool", lambda e, j=j: e.affine_select(out=bmask[:, 32 * j:32 * j + 32], in_=bmask[:, 32 * j:32 * j + 32],
                                                        compare_op=ALU.is_ge, fill=0.0, base=-32 * j, pattern=[[0, 32]],
                                                        channel_multiplier=1), reads=["bmask"], writes=["bmask"])
            S.op("pool", lambda e, j=j: e.affine_select(out=bmask[:, 32 * j:32 * j + 32], in_=bmask[:, 32 * j:32 * j + 32],
                                                        compare_op=ALU.is_ge, fill=0.0, base=32 * j + 31, pattern=[[0, 32]],
                                                        channel_multiplier=-1), reads=["bmask"], writes=["bmask"])
        S.op("pool", lambda e: e.tensor_tensor(out=m01[:, 0:1], in0=bmask[:, 0:1], in1=bmask[:, 64:65], op=ALU.add),
             reads=["bmask"], writes=["m01"])
        S.op("pool", lambda e: e.tensor_tensor(out=m01[:, 1:2], in0=bmask[:, 32:33], in1=bmask[:, 96:97], op=ALU.add),
             reads=["bmask"], writes=["m01"])

        TWO_PI = 2.0 * math.pi
        LR, LI, DT, MAG, ANG, U, FR, SN, CS, DEN, NRm, T0, T1, CR, CI, T2 = [prm[:, i, :] for i in range(16)]
        S.dma("sp", lambda e: e.dma_start(out=LR, in_=lamre_sm), writes=["prm"])
        S.dma("sp", lambda e: e.dma_start(out=LI, in_=lamim_sm), writes=["prm"])
        S.dma("sp", lambda e: e.dma_start(out=DT, in_=logdt_sm), writes=["prm"])
        S.dma("sp", lambda e: e.dma_start(out=Braw_re, in_=bre_smz), writes=["Braw"])
        S.dma("sp", lambda e: e.dma_start(out=Braw_im, in_=bim_smz), writes=["Braw"])
        S.dma("sp", lambda e: e.dma_start(out=Csm_re, in_=cre_smz), writes=["Csm"])
        S.dma("sp", lambda e: e.dma_start(out=Csm_nim, in_=cim_smz), writes=["Csm"])
        S.dma("sp", lambda e: e.dma_start(out=dsk, in_=d_skipT), writes=["dsk"])

        def P(eng, fn):
            S.op(eng, fn, reads=["prm", "Braw", "Csm"], writes=["prm"])

        P("act", lambda e: e.activation(out=DT, in_=DT, func=AF.Exp))
        P("dve", lambda e: e.tensor_tensor(out=MAG, in0=LR, in1=DT, op=ALU.mult))
        P("act", lambda e: e.activation(out=MAG, in_=MAG, func=AF.Exp))
        P("dve", lambda e: e.tensor_tensor(out=ANG, in0=LI, in1=DT, op=ALU.mult))
        P("dve", lambda e: e.tensor_scalar(out=U, in0=ANG, scalar1=1.0 / TWO_PI, scalar2=None, op0=ALU.mult))
        P("dve", lambda e: e.tensor_copy(out=kint, in_=U))
        P("dve", lambda e: e.tensor_copy(out=FR, in_=kint))
        P("dve", lambda e: e.tensor_tensor(out=FR, in0=U, in1=FR, op=ALU.subtract))
        P("act", lambda e: e.activation(out=SN, in_=FR, func=AF.Sin, scale=TWO_PI))
        P("dve", lambda e: e.tensor_scalar(out=U, in0=U, scalar1=0.25, scalar2=None, op0=ALU.add))
        P("dve", lambda e: e.tensor_copy(out=kint, in_=U))
        P("dve", lambda e: e.tensor_copy(out=FR, in_=kint))
        P("dve", lambda e: e.tensor_tensor(out=FR, in0=U, in1=FR, op=ALU.subtract))
        P("act", lambda e: e.activation(out=CS, in_=FR, func=AF.Sin, scale=TWO_PI))
        P("dve", lambda e: e.memset(pw_re[:, 0, :], 1.0))
        P("dve", lambda e: e.memset(pw_im[:, 0, :], 0.0))
        P("dve", lambda e: e.tensor_tensor(out=pw_re[:, 1, :], in0=MAG, in1=CS, op=ALU.mult))
        P("dve", lambda e: e.tensor_tensor(out=pw_im[:, 1, :], in0=MAG, in1=SN, op=ALU.mult))
        P("dve", lambda e: e.tensor_tensor(out=DEN, in0=LR, in1=LR, op=ALU.mult))
        P("dve", lambda e: e.tensor_tensor(out=T0, in0=LI, in1=LI, op=ALU.mult))
        P("dve", lambda e: e.tensor_tensor(out=DEN, in0=DEN, in1=T0, op=ALU.add))
        P("dve", lambda e: e.reciprocal(out=DEN, in_=DEN))
        P("dve", lambda e: e.tensor_scalar(out=NRm, in0=pw_re[:, 1, :], scalar1=-1.0, scalar2=None, op0=ALU.add))
        P("dve", lambda e: e.tensor_tensor(out=T0, in0=NRm, in1=LR, op=ALU.mult))
        P("dve", lambda e: e.tensor_tensor(out=T1, in0=pw_im[:, 1, :], in1=LI, op=ALU.mult))
        P("dve", lambda e: e.tensor_tensor(out=T0, in0=T0, in1=T1, op=ALU.add))
        P("dve", lambda e: e.tensor_tensor(out=CR, in0=T0, in1=DEN, op=ALU.mult))
        P("dve", lambda e: e.tensor_tensor(out=T0, in0=pw_im[:, 1, :], in1=LR, op=ALU.mult))
        P("dve", lambda e: e.tensor_tensor(out=T1, in0=NRm, in1=LI, op=ALU.mult))
        P("dve", lambda e: e.tensor_tensor(out=T0, in0=T0, in1=T1, op=ALU.subtract))
        P("dve", lambda e: e.tensor_tensor(out=CI, in0=T0, in1=DEN, op=ALU.mult))

        def bc(ap2, n):
            return ap2.unsqueeze(2).broadcast_to([128, 32, n])

        def cmul3(eng, o_re, o_im, a_re, a_im, b_re, b_im, ta, tb, rk, wk, conj=False):
            op_re = ALU.add if conj else ALU.subtract
            S.op(eng, lambda e: e.tensor_tensor(out=ta, in0=b_re, in1=a_re, op=ALU.mult), reads=rk, writes=["cm_ta"])
            S.op(eng, lambda e: e.tensor_tensor(out=tb, in0=b_im, in1=a_im, op=ALU.mult), reads=rk, writes=["cm_tb"])
            S.op(eng, lambda e: e.tensor_tensor(out=o_re, in0=ta, in1=tb, op=op_re),
                 reads=["cm_ta", "cm_tb"], writes=wk)
            op_im = ALU.subtract if conj else ALU.add
            S.op(eng, lambda e: e.tensor_tensor(out=ta, in0=b_im, in1=a_re, op=ALU.mult), reads=rk, writes=["cm_ta"])
            S.op(eng, lambda e: e.tensor_tensor(out=tb, in0=b_re, in1=a_im, op=ALU.mult), reads=rk, writes=["cm_tb"])
            S.op(eng, lambda e: e.tensor_tensor(out=o_im, in0=ta, in1=tb, op=op_im),
                 reads=["cm_ta", "cm_tb"], writes=wk)

        cmul3("dve", Bb_re, Bb_im, bc(CR, 32), bc(CI, 32), Braw_re, Braw_im, tA, tB, ["prm", "Braw"], ["Bb"])
        S.op("pool", lambda e: e.tensor_scalar(out=Csm_nim, in0=Csm_nim, scalar1=-1.0, scalar2=None, op0=ALU.mult),
             reads=["Csm"], writes=["Csm"])
        for k in range(1, 16):
            cmul3("dve", pw_re[:, k + 1, :], pw_im[:, k + 1, :], pw_re[:, 1, :], pw_im[:, 1, :],
                  pw_re[:, k, :], pw_im[:, k, :], tA[:, 0, :], tB[:, 0, :], ["prm", "pw"], ["pw"])
        S.op("dve", lambda e: e.tensor_copy(out=Ak_re[:, 0, :], in_=pw_re[:, 16, :]), reads=["pw"], writes=["Ak"])
        S.op("dve", lambda e: e.tensor_copy(out=Ak_im[:, 0, :], in_=pw_im[:, 16, :]), reads=["pw"], writes=["Ak"])
        for k in range(6):
            cmul3("dve", Ak_re[:, k + 1, :], Ak_im[:, k + 1, :], Ak_re[:, k, :], Ak_im[:, k, :],
                  Ak_re[:, k, :], Ak_im[:, k, :], tA[:, 0, :], tB[:, 0, :], ["Ak"], ["Ak"])
        S.op("dve", lambda e: e.memset(S_re[:, :, 0:4], 0.0), writes=["S_re"])
        S.op("dve", lambda e: e.memset(S_im[:, :, 0:4], 0.0), writes=["S_im"])

        if dbg and dbg.get("stop") == "S1a":
            dump("d_pw_re", pw_re, ["pw"])
            dump("d_pw_im", pw_im, ["pw"])
            S.emit()
            return nc
        bsm_re = Braw_re
        bsm_im = Braw_im
        ucount = [0]
        for half in range(2):
            for tau in range(16):
                k = 15 - tau
                cmul3("dve", bsm_re, bsm_im, bc(pw_re[:, k, :], 32), bc(pw_im[:, k, :], 32), Bb_re, Bb_im, tA, tB,
                      ["pw", "Bb", "Braw"], ["bsm"])
                for fl in range(4):
                    f = half * 4 + fl
                    tb0 = 4 * (fl % 2)
                    kb1 = 1 + 4 * (fl % 2)
                    for ri, src in ((0, bsm_re), (1, bsm_im)):
                        S.op("pe", lambda e, src=src, f=f, ri=ri, tb0=tb0: e.transpose(
                            out=bank(tb0)[:, ri * 128:(ri + 1) * 128],
                            in_=src[:, 4 * f:4 * f + 4, :].rearrange("p a b -> p (a b)"), identity=identf),
                            reads=["bsm", "identf"], writes=bkeys(tb0))
                    S.op("act", lambda e, fl=fl, tau=tau, tb0=tb0: e.activation(
                        out=BendT[:, fl, tau, :, :].rearrange("p a b -> p (a b)"), in_=bank(tb0)[:, 0:256], func=AF.Copy),
                        reads=bkeys(tb0), writes=["BendT"])
                    for j in range(4):
                        gp = 4 * f + j
                        S.op("pe", lambda e, gp=gp, j=j, kb1=kb1, f=f: e.matmul(
                            bank(kb1)[:, 32 * j:32 * j + 32],
                            lhsT=bsm_re[:, 4 * f:4 * f + 4, :].rearrange("p a b -> p (a b)"), rhs=Csm_re[:, gp, :],
                            start=True, stop=False), reads=["bsm", "Csm"], writes=bkeys(kb1))
                        S.op("pe", lambda e, gp=gp, j=j, kb1=kb1, f=f: e.matmul(
                            bank(kb1)[:, 32 * j:32 * j + 32],
                            lhsT=bsm_im[:, 4 * f:4 * f + 4, :].rearrange("p a b -> p (a b)"), rhs=Csm_nim[:, gp, :],
                            start=False, stop=True), reads=["bsm", "Csm"], writes=bkeys(kb1))
                    if k == 0:
                        S.op("dve", lambda e, kb1=kb1: e.tensor_tensor(out=ktmp, in0=bank(kb1)[:, 0:128], in1=bmask,
                                                                       op=ALU.mult),
                             reads=bkeys(kb1) + ["bmask"], writes=["ktmp"])
                        S.op("dve", lambda e, f=f: e.scalar_tensor_tensor(out=Kblk[:, f, 0, :], in0=identf,
                                                                          scalar=dsk[:, f:f + 1], in1=ktmp,
                                                                          op0=ALU.mult, op1=ALU.add),
                             reads=["ktmp", "dsk", "identf"], writes=["Kblk"])
                    else:
                        S.op("dve", lambda e, f=f, k=k, kb1=kb1: e.tensor_tensor(out=Kblk[:, f, k, :],
                                                                                 in0=bank(kb1)[:, 0:128], in1=bmask,
                                                                                 op=ALU.mult),
                             reads=bkeys(kb1) + ["bmask"], writes=["Kblk"])
            if dbg and dbg.get("stop") == "S1b":
                dump("d_Kblk", Kblk, ["Kblk"], BF16)
                S.emit()
                return nc
            for fl in range(4):
                f = half * 4 + fl
                ui = ucount[0] % 2
                ucount[0] += 1
                u_, uk = uTf[ui], "uTf%d" % ui
                S.dma("sp", lambda e, u_=u_, f=f: e.dma_start(out=u_, in_=UT[f]), writes=[uk])
                uv = u_.rearrange("p (n t) -> p t n", t=16)
                uzv = [uz[i].rearrange("p (t n) -> p t n", t=16) for i in range(4)]
                for i in range(4):
                    S.op("dve", lambda e, uv=uv, uzv=uzv, i=i: e.tensor_scalar(out=uzv[i], in0=uv,
                                                                              scalar1=bmask[:, 32 * i:32 * i + 1],
                                                                              scalar2=None, op0=ALU.mult),
                         reads=[uk, "bmask"], writes=["uz%d" % i])
                bmode = dbg.get("bmode", 9) if dbg else 9
                for j in range(4 if bmode >= 1 else 0):
                    gp = 4 * f + j
                    pb = 2 + (gp % 2)
                    w0 = 64 * (j // 2)
                    jj = j % 2
                    for ri in range(2):
                        for tau in range(16):
                            S.op("pe", lambda e, pb=pb, ri=ri, tau=tau, fl=fl, j=j, uzv=uzv: e.matmul(
                                bank(pb)[:, ri * 128:(ri + 1) * 128], lhsT=BendT[:, fl, tau, ri, :],
                                rhs=uzv[j][:, tau, :], start=(tau == 0), stop=(tau == 15)),
                                reads=["BendT", "uz%d" % j], writes=bkeys(pb))
                    if bmode < 2:
                        continue
                    S.op("dve", lambda e, pb=pb, gp=gp: e.tensor_copy(out=S_re[:, gp, 4:132], in_=bank(pb)[:, 0:128]),
                         reads=bkeys(pb), writes=["S_re"])
                    S.op("dve", lambda e, pb=pb, gp=gp: e.tensor_copy(out=S_im[:, gp, 4:132], in_=bank(pb)[:, 128:256]),
                         reads=bkeys(pb), writes=["S_im"])
        dump("d_pw_re", pw_re, ["pw"])
        dump("d_pw_im", pw_im, ["pw"])
        dump("d_Sloc_re", S_re, ["S_re"])
        dump("d_Sloc_im", S_im, ["S_im"])
        dump("d_Kblk", Kblk, ["Kblk"], BF16)
        S.barrier()
        A.reset(p2_mark)
        if dbg and dbg.get("stop") == "S1":
            S.emit()
            return nc

        E_re = A.alloc([128, 32, 128])
        E_im = A.alloc([128, 32, 128])
        sc = [A.alloc([128, 16, 128]) for _ in range(4)]
        et = [A.alloc([128, 32 * 64]) for _ in range(2)]
        Sp = A.alloc([128, 64])
        Sp2 = A.alloc([128, 64])
        S.op("pool", lambda e: e.tensor_copy(out=E_re[:, :, 0:1], in_=Ak_re[:, 0, :].unsqueeze(2)), writes=["E"])
        S.op("pool", lambda e: e.tensor_copy(out=E_im[:, :, 0:1], in_=Ak_im[:, 0, :].unsqueeze(2)), writes=["E"])
        for k in range(7):
            w = 1 << k
            ta = et[0][:, 0:32 * w].rearrange("p (a b) -> p a b", a=32)
            tb = et[1][:, 0:32 * w].rearrange("p (a b) -> p a b", a=32)
            cmul3("pool", E_re[:, :, w:2 * w], E_im[:, :, w:2 * w], bc(Ak_re[:, k, :], w), bc(Ak_im[:, k, :], w),
                  E_re[:, :, 0:w], E_im[:, :, 0:w], ta, tb, ["E"], ["E"])
        for k in range(7):
            w = 1 << k
            L = 128 - w
            for hf in range(2):
                g0 = hf * 16
                are = Ak_re[:, k, g0:g0 + 16].unsqueeze(2).broadcast_to([128, 16, L])
                aim = Ak_im[:, k, g0:g0 + 16].unsqueeze(2).broadcast_to([128, 16, L])
                sre_lo = S_re[:, g0:g0 + 16, 4:4 + L]
                sim_lo = S_im[:, g0:g0 + 16, 4:4 + L]
                sre_hi = S_re[:, g0:g0 + 16, 4 + w:132]
                sim_hi = S_im[:, g0:g0 + 16, 4 + w:132]
                t = [sc[i][:, :, 0:L] for i in range(4)]
                S.op("dve", lambda e, t=t, are=are, sre_lo=sre_lo: e.tensor_tensor(out=t[0], in0=sre_lo, in1=are, op=ALU.mult),
                     reads=["S_re"], writes=["sc0"])
                S.op("dve", lambda e, t=t, aim=aim, sim_lo=sim_lo: e.tensor_tensor(out=t[1], in0=sim_lo, in1=aim, op=ALU.mult),
                     reads=["S_im"], writes=["sc1"])
                S.op("pool", lambda e, t=t, are=are, sim_lo=sim_lo: e.tensor_tensor(out=t[2], in0=sim_lo, in1=are, op=ALU.mult),
                     reads=["S_im"], writes=["sc2"])
                S.op("pool", lambda e, t=t, aim=aim, sre_lo=sre_lo: e.tensor_tensor(out=t[3], in0=sre_lo, in1=aim, op=ALU.mult),
                     reads=["S_re"], writes=["sc3"])
                S.op("dve", lambda e, t=t, sre_hi=sre_hi: e.tensor_tensor(out=sre_hi, in0=sre_hi, in1=t[0], op=ALU.add),
                     reads=["sc0", "S_re"], writes=["S_re"])
                S.op("dve", lambda e, t=t, sre_hi=sre_hi: e.tensor_tensor(out=sre_hi, in0=sre_hi, in1=t[1], op=ALU.subtract),
                     reads=["sc1", "S_re"], writes=["S_re"])
                S.op("pool", lambda e, t=t, sim_hi=sim_hi: e.tensor_tensor(out=sim_hi, in0=sim_hi, in1=t[2], op=ALU.add),
                     reads=["sc2", "S_im"], writes=["S_im"])
                S.op("pool", lambda e, t=t, sim_hi=sim_hi: e.tensor_tensor(out=sim_hi, in0=sim_hi, in1=t[3], op=ALU.add),
                     reads=["sc3", "S_im"], writes=["S_im"])
        SK = ["S_re", "S_im"]
        dump("d_S_re", S_re, SK)
        dump("d_S_im", S_im, SK)
        S.op("dve", lambda e: e.tensor_copy(out=Sp[:, 0:32], in_=S_re[:, :, 131]), reads=SK, writes=["Sp"])
        S.op("dve", lambda e: e.tensor_copy(out=Sp[:, 32:64], in_=S_im[:, :, 131]), reads=SK, writes=["Sp"])
        S.dma("pool", lambda e: e.dma_start(out=st_in[:, :], in_=Sp), reads=["Sp"], writes=["st_in"])
        ncores = dbg.get("ncores", 8) if dbg else 8
        groups = [[2 * i, 2 * i + 1] for i in range(ncores // 2)] if ncores > 1 else [[0]]
        S.op("pool", lambda e: e.collective_compute("AllGather", ALU.bypass, replica_groups=groups,
                                                    ins=[st_in.ap().opt()], outs=[st_out.ap().opt()]),
             reads=["st_in"], writes=["st_out"])
        S.dma("pool", lambda e: e.dma_start(out=Sp2, in_=st_out[0:128, :]), reads=["st_out"], writes=["Sp2"])
        S.op("dve", lambda e: e.tensor_scalar(out=Sp2, in0=Sp2, scalar1=flag_sb, scalar2=None, op0=ALU.mult),
             reads=["Sp2", "flag"], writes=["Sp2"])
        S.op("dve", lambda e: e.tensor_copy(out=S_re[:, :, 3:4], in_=Sp2[:, 0:32].unsqueeze(2)), reads=["Sp2"] + SK,
             writes=["S_re"])
        S.op("dve", lambda e: e.tensor_copy(out=S_im[:, :, 3:4], in_=Sp2[:, 32:64].unsqueeze(2)), reads=["Sp2"] + SK,
             writes=["S_im"])
        for hf in range(2):
            g0 = hf * 16
            pre_ = Sp2[:, g0:g0 + 16].unsqueeze(2).broadcast_to([128, 16, 128])
            pim_ = Sp2[:, 32 + g0:32 + g0 + 16].unsqueeze(2).broadcast_to([128, 16, 128])
            ere = E_re[:, g0:g0 + 16, :]
            eim = E_im[:, g0:g0 + 16, :]
            sre = S_re[:, g0:g0 + 16, 4:132]
            sim = S_im[:, g0:g0 + 16, 4:132]
            S.op("dve", lambda e, ere=ere, pre_=pre_: e.tensor_tensor(out=sc[0], in0=ere, in1=pre_, op=ALU.mult),
                 reads=["E", "Sp2"], writes=["sc0"])
            S.op("dve", lambda e, eim=eim, pim_=pim_: e.tensor_tensor(out=sc[1], in0=eim, in1=pim_, op=ALU.mult),
                 reads=["E", "Sp2"], writes=["sc1"])
            S.op("pool", lambda e, ere=ere, pim_=pim_: e.tensor_tensor(out=sc[2], in0=ere, in1=pim_, op=ALU.mult),
                 reads=["E", "Sp2"], writes=["sc2"])
            S.op("pool", lambda e, eim=eim, pre_=pre_: e.tensor_tensor(out=sc[3], in0=eim, in1=pre_, op=ALU.mult),
                 reads=["E", "Sp2"], writes=["sc3"])
            S.op("dve", lambda e, sre=sre: e.tensor_tensor(out=sre, in0=sre, in1=sc[0], op=ALU.add),
                 reads=["sc0", "S_re"], writes=["S_re"])
            S.op("dve", lambda e, sre=sre: e.tensor_tensor(out=sre, in0=sre, in1=sc[1], op=ALU.subtract),
                 reads=["sc1", "S_re"], writes=["S_re"])
            S.op("pool", lambda e, sim=sim: e.tensor_tensor(out=sim, in0=sim, in1=sc[2], op=ALU.add),
                 reads=["sc2", "S_im"], writes=["S_im"])
            S.op("pool", lambda e, sim=sim: e.tensor_tensor(out=sim, in0=sim, in1=sc[3], op=ALU.add),
                 reads=["sc3", "S_im"], writes=["S_im"])
        S.op("dve", lambda e: e.tensor_copy(out=Sst_re, in_=S_re[:, :, 3:131]), reads=["S_re"], writes=["Sst"])
        S.op("pool", lambda e: e.tensor_copy(out=Sst_im, in_=S_im[:, :, 3:131]), reads=["S_im"], writes=["Sst"])
        S.barrier()
        A.reset(p3_mark)

        Cp = A.alloc([128, 32, 16, 2, 32], BF16)
        tA = A.alloc([128, 32, 32])
        tB = A.alloc([128, 32, 32])
        uTf = [A.alloc([128, LH], BF16) for _ in range(2)]
        upm = [A.alloc([128, LH], BF16) for _ in range(2)]
        gch = [A.alloc([128, 16, 128], BF16) for _ in range(2)]
        gTf = [A.alloc([128, LH], BF16) for _ in range(2)]
        for tau in range(16):
            cmul3("dve", Cp[:, :, tau, 0, :], Cp[:, :, tau, 1, :], bc(pw_re[:, tau + 1, :], 32),
                  bc(pw_im[:, tau + 1, :], 32), Csm_re, Csm_nim, tA, tB, ["pw", "Csm"], ["Cp"], conj=True)
        for f in range(8):
            u_, uk = uTf[f % 2], "uTf%d" % (f % 2)
            S.dma("sp", lambda e, u_=u_, f=f: e.dma_start(out=u_, in_=UT[f]), writes=[uk])
            uv = upm[f % 2].rearrange("p (t n) -> p t n", t=16)
            upk = "upm%d" % (f % 2)
            S.op("pool", lambda e, uv=uv, u_=u_: e.tensor_copy(out=uv, in_=u_.rearrange("p (n t) -> p t n", t=16)),
                 reads=[uk], writes=[upk])
            uk = upk
            yv = bank(0, 4).rearrange("p (t c) -> p t c", t=16)
            S.op("dve", lambda e: e.memset(bank(0, 4), 0.0), writes=bkeys(0, 4))
            for tau in range(16):
                for tp in range(tau + 1):
                    S.op("pe", lambda e, yv=yv, uv=uv, tau=tau, tp=tp, f=f: e.matmul(
                        yv[:, tau, :], lhsT=uv[:, tp, :], rhs=Kblk[:, f, tau - tp, :], start=False, stop=False,
                        skip_group_check=True), reads=["Kblk", uk] + bkeys(0, 4), writes=bkeys(0, 4))
                for j in range(4):
                    gp = 4 * f + j
                    S.op("pe", lambda e, yv=yv, tau=tau, gp=gp, j=j: e.matmul(
                        yv[:, tau, 32 * j:32 * j + 32], lhsT=Sst_re[:, gp, :], rhs=Cp[:, gp, tau, 0, :], start=False,
                        stop=False, skip_group_check=True), reads=["Cp", "Sst"] + bkeys(0, 4), writes=bkeys(0, 4))
                    S.op("pe", lambda e, yv=yv, tau=tau, gp=gp, j=j: e.matmul(
                        yv[:, tau, 32 * j:32 * j + 32], lhsT=Sst_im[:, gp, :], rhs=Cp[:, gp, tau, 1, :], start=False,
                        stop=False, skip_group_check=True), reads=["Cp", "Sst"] + bkeys(0, 4), writes=bkeys(0, 4))
            if dbg and dbg.get("dump") and f == 0:
                ydump = A.alloc([128, 2048])
                S.op("act", lambda e: e.activation(out=ydump, in_=bank(0, 4), func=AF.Copy), reads=bkeys(0, 4),
                     writes=["ydump"])
                dump("d_yssm0", ydump, ["ydump"])
            gc_, gck = gch[f % 2], "gch%d" % (f % 2)
            S.op("act", lambda e, gc_=gc_: e.activation(out=gc_.rearrange("p a b -> p (a b)"), in_=bank(0, 4),
                                                        func=AF.Gelu_apprx_tanh), reads=bkeys(0, 4), writes=[gck])
            ptb = bank(4, 2).bitcast(BF16)[:, 0:2048].rearrange("p (t n) -> p t n", t=16)
            for tau in range(16):
                S.op("pe", lambda e, gc_=gc_, tau=tau, ptb=ptb: e.transpose(out=ptb[:, tau, :], in_=gc_[:, tau, :],
                                                                           identity=ident),
                     reads=[gck, "ident"], writes=bkeys(4, 2))
            g_, gk = gTf[f % 2], "gTf%d" % (f % 2)
            S.op("dve", lambda e, g_=g_, ptb=ptb: e.tensor_copy(out=g_.rearrange("p (n t) -> p t n", t=16), in_=ptb),
                 reads=bkeys(4, 2), writes=[gk])
            S.dma("sp", lambda e, g_=g_, f=f: e.dma_start(out=GT[f], in_=g_), reads=[gk])
        S.barrier()
        A.reset(l1_mark)
        if dbg and dbg.get("stop") == "S3":
            S.emit()
            return nc

        wg_sb = A.alloc([128, 8, 1024], BF16)
        wo2_sb = A.alloc([128, 8, D], BF16)
        bglu = A.alloc([128, 8])
        gtl = [A.alloc([128, 8, 512], BF16) for _ in range(2)]
        szl = [A.alloc([128, 8, 512], BF16) for _ in range(2)]
        yyT = [A.alloc([128, 8, 512], BF16) for _ in range(2)]
        sig = [A.alloc([128, 512]) for _ in range(2)]
        xres = [A.alloc([128, D]) for _ in range(2)]
        yt = [A.alloc([128, D]) for _ in range(2)]
        junk2 = A.alloc([128, D], BF16)
        ssc = A.alloc([128, 4])
        S.dma("sp", lambda e: e.dma_start(out=bglu, in_=b_gluT), writes=["bglu"])
        for kt in range(8):
            S.dma("pool", lambda e, kt=kt: e.dma_start(out=wg_sb[:, kt, :], in_=w_glu[kt * 128:(kt + 1) * 128, :]),
                  writes=["wg"])
        for kt in range(8):
            S.dma("pool", lambda e, kt=kt: e.dma_start(out=wo2_sb[:, kt, :], in_=w_out_ssm[kt * 128:(kt + 1) * 128, :]),
                  writes=["wo2"])
        fcnt = [0]

        def do_tile_F(tt):
            i2 = tt % 2
            g_, gk = gtl[i2], "gtl%d" % i2
            z_, zk = szl[i2], "szl%d" % i2
            y_, yk = yyT[i2], "yyT%d" % i2
            ts = slice(tt * 512, (tt + 1) * 512)
            S.dma("sp", lambda e: e.dma_start(out=g_, in_=GT[:, :, ts].rearrange("f d t -> d f t")), writes=[gk])
            S.dma("sp", lambda e: e.dma_start(out=z_, in_=SZ1[:, :, ts].rearrange("f d t -> d f t")), writes=[zk])
            for ct in range(8):
                b = fcnt[0] % 2
                fcnt[0] += 1
                sg, sgk = sig[b], "sig%d" % b
                for kt in range(8):
                    S.op("pe", lambda e, b=b, kt=kt, ct=ct: e.matmul(
                        bank(b), lhsT=wg_sb[:, kt, ct * 128:(ct + 1) * 128], rhs=g_[:, kt, :],
                        start=(kt == 0), stop=(kt == 7)), reads=["wg", gk], writes=bkeys(b))
                S.op("act", lambda e, b=b, sg=sg, ct=ct: e.activation(out=sg, in_=bank(b), func=AF.Sigmoid,
                                                                      bias=bglu[:, ct:ct + 1]),
                     reads=bkeys(b) + ["bglu"], writes=[sgk])
                S.op("dve", lambda e, sg=sg, ct=ct: e.tensor_tensor(out=sg, in0=sg, in1=g_[:, ct, :], op=ALU.mult),
                     reads=[sgk, gk], writes=[sgk])
                S.op("pool", lambda e, sg=sg, ct=ct: e.tensor_tensor(out=y_[:, ct, :], in0=sg, in1=z_[:, ct, :],
                                                                     op=ALU.mult),
                     reads=[sgk, zk], writes=[yk])

            def catf(kt, s):
                return y_[:, kt, s * 128:(s + 1) * 128], yk
            out_proj_resid(1, tt, catf, 8, wo2_sb, "wo2", X1, out, xres, yt, ssc, junk2, tt * 4)

        for tt in range(4):
            do_tile_F(tt)
        S.emit()
    return nc


def prep_inputs(inputs):
    f = lambda a: np.ascontiguousarray(np.asarray(a, dtype=np.float32))
    x = f(inputs["x"])
    c = f(inputs["c"])
    fm = lambda v, n: np.ascontiguousarray(v.reshape(n, 128).T)
    shared = {
        "g_preT": np.stack([fm(f(inputs["ln_pre_g"])[l], 16) for l in range(2)]),
        "g_post": f(inputs["ln_post_g"]),
        "w_mod": f(inputs["w_mod"]),
        "b_modT": np.stack([fm(f(inputs["b_mod"])[l], 48) for l in range(2)]),
        "b_mod": f(inputs["b_mod"]),
        "w_in_ab": f(inputs["w_in_ab"])[0],
        "w_out_ab": f(inputs["w_out_ab"])[0],
        "sgu_norm_g": f(inputs["sgu_norm_g"])[0],
        "sgu_w": f(inputs["sgu_w"])[0],
        "sgu_b": f(inputs["sgu_b"])[0].reshape(1024),
        "w_in_ssm": f(inputs["w_in_ssm"])[0],
        "w_out_ssm": f(inputs["w_out_ssm"])[0],
        "w_glu": f(inputs["w_glu"])[0],
        "b_gluT": fm(f(inputs["b_glu"])[0], 8),
        "d_skipT": fm(f(inputs["d_skip"])[0], 8),
    }
    def sm(a):
        return np.ascontiguousarray(a.reshape(32, 2, 64).transpose(1, 2, 0).reshape(128, 32))
    shared["lamre_sm"] = sm(f(inputs["lam_re"])[0])
    shared["lamim_sm"] = sm(f(inputs["lam_im"])[0])
    shared["logdt_sm"] = sm(np.repeat(f(inputs["log_dt"])[0][:, None], 64, axis=1))

    def smz_b(b):
        o = np.zeros((2, 64, 32, 2, 16), np.float32)
        bb = b.reshape(32, 2, 64, 16)
        for g2 in range(2):
            o[g2, :, :, g2, :] = bb[:, g2].transpose(1, 0, 2)
        return np.ascontiguousarray(o.reshape(128, 32, 32))

    def smz_c(cm):
        return smz_b(np.ascontiguousarray(cm.transpose(0, 2, 1)))
    shared["bre_smz"] = smz_b(f(inputs["b_re"])[0])
    shared["bim_smz"] = smz_b(f(inputs["b_im"])[0])
    shared["cre_smz"] = smz_c(f(inputs["c_re"])[0])
    shared["cim_smz"] = smz_c(f(inputs["c_im"])[0])
    in_maps = []
    for core in range(8):
        b, s = core // 2, core % 2
        m = dict(shared)
        m["x_own"] = np.ascontiguousarray(x[b, s * LH:(s + 1) * LH])
        m["x_prev"] = np.ascontiguousarray(x[b, 0:LH])
        m["flag"] = np.full((128, 1), float(s), np.float32)
        m["cT"] = fm(c[b], 16)
        in_maps.append(m)
    return in_maps


def kernel(**inputs):
    nc = build_program()
    in_maps = prep_inputs(inputs)
    res = run_bass_kernel_spmd(nc, in_maps, core_ids=list(range(8)))
    out = np.empty((4, 2 * LH, D), np.float32)
    for core in range(8):
        b, s = core // 2, core % 2
        out[b, s * LH:(s + 1) * LH] = np.asarray(res.results[core]["out"], np.float32)
    return out
```
